# Optimizing a Trainium2 kernel written in Bass

```python
import math
import jax, jax.numpy as jnp
from jax import lax
import numpy as np

D_MODEL = 2048
BATCH = 4
SEQ = 2048
DEPTH = 2
DEC_BATCH = 128
DEC_SEQ = 1
PAST_LEN = 16384
PAGE_SIZE = 128

N_BRANCH = 4
BRANCH_W = D_MODEL // 2
EPS = 1e-6
LRU_W = BRANCH_W
LRU_BLOCKS = 8
LRU_BLOCK = LRU_W // LRU_BLOCKS
LRU_C = 8.0
CONV_W = 4
HG_HEADS = 8
HG_DK = BRANCH_W // HG_HEADS
HG_DV = BRANCH_W // HG_HEADS
HG_CHUNK = 32
SSD_HEADDIM = 64
SSD_HEADS = BRANCH_W // SSD_HEADDIM
SSD_GROUPS = 2
SSD_STATE = 128
SSD_CONV_DIM = BRANCH_W + 2 * SSD_GROUPS * SSD_STATE
SSD_CHUNK = 64
RET_HEADS = 8
RET_DK = BRANCH_W // RET_HEADS
RET_DV = BRANCH_W // RET_HEADS
RET_CHUNK = 64
ROPE_BASE = 10000.0
D_FF = 5632
FFN_CONV_W = 3
IN_SIZES = (LRU_W, LRU_W,
            HG_HEADS * HG_DK, HG_HEADS * HG_DK, HG_HEADS * HG_DV, HG_HEADS * HG_DV,
            BRANCH_W, SSD_CONV_DIM, SSD_HEADS,
            RET_HEADS * RET_DK, RET_HEADS * RET_DK, RET_HEADS * RET_DV, RET_HEADS * RET_DV)
N_IN = sum(IN_SIZES)

kernel_name = 'hybrid_lru_hgrn2_ssd_retention_step'


def _rms(x):
    return x * lax.rsqrt(jnp.mean(x * x, axis=-1, keepdims=True) + EPS)


def rmsnorm(x, g):
    xf = x.astype(jnp.float32)
    return (_rms(xf) * g.astype(jnp.float32)).astype(x.dtype)


def causal_dwconv(x, buf, w, b):
    width = w.shape[0]
    T = x.shape[1]
    xc = jnp.concatenate([buf.astype(x.dtype), x], axis=1)
    y = b
    for j in range(width):
        y = y + xc[:, j:j + T] * w[j]
    return y, xc[:, T:]


def diag_linear_scan(a, u, h0):
    def combine(left, right):
        a_l, u_l = left
        a_r, u_r = right
        return a_l * a_r, a_r * u_l + u_r
    a_cum, u_cum = lax.associative_scan(combine, (a, u), axis=1)
    h = a_cum * h0[:, None] + u_cum
    return h, h[:, -1]


def _chunk(T, C):
    return C if T % C == 0 else T


def _masked_exp(diff, mask):
    return jnp.where(mask, jnp.exp(jnp.where(mask, diff, 0.0)), 0.0)


def gla_chunked(q, k, v, logf, S0, chunk):
    B, T, H, K = q.shape
    V = v.shape[-1]
    C = _chunk(T, chunk)
    N = T // C
    to_chunks = lambda z: jnp.moveaxis(z.reshape(B, N, C, H, z.shape[-1]), 1, 0)
    qc, kc, vc = to_chunks(q), to_chunks(k), to_chunks(v)
    bc = jnp.cumsum(to_chunks(logf), axis=2)
    mask = jnp.tril(jnp.ones((C, C), dtype=bool))[None, :, :, None, None]

    def step(S, inp):
        qi, ki, vi, bi = inp
        dec = _masked_exp(bi[:, :, None] - bi[:, None], mask)
        att = jnp.einsum('btshk,bshk->bhts', qi[:, :, None] * dec, ki)
        o = (jnp.einsum('bhts,bshv->bthv', att, vi)
             + jnp.einsum('bthk,bhkv->bthv', qi * jnp.exp(bi), S))
        b_last = bi[:, -1]
        S = (S * jnp.exp(b_last)[..., None]
             + jnp.einsum('bshk,bshv->bhkv', ki * jnp.exp(b_last[:, None] - bi), vi))
        return S, o

    S, o = lax.scan(step, S0, (qc, kc, vc, bc))
    return jnp.moveaxis(o, 0, 1).reshape(B, T, H, V), S


def scalar_decay_chunked(q, k, v, logd, S0, chunk):
    B, T, H, K = q.shape
    V = v.shape[-1]
    C = _chunk(T, chunk)
    N = T // C
    rs = lambda z: z.reshape((B, N, C) + z.shape[2:])
    q, k, v, logd = rs(q), rs(k), rs(v), rs(logd)
    b = jnp.cumsum(logd, axis=2)
    mask = jnp.tril(jnp.ones((C, C), dtype=bool))[None, None, :, :, None]
    L = _masked_exp(b[:, :, :, None] - b[:, :, None], mask)
    scores = jnp.einsum('bnthk,bnshk->bntsh', q, k) * L
    o_intra = jnp.einsum('bntsh,bnshv->bnthv', scores, v)
    b_last = b[:, :, -1]
    U = jnp.einsum('bnsh,bnshk,bnshv->bnhkv', jnp.exp(b_last[:, :, None] - b), k, v)
    q_in = q * jnp.exp(b)[..., None]

    def step(S, inp):
        qi, ui, di = inp
        o = jnp.einsum('bthk,bhkv->bthv', qi, S)
        return S * di[..., None, None] + ui, o

    S, o_inter = lax.scan(step, S0, (jnp.moveaxis(q_in, 1, 0), jnp.moveaxis(U, 1, 0),
                                     jnp.moveaxis(jnp.exp(b_last), 1, 0)))
    o = o_intra + jnp.moveaxis(o_inter, 0, 1)
    return o.reshape(B, T, H, V), S


def rope(x, pos):
    half = x.shape[-1] // 2
    freqs = ROPE_BASE ** (-jnp.arange(half, dtype=jnp.float32) / half)
    ang = pos[:, None] * freqs[None]
    cos = jnp.cos(ang)[None, :, None]
    sin = jnp.sin(ang)[None, :, None]
    x1, x2 = x[..., :half], x[..., half:]
    return jnp.concatenate([x1 * cos - x2 * sin, x1 * sin + x2 * cos], axis=-1)


def token_mixers(h, pos, lru_h, lru_conv, hg_S, ssd_S, ssd_conv, ret_S, P, lb):
    f32 = jnp.float32
    B, T, _ = h.shape
    proj = jnp.matmul(h, P['w_in']).astype(f32)
    splits = np.cumsum(IN_SIZES)[:-1].tolist()
    (xa, ya, hq, hf, hi, hg, sz, sxbc, sdt, rq, rk, rv, rg) = jnp.split(proj, splits, axis=-1)

    xa, lru_conv_new = causal_dwconv(xa, lru_conv.astype(f32), P['lru_conv_w'].astype(f32),
                                     P['lru_conv_b'].astype(f32))
    xblk = xa.reshape(B, T, LRU_BLOCKS, LRU_BLOCK)
    r = jax.nn.sigmoid(jnp.einsum('btnc,ncd->btnd', xblk, P['lru_wa'].astype(f32))
                       + P['lru_ba'].astype(f32)).reshape(B, T, LRU_W)
    i = jax.nn.sigmoid(jnp.einsum('btnc,ncd->btnd', xblk, P['lru_wx'].astype(f32))
                       + P['lru_bx'].astype(f32)).reshape(B, T, LRU_W)
    log_a = -LRU_C * r * jax.nn.softplus(-P['lru_lambda'].astype(f32))
    u = jnp.sqrt(-jnp.expm1(2.0 * log_a)) * (i * xa)
    hs, lru_h_new = diag_linear_scan(jnp.exp(log_a), u, lru_h.astype(f32))
    out_a = hs * jax.nn.gelu(ya)

    q_b = jax.nn.silu(hq).reshape(B, T, HG_HEADS, HG_DK)
    f = lb + (1.0 - lb) * jax.nn.sigmoid(hf)
    logf = jnp.log(f).reshape(B, T, HG_HEADS, HG_DK)
    k_b = ((1.0 - lb) * jax.nn.sigmoid(-hf)).reshape(B, T, HG_HEADS, HG_DK)
    o_b, hg_S_new = gla_chunked(q_b, k_b, hi.reshape(B, T, HG_HEADS, HG_DV), logf,
                                hg_S.astype(f32), HG_CHUNK)
    o_b = _rms(o_b) * P['hg_norm_w'].astype(f32) * jax.nn.silu(hg.reshape(B, T, HG_HEADS, HG_DV))
    out_b = o_b.reshape(B, T, BRANCH_W)

    xbc, ssd_conv_new = causal_dwconv(sxbc, ssd_conv.astype(f32), P['ssd_conv_w'].astype(f32),
                                      P['ssd_conv_b'].astype(f32))
    xbc = jax.nn.silu(xbc)
    xs_, Bm, Cm = jnp.split(xbc, [BRANCH_W, BRANCH_W + SSD_GROUPS * SSD_STATE], axis=-1)
    xs_ = xs_.reshape(B, T, SSD_HEADS, SSD_HEADDIM)
    rep = SSD_HEADS // SSD_GROUPS
    Bh = jnp.repeat(Bm.reshape(B, T, SSD_GROUPS, SSD_STATE), rep, axis=2)
    Ch = jnp.repeat(Cm.reshape(B, T, SSD_GROUPS, SSD_STATE), rep, axis=2)
    dt = jax.nn.softplus(sdt + P['ssd_dt_bias'].astype(f32))
    A = -jnp.exp(P['ssd_a_log'].astype(f32))
    y, ssd_S_new = scalar_decay_chunked(Ch, Bh, xs_ * dt[..., None], dt * A,
                                        ssd_S.astype(f32), SSD_CHUNK)
    y = y + P['ssd_d'].astype(f32)[:, None] * xs_
    y = (y.reshape(B, T, BRANCH_W) * jax.nn.silu(sz)).reshape(B, T, SSD_GROUPS, BRANCH_W // SSD_GROUPS)
    out_c = (_rms(y) * P['ssd_norm_w'].astype(f32).reshape(SSD_GROUPS, -1)).reshape(B, T, BRANCH_W)

    q_d = rope(rq.reshape(B, T, RET_HEADS, RET_DK), pos)
    k_d = rope(rk.reshape(B, T, RET_HEADS, RET_DK), pos) * RET_DK ** -0.5
    log_gamma = jnp.log1p(-jnp.exp2(-5.0 - jnp.arange(RET_HEADS, dtype=f32)))
    o_d, ret_S_new = scalar_decay_chunked(q_d, k_d, rv.reshape(B, T, RET_HEADS, RET_DV),
                                          jnp.broadcast_to(log_gamma, (B, T, RET_HEADS)),
                                          ret_S.astype(f32), RET_CHUNK)
    out_d = (_rms(o_d) * jax.nn.silu(rg.reshape(B, T, RET_HEADS, RET_DV))).reshape(B, T, BRANCH_W)

    stacked = jnp.stack([out_a, out_b, out_c, out_d], axis=2).astype(h.dtype)
    branches = jnp.einsum('btkc,kcd->btkd', stacked, P['w_branch'])
    gates = jax.nn.sigmoid(jnp.einsum('btd,dke->btke', h, P['w_gate']).astype(f32))
    merged = jnp.sum(gates * branches.astype(f32), axis=2).astype(h.dtype)
    out = jnp.matmul(merged, P['w_out'])
    return out, (lru_h_new, lru_conv_new, hg_S_new, ssd_S_new, ssd_conv_new, ret_S_new)


def conv_ffn(h, buf, P):
    f32 = jnp.float32
    u = jnp.matmul(h, P['ffn_w_up']).astype(f32)
    v = jnp.matmul(h, P['ffn_w_val']).astype(f32)
    uc, buf_new = causal_dwconv(u, buf.astype(f32), P['ffn_conv_w'].astype(f32), P['ffn_conv_b'].astype(f32))
    a = (jax.nn.gelu(uc) * v).astype(h.dtype)
    return jnp.matmul(a, P['ffn_w_down']), buf_new


def trunk_layer(x, start, states, P, lb):
    T = x.shape[1]
    pos = jnp.arange(T, dtype=jnp.float32) + start
    mix, mix_states = token_mixers(rmsnorm(x, P['g_mix']), pos, states[0], states[1], states[2],
                                   states[3], states[4], states[5], P, lb)
    x = x + mix.astype(x.dtype)
    ffn, ffn_buf = conv_ffn(rmsnorm(x, P['g_ffn']), states[6], P)
    x = x + ffn.astype(x.dtype)
    new_states = tuple(s.astype(x.dtype) for s in mix_states + (ffn_buf,))
    return x, new_states


def setup_inputs(seed: int = 0) -> dict:
    key = jax.random.key(seed)
    ks = iter(jax.random.split(key, 48))
    f32 = jnp.float32
    L = DEPTH
    D = D_MODEL

    def nrm(shape, scale):
        return jax.random.normal(next(ks), shape, f32) * scale

    def unif(shape, lo, hi):
        return jax.random.uniform(next(ks), shape, f32, minval=lo, maxval=hi)

    x_prompt = nrm((BATCH, SEQ, D), 1.0)
    x_sample = nrm((DEC_BATCH, DEC_SEQ, D), 1.0)
    state_lru_h = nrm((L, DEC_BATCH, LRU_W), 0.5)
    state_lru_conv = nrm((L, DEC_BATCH, CONV_W - 1, LRU_W), 1.0)
    state_hgrn = nrm((L, DEC_BATCH, HG_HEADS, HG_DK, HG_DV), 0.5)
    state_ssd = nrm((L, DEC_BATCH, SSD_HEADS, SSD_STATE, SSD_HEADDIM), 0.5)
    state_ssd_conv = nrm((L, DEC_BATCH, CONV_W - 1, SSD_CONV_DIM), 1.0)
    state_ret = nrm((L, DEC_BATCH, RET_HEADS, RET_DK, RET_DV), 1.0)
    state_ffn_conv = nrm((L, DEC_BATCH, FFN_CONV_W - 1, D_FF), 1.0)

    g_mix = 1.0 + nrm((L, D), 0.02)
    g_ffn = 1.0 + nrm((L, D), 0.02)
    w_in = nrm((L, D, N_IN), D ** -0.5)
    lru_conv_w = nrm((L, CONV_W, LRU_W), CONV_W ** -0.5)
    lru_conv_b = nrm((L, LRU_W), 0.02)
    lru_wa = nrm((L, LRU_BLOCKS, LRU_BLOCK, LRU_BLOCK), LRU_BLOCK ** -0.5)
    lru_ba = nrm((L, LRU_BLOCKS, LRU_BLOCK), 0.02)
    lru_wx = nrm((L, LRU_BLOCKS, LRU_BLOCK, LRU_BLOCK), LRU_BLOCK ** -0.5)
    lru_bx = nrm((L, LRU_BLOCKS, LRU_BLOCK), 0.02)
    s = unif((L, LRU_W), 0.9, 0.999) ** (1.0 / LRU_C)
    lru_lambda = jnp.log(s) - jnp.log1p(-s)
    hg_lb_logits = nrm((L, HG_HEADS * HG_DK), 0.5)
    hg_norm_w = 1.0 + nrm((L, HG_DV), 0.02)
    ssd_conv_w = nrm((L, CONV_W, SSD_CONV_DIM), CONV_W ** -0.5)
    ssd_conv_b = nrm((L, SSD_CONV_DIM), 0.02)
    dt0 = jnp.exp(unif((L, SSD_HEADS), math.log(1e-3), math.log(1e-1)))
    ssd_dt_bias = dt0 + jnp.log(-jnp.expm1(-dt0))
    ssd_a_log = jnp.log(unif((L, SSD_HEADS), 1.0, 16.0))
    ssd_d = 1.0 + nrm((L, SSD_HEADS), 0.02)
    ssd_norm_w = 1.0 + nrm((L, BRANCH_W), 0.02)
    w_branch = nrm((L, N_BRANCH, BRANCH_W, D), BRANCH_W ** -0.5)
    w_gate = nrm((L, D, N_BRANCH, D), D ** -0.5)
    w_out = nrm((L, D, D), D ** -0.5)
    ffn_w_up = nrm((L, D, D_FF), D ** -0.5)
    ffn_w_val = nrm((L, D, D_FF), D ** -0.5)
    ffn_conv_w = nrm((L, FFN_CONV_W, D_FF), FFN_CONV_W ** -0.5)
    ffn_conv_b = nrm((L, D_FF), 0.02)
    ffn_w_down = nrm((L, D_FF, D), D_FF ** -0.5)
    g_final = 1.0 + nrm((D,), 0.02)
    return {'x_prompt': x_prompt, 'x_sample': x_sample,
            'state_lru_h': state_lru_h, 'state_lru_conv': state_lru_conv, 'state_hgrn': state_hgrn,
            'state_ssd': state_ssd, 'state_ssd_conv': state_ssd_conv, 'state_ret': state_ret,
            'state_ffn_conv': state_ffn_conv,
            'g_mix': g_mix, 'g_ffn': g_ffn, 'w_in': w_in,
            'lru_conv_w': lru_conv_w, 'lru_conv_b': lru_conv_b, 'lru_wa': lru_wa, 'lru_ba': lru_ba,
            'lru_wx': lru_wx, 'lru_bx': lru_bx, 'lru_lambda': lru_lambda,
            'hg_lb_logits': hg_lb_logits, 'hg_norm_w': hg_norm_w,
            'ssd_conv_w': ssd_conv_w, 'ssd_conv_b': ssd_conv_b, 'ssd_dt_bias': ssd_dt_bias,
            'ssd_a_log': ssd_a_log, 'ssd_d': ssd_d, 'ssd_norm_w': ssd_norm_w,
            'w_branch': w_branch, 'w_gate': w_gate, 'w_out': w_out,
            'ffn_w_up': ffn_w_up, 'ffn_w_val': ffn_w_val, 'ffn_conv_w': ffn_conv_w,
            'ffn_conv_b': ffn_conv_b, 'ffn_w_down': ffn_w_down, 'g_final': g_final}


def reference(x_prompt, x_sample, state_lru_h, state_lru_conv, state_hgrn, state_ssd, state_ssd_conv,
              state_ret, state_ffn_conv, g_mix, g_ffn, w_in, lru_conv_w, lru_conv_b, lru_wa, lru_ba,
              lru_wx, lru_bx, lru_lambda, hg_lb_logits, hg_norm_w, ssd_conv_w, ssd_conv_b, ssd_dt_bias,
              ssd_a_log, ssd_d, ssd_norm_w, w_branch, w_gate, w_out, ffn_w_up, ffn_w_val, ffn_conv_w,
              ffn_conv_b, ffn_w_down, g_final):
    f32 = jnp.float32
    ls = jax.nn.softmax(hg_lb_logits.astype(f32), axis=0)
    lbs = jnp.cumsum(ls, axis=0) - ls[0]

    bp = x_prompt.shape[0]
    dtp = x_prompt.dtype
    zero_states = (jnp.zeros((bp, LRU_W), dtp), jnp.zeros((bp, CONV_W - 1, LRU_W), dtp),
                   jnp.zeros((bp, HG_HEADS, HG_DK, HG_DV), dtp),
                   jnp.zeros((bp, SSD_HEADS, SSD_STATE, SSD_HEADDIM), dtp),
                   jnp.zeros((bp, CONV_W - 1, SSD_CONV_DIM), dtp),
                   jnp.zeros((bp, RET_HEADS, RET_DK, RET_DV), dtp),
                   jnp.zeros((bp, FFN_CONV_W - 1, D_FF), dtp))

    xp, xs = x_prompt, x_sample
    new_p, new_s = [], []
    for l in range(DEPTH):
        P = {'g_mix': g_mix[l], 'g_ffn': g_ffn[l], 'w_in': w_in[l],
             'lru_conv_w': lru_conv_w[l], 'lru_conv_b': lru_conv_b[l], 'lru_wa': lru_wa[l],
             'lru_ba': lru_ba[l], 'lru_wx': lru_wx[l], 'lru_bx': lru_bx[l], 'lru_lambda': lru_lambda[l],
             'hg_norm_w': hg_norm_w[l], 'ssd_conv_w': ssd_conv_w[l], 'ssd_conv_b': ssd_conv_b[l],
             'ssd_dt_bias': ssd_dt_bias[l], 'ssd_a_log': ssd_a_log[l], 'ssd_d': ssd_d[l],
             'ssd_norm_w': ssd_norm_w[l], 'w_branch': w_branch[l], 'w_gate': w_gate[l], 'w_out': w_out[l],
             'ffn_w_up': ffn_w_up[l], 'ffn_w_val': ffn_w_val[l], 'ffn_conv_w': ffn_conv_w[l],
             'ffn_conv_b': ffn_conv_b[l], 'ffn_w_down': ffn_w_down[l]}
        xp, sp = trunk_layer(xp, 0, zero_states, P, lbs[l])
        sample_states = (state_lru_h[l], state_lru_conv[l], state_hgrn[l], state_ssd[l],
                         state_ssd_conv[l], state_ret[l], state_ffn_conv[l])
        xs, ss = trunk_layer(xs, PAST_LEN, sample_states, P, lbs[l])
        new_p.append(sp)
        new_s.append(ss)

    def stack(lst, i):
        return jnp.stack([st[i] for st in lst], axis=0)

    y_prompt = rmsnorm(xp, g_final)
    y_sample = rmsnorm(xs, g_final)
    return (y_prompt, y_sample,
            stack(new_p, 0), stack(new_s, 0), stack(new_p, 1), stack(new_s, 1),
            stack(new_p, 2), stack(new_s, 2), stack(new_p, 3), stack(new_s, 3),
            stack(new_p, 4), stack(new_s, 4), stack(new_p, 5), stack(new_s, 5),
            stack(new_p, 6), stack(new_s, 6))
```

```python
import math
from contextlib import ExitStack
from concourse.bass_utils import run_bass_kernel_spmd
import numpy as np
import concourse.bass as bass
import concourse.mybir as mybir

F32 = mybir.dt.float32
BF16 = mybir.dt.bfloat16
I32 = mybir.dt.int32
AF = mybir.ActivationFunctionType
ALU = mybir.AluOpType
AX = mybir.AxisListType


class V:
    __slots__ = ("buf", "ap")

    def __init__(self, buf, ap):
        self.buf = buf
        self.ap = ap

    def __getitem__(self, idx):
        return V(self.buf, self.ap[idx])


class Buf:
    def __init__(self, ctx, tensor, name):
        self.ctx = ctx
        self.t = tensor
        self.name = name
        self.w = None
        self.r = []
        self.dsem = None
        self.dcnt = 0

    def __getitem__(self, idx):
        return V(self, self.t[idx])

    def v(self, ap):
        return V(self, ap)


class Eng:
    def __init__(self, ctx, e, name):
        self.ctx = ctx
        self.e = e
        self.name = name
        self.sem = ctx.new_sem("c_" + name)
        self.cnt = 0
        self.waited = {}
        self.pend_r = []
        self.pend_w = []
        self.old = []

    def need(self, tok):
        if tok is None:
            return
        sem, val = tok
        k = id(sem)
        if self.waited.get(k, 0) >= val:
            return
        self.e.wait_ge(sem, val)
        self.waited[k] = val

    def deps(self, reads, writes):
        for b in reads:
            self.need(b.w)
        for b in writes:
            self.need(b.w)
            for t in b.r:
                self.need(t)

    def done(self, inst, reads, writes, inc=True):
        self.pend_r += reads
        self.pend_w += writes
        if inc:
            if self.cnt >= 30000:
                self.old.append((self.sem, self.cnt))
                self.sem = self.ctx.new_sem("c_" + self.name + str(self.ctx.nsem))
                self.cnt = 0
            inst.then_inc(self.sem, 1)
            self.cnt += 1
            tok = (self.sem, self.cnt)
            for b in self.pend_w:
                b.w = tok
                b.r = []
            for b in self.pend_r:
                if b.w is not tok:
                    b.r.append(tok)
                    if len(b.r) > 6:
                        b.r = b.r[-6:] if False else b.r
            self.pend_r = []
            self.pend_w = []


def _bufs(views):
    out = []
    for v in views:
        if isinstance(v, V) and v.buf is not None and v.buf not in out:
            out.append(v.buf)
    return out


def _ap(x):
    return x.ap if isinstance(x, V) else x


class Ctx:
    def __init__(self, nc, stack):
        self.nc = nc
        self.stack = stack
        self.nsem = 0
        self.pe = Eng(self, nc.tensor, "pe")
        self.dve = Eng(self, nc.vector, "dve")
        self.act = Eng(self, nc.scalar, "act")
        self.pool = Eng(self, nc.gpsimd, "pool")
        self.sp = Eng(self, nc.sync, "sp")
        self.dma_bufs = []
        self.drambuf = Buf(self, None, "dram")
        self.uid = 0

    def new_sem(self, name):
        self.nsem += 1
        return self.stack.enter_context(self.nc.semaphore(name))

    def sbuf(self, name, shape, dt=F32):
        t = self.stack.enter_context(self.nc.sbuf_tensor(name, list(shape), dt))
        return Buf(self, t, name)

    def psum(self, name, shape, dt=F32):
        t = self.stack.enter_context(self.nc.psum_tensor(name, list(shape), dt))
        return Buf(self, t, name)

    def op(self, eng, fn, out, ins, *args, **kw):
        extra_r = kw.pop("_reads", [])
        rb = _bufs(list(ins) + list(extra_r) + [v for v in kw.values() if isinstance(v, V)])
        wb = _bufs([out] + ([kw["accum_out"]] if "accum_out" in kw else []))
        eng.deps(rb, wb)
        kw2 = {k: _ap(v) for k, v in kw.items()}
        inst = getattr(eng.e, fn)(_ap(out), *[_ap(i) for i in ins], *args, **kw2)
        eng.done(inst, rb, wb)
        return inst

    def mm(self, out, lhsT, rhs, start=True, stop=True, inc=None, **kw):
        pe = self.pe
        rb = _bufs([lhsT, rhs])
        wb = _bufs([out])
        pe.deps(rb, wb if start else [])
        inst = pe.e.matmul(_ap(out), _ap(lhsT), _ap(rhs), start=start, stop=stop, **kw)
        pe.done(inst, rb, wb, inc=(stop if inc is None else inc))
        return inst

    def tr(self, out, in_, ident):
        pe = self.pe
        rb = _bufs([in_, ident])
        wb = _bufs([out])
        pe.deps(rb, wb)
        inst = pe.e.transpose(_ap(out), _ap(in_), _ap(ident))
        pe.done(inst, rb, wb, inc=True)
        return inst

    def dma(self, q, out, in_, **kw):
        rb = _bufs([in_])
        wb = _bufs([out])
        q.deps(rb, wb)
        b = (wb + rb)[0] if (wb + rb) else self.drambuf
        if b.dsem is None:
            b.dsem = self.new_sem("d_" + b.name)
            self.dma_bufs.append(b)
        inst = q.e.dma_start(out=_ap(out), in_=_ap(in_), **kw)
        inst.then_inc(b.dsem, 16)
        b.dcnt += 16
        tok = (b.dsem, b.dcnt)
        for x in wb:
            x.w = tok
            x.r = []
        for x in rb:
            x.r.append(tok)
        return inst

    def finish(self):
        for b in self.dma_bufs:
            self.sp.need((b.dsem, b.dcnt))


class Pool:
    def __init__(self, ctx, name, n, shape, dt=F32, psum=False):
        self.bufs = [(ctx.psum if psum else ctx.sbuf)(f"{name}{i}", shape, dt) for i in range(n)]
        self.i = 0

    def get(self):
        b = self.bufs[self.i % len(self.bufs)]
        self.i += 1
        return b


def bc(v, shape_ap):
    a = _ap(v)
    ap = bass.AP(a.tensor, a.offset, [list(a.ap[0])] + [list(x) for x in shape_ap])
    return V(v.buf, ap) if isinstance(v, V) else ap

D = 2048
KC = 16
DEPTH = 2
NS = 16
LRU_W = 1024
DFF = 5632
FC = 44
N_IN = 12816
EPS = 1e-6
O_XA, O_YA = 0, 1024
O_HQ, O_HF, O_HI, O_HG = 2048, 3072, 4096, 5120
O_SZ, O_XBC, O_DT = 6144, 7168, 8704
O_RQ, O_RK, O_RV, O_RG = 8720, 9744, 10768, 11792
LOG_GAMMA = [math.log1p(-2.0 ** (-5.0 - h)) for h in range(8)]


class K:
    pass


def build(T):
    NT = T // 512
    nc = bass.Bass("TRN2", target_bir_lowering=False)
    di = {}
    do = {}

    def din(name, shape):
        di[name] = nc.dram_tensor(name, list(shape), F32, kind="ExternalInput").ap()

    def dout(name, shape):
        do[name] = nc.dram_tensor(name, list(shape), F32, kind="ExternalOutput").ap()

    din("x_prompt", [T, D]); din("x_sample", [NS, D])
    din("state_lru_h", [2, NS, 1024]); din("state_lru_conv", [2, NS, 3, 1024])
    din("state_hgrn", [2, NS, 8, 128, 128]); din("state_ssd", [2, NS, 16, 128, 64])
    din("state_ssd_conv", [2, NS, 3, 1536]); din("state_ret", [2, NS, 8, 128, 128])
    din("state_ffn_conv", [2, NS, 2, DFF])
    din("g_mix", [2, D]); din("g_ffn", [2, D]); din("w_in", [2, D, N_IN])
    din("lru_conv_w", [2, 4, 1024]); din("lru_conv_b", [2, 1024]); din("lru_wa", [2, 8, 128, 128])
    din("lru_ba", [2, 8, 128]); din("lru_wx", [2, 8, 128, 128]); din("lru_bx", [2, 8, 128])
    din("lru_lambda", [2, 1024]); din("hg_lb_logits", [2, 1024]); din("hg_norm_w", [2, 128])
    din("ssd_conv_w", [2, 4, 1536]); din("ssd_conv_b", [2, 1536]); din("ssd_dt_bias", [2, 16])
    din("ssd_a_log", [2, 16]); din("ssd_d", [2, 16]); din("ssd_norm_w", [2, 1024])
    din("w_branch", [2, 4, 1024, D]); din("w_gate", [2, D, 4, D]); din("w_out", [2, D, D])
    din("ffn_w_up", [2, D, DFF]); din("ffn_w_val", [2, D, DFF]); din("ffn_conv_w", [2, 3, DFF])
    din("ffn_conv_b", [2, DFF]); din("ffn_w_down", [2, DFF, D]); din("g_final", [D])
    dout("y_prompt", [T, D]); dout("y_sample", [NS, D])
    dout("lru_h_prompt", [2, 1024]); dout("lru_h_sample", [2, NS, 1024])
    dout("lru_conv_prompt", [2, 3, 1024]); dout("lru_conv_sample", [2, NS, 3, 1024])
    dout("hgrn_prompt", [2, 8, 128, 128]); dout("hgrn_sample", [2, NS, 8, 128, 128])
    dout("ssd_prompt", [2, 16, 128, 64]); dout("ssd_sample", [2, NS, 16, 128, 64])
    dout("ssd_conv_prompt", [2, 3, 1536]); dout("ssd_conv_sample", [2, NS, 3, 1536])
    dout("ret_prompt", [2, 8, 128, 128]); dout("ret_sample", [2, NS, 8, 128, 128])
    dout("ffn_conv_prompt", [2, 2, DFF]); dout("ffn_conv_sample", [2, NS, 2, DFF])

    with ExitStack() as st:
        c = Ctx(nc, st)
        _program(c, nc, di, do, T, NT)
        c.finish()
    return nc

def _program(c, nc, di, do, T, NT):
    dve, act, pool, pe, sp = c.dve, c.act, c.pool, c.pe, c.sp
    PI = math.pi
    NB = T // 128

    def OP(eng, fn, out, ins, *a, **k):
        return c.op(eng, fn, out, ins, *a, **k)

    def TT(out, a, b, op, eng=None):
        return c.op(eng or dve, "tensor_tensor", out, [a, b], op)

    def TS(out, a, s1, s2, op0, op1=ALU.bypass, eng=None):
        return c.op(eng or dve, "tensor_scalar", out, [a, s1, s2], op0, op1)

    def STT(out, a, s, b, op0, op1):
        return c.op(dve, "scalar_tensor_tensor", out, [a, s, b], op0, op1)

    def ACT(out, a, f, **k):
        return c.op(act, "activation", out, [a], f, **k)

    def CP(out, a, eng=None):
        e = eng or dve
        if e is act:
            return c.op(act, "activation", out, [a], AF.Copy)
        return c.op(e, "tensor_copy", out, [a])

    def MS(buf_view, val, eng=None):
        return c.op(eng or dve, "memset", buf_view, [], val)

    def ASEL(out, in_, pattern, cmp, base, cm):
        return c.op(pool, "affine_select", out, [in_], pattern=pattern, compare_op=cmp, fill=0.0,
                    base=base, channel_multiplier=cm)

    PS = Pool(c, "ps", 6, [128, 512], F32, psum=True)
    PSB = Pool(c, "psb", 2, [128, 1024], BF16, psum=True)
    WP = Pool(c, "wbuf", 3, [128, 4096], BF16)
    TA = Pool(c, "ta", 5, [128, 512], F32)
    TBp = Pool(c, "tb", 4, [128, 512], BF16)
    TSm = Pool(c, "tsm", 8, [128, 64], F32)

    ones = c.sbuf("ones", [128, 128], F32); MS(ones[:], 1.0)
    onesb = c.sbuf("onesb", [128, 128], BF16); MS(onesb[:], 1.0)
    ident = c.sbuf("ident", [128, 128], F32)
    ASEL(ident[:], ones[:], [[-1, 128]], ALU.is_equal, 0, 1)
    identb = c.sbuf("identb", [128, 128], BF16); CP(identb[:], ident[:])
    causal = c.sbuf("causal", [128, 128], F32)
    ASEL(causal[:], ones[:], [[1, 128]], ALU.is_ge, 0, -1)
    trirev = c.sbuf("trirev", [128, 128], F32)
    ASEL(trirev[:], ones[:], [[-1, 128]], ALU.is_gt, 0, 1)
    same = c.sbuf("same", [128, 4, 32], F32)
    tmpc = c.sbuf("tmpc", [128, 4, 32], F32)
    ASEL(tmpc[:], bc(ones[:, 0:1], [[0, 4], [0, 32]]), [[-32, 4], [0, 32]], ALU.is_ge, 0, 1)
    ASEL(same[:], tmpc[:], [[32, 4], [0, 32]], ALU.is_ge, 31, -1)
    samef = same.v(same.t[:].rearrange("p c j -> p (c j)"))
    tri32 = c.sbuf("tri32", [128, 128], F32); TT(tri32[:], causal[:], samef, ALU.mult)
    rev32 = c.sbuf("rev32", [128, 128], F32); TT(rev32[:], trirev[:], samef, ALU.mult)
    mbd = tri32
    ind = c.sbuf("ind", [128, 4], F32); CP(ind[:], same[:, :, 0])
    negm = c.sbuf("negm", [128, 128], F32)
    TS(negm[:], causal[:], 30000.0, -30000.0, ALU.mult, ALU.add)
    dti = c.sbuf("dti", [128, 128], I32)
    c.op(pool, "iota", dti[:], [], pattern=[[1, 128]], base=0, channel_multiplier=-1)
    dtf = c.sbuf("dtf", [128, 128], F32); CP(dtf[:], dti[:])
    GM = c.sbuf("GM", [128, 8, 128], F32)
    pidx_i = c.sbuf("pidx_i", [128, 1], I32)
    c.op(pool, "iota", pidx_i[:], [], pattern=[[0, 1]], base=0, channel_multiplier=1)
    pidx = c.sbuf("pidx", [128, 1], F32); CP(pidx[:], pidx_i[:])
    pp1 = c.sbuf("pp1", [128, 1], F32); TS(pp1[:], pidx[:], 1.0, None, ALU.add)
    prv = c.sbuf("prv", [128, 1], F32); TS(prv[:], pidx[:], -1.0, 127.0, ALU.mult, ALU.add)
    gpow = c.sbuf("gpow", [128, 8], F32)
    grev = c.sbuf("grev", [128, 8], F32)
    for h in range(8):
        ACT(GM[:, h, :], dtf[:], AF.Exp, scale=LOG_GAMMA[h])
        TT(GM[:, h, :], GM[:, h, :], causal[:], ALU.mult)
        ACT(gpow[:, h:h + 1], pp1[:], AF.Exp, scale=LOG_GAMMA[h])
        ACT(grev[:, h:h + 1], prv[:], AF.Exp, scale=LOG_GAMMA[h])
    fi = c.sbuf("fi", [128, 64], I32)
    c.op(pool, "iota", fi[:], [], pattern=[[1, 64]], base=0, channel_multiplier=0)
    FR = c.sbuf("FR", [128, 64], F32); CP(FR[:], fi[:])
    ACT(FR[:], FR[:], AF.Exp, scale=-math.log(10000.0) / 64.0)
    RB = 2
    cosT = c.sbuf("cosT", [128, RB, 64], F32)
    sinT = c.sbuf("sinT", [128, RB, 64], F32)
    angb = c.sbuf("angb", [128, RB, 64], F32)
    ang2 = c.sbuf("ang2", [128, RB, 64], F32)
    angi = c.sbuf("angi", [128, RB, 64], I32)
    angm = c.sbuf("angm", [128, RB, 64], F32)
    posf = c.sbuf("posf", [128, RB], F32)

    def sin_of(out, src, P, nb):
        a2 = ang2[0:P, 0:nb, :]; ai = angi[0:P, 0:nb, :]; am = angm[0:P, 0:nb, :]
        TS(a2, src, 1.0 / (2 * PI), None, ALU.mult)
        CP(ai, a2)
        CP(a2, ai)
        STT(a2, a2, -2 * PI, src, ALU.mult, ALU.add)
        TS(am, a2, PI, None, ALU.is_gt)
        STT(a2, am, -2 * PI, a2, ALU.mult, ALU.add)
        TS(am, a2, -PI, None, ALU.is_lt)
        STT(a2, am, 2 * PI, a2, ALU.mult, ALU.add)
        TS(a2, a2, PI, -PI, ALU.min, ALU.max)
        ACT(out, a2, AF.Sin)

    def make_rope(tok0, nb):
        for b_ in range(nb):
            TS(posf[:, b_:b_ + 1], pidx[:, 0:1], float(tok0 + 128 * b_), None, ALU.add)
        TT(angb[:, 0:nb, :], bc(posf[:, 0:1], [[1, nb], [0, 64]]), bc(FR[:, 0:1], [[0, nb], [1, 64]]), ALU.mult)
        sin_of(sinT[:, 0:nb, :], angb[:, 0:nb, :], 128, nb)
        TS(angb[:, 0:nb, :], angb[:, 0:nb, :], PI / 2, None, ALU.add)
        sin_of(cosT[:, 0:nb, :], angb[:, 0:nb, :], 128, nb)

    cosS = c.sbuf("cosS", [1, 1, 64], F32)
    sinS = c.sbuf("sinS", [1, 1, 64], F32)
    TS(angb[0:1, 0, :], FR[0:1, :], 16384.0, None, ALU.mult)
    sin_of(sinS[:, :, :], angb[0:1, 0:1, :], 1, 1)
    TS(angb[0:1, 0:1, :], angb[0:1, 0:1, :], PI / 2, None, ALU.add)
    sin_of(cosS[:, :, :], angb[0:1, 0:1, :], 1, 1)
    TTp = Pool(c, "ttp", 2, [128, 384], BF16)
    PMp = Pool(c, "pmp", 2, [128, 128], BF16)
    SBFp = Pool(c, "sbfp", 2, [128, 128], BF16)
    SP0p = Pool(c, "sp0p", 2, [1, 1024], F32)
    SSp = Pool(c, "ssp", 2, [128, 4, 128], F32)
    BTp = Pool(c, "btp", 2, [128, 128], BF16)
    SB4p = Pool(c, "sb4p", 2, [128, 512], BF16)
    QHp = Pool(c, "qhp", 2, [128, 128], BF16)
    C_HSC = c.sbuf("C_HSC", [128, 12, 3, NS], F32)
    C_XSC = c.sbuf("C_XSC", [128, 12, NS], F32)
    C_XCS = c.sbuf("C_XCS", [128, 6, NS], F32)
    SPB = c.sbuf("SPB", [NS, 1024], F32)

    plist = [("g_mix", di["g_mix"].rearrange("l (c p) -> (l c) p", p=128)),
             ("g_ffn", di["g_ffn"].rearrange("l (c p) -> (l c) p", p=128)),
             ("g_final", di["g_final"].rearrange("(c p) -> c p", p=128)),
             ("lru_conv_w", di["lru_conv_w"].rearrange("l i (j p) -> (l i j) p", p=128)),
             ("lru_conv_b", di["lru_conv_b"].rearrange("l (j p) -> (l j) p", p=128)),
             ("lru_ba", di["lru_ba"].rearrange("l j p -> (l j) p")),
             ("lru_bx", di["lru_bx"].rearrange("l j p -> (l j) p")),
             ("lru_lambda", di["lru_lambda"].rearrange("l (j p) -> (l j) p", p=128)),
             ("hg_norm_w", di["hg_norm_w"]),
             ("ssd_conv_w", di["ssd_conv_w"].rearrange("l i (j p) -> (l i j) p", p=128)),
             ("ssd_conv_b", di["ssd_conv_b"].rearrange("l (j p) -> (l j) p", p=128)),
             ("ffn_conv_w", di["ffn_conv_w"].rearrange("l i (j p) -> (l i j) p", p=128)),
             ("ffn_conv_b", di["ffn_conv_b"].rearrange("l (j p) -> (l j) p", p=128)),
             ("ssd_norm_w", di["ssd_norm_w"].rearrange("l (j p) -> (l j) p", p=128))]
    tot = sum(a.shape[0] for _, a in plist)
    ntile = (tot + 127) // 128
    PVT = c.sbuf("PVT", [128, ntile * 128], F32)
    prow = c.sbuf("prow", [128, 128], F32)
    poff = {}
    g = 0
    segs = []
    for name, a in plist:
        poff[name] = g
        R = a.shape[0]
        r = 0
        while r < R:
            ti, ro = divmod(g + r, 128)
            m = min(R - r, 128 - ro)
            segs.append((ti, ro, a[r:r + m, :], m))
            r += m
        g += R
    for ti in range(ntile):
        MS(prow[:], 0.0)
        for (t2, ro, ap_, m) in segs:
            if t2 == ti:
                c.dma(sp, prow[ro:ro + m, :], ap_)
        pst = PS.get()
        c.tr(pst[:, 0:128], prow[:], ident[:])
        CP(PVT[:, ti * 128:(ti + 1) * 128], pst[:, 0:128])

    def pv(name, idx, n=1):
        o = poff[name] + idx
        return PVT[:, o:o + n]

    def bload(name, src2d_row, ncol):
        b = c.sbuf(name, [128, ncol], F32)
        a = src2d_row
        c.dma(sp, b[:], bass.AP(a.tensor, a.offset, [[0, 128], [1, ncol]]))
        return b

    LB1 = bload("lb1", di["hg_lb_logits"][1, :], 1024)
    OML1 = bload("oml1", di["hg_lb_logits"][0, :], 1024)
    TT(OML1[:], LB1[:], OML1[:], ALU.subtract)
    ACT(LB1[:], OML1[:], AF.Sigmoid)
    ACT(OML1[:], OML1[:], AF.Sigmoid, scale=-1.0)
    DTB = [bload(f"dtb{l}", di["ssd_dt_bias"][l, :], 16) for l in range(2)]
    NEGA = [bload(f"nega{l}", di["ssd_a_log"][l, :], 16) for l in range(2)]
    SSD_D = [bload(f"ssdd{l}", di["ssd_d"][l, :], 16) for l in range(2)]
    for l in range(2):
        ACT(NEGA[l][:], NEGA[l][:], AF.Exp)
        TS(NEGA[l][:], NEGA[l][:], -1.0, None, ALU.mult)
    LCC = c.sbuf("lcc", [128, 16], F32)
    LCC2 = c.sbuf("lcc2", [128, 16], F32)
    ACT(LCC[:], pv("lru_lambda", 0, 16), AF.Exp, scale=-1.0)
    ACT(LCC[:], LCC[:], AF.Ln, bias=1.0)
    TS(LCC2[:], LCC[:], -16.0, None, ALU.mult)
    TS(LCC[:], LCC[:], -8.0, None, ALU.mult)
    LWA = c.sbuf("lwa", [128, 2, 8, 128], BF16)
    LWX = c.sbuf("lwx", [128, 2, 8, 128], BF16)
    c.dma(pool, LWA[:], di["lru_wa"].rearrange("l n c d -> c l n d"))
    c.dma(pool, LWX[:], di["lru_wx"].rearrange("l n c d -> c l n d"))

    def zbuf(name, shape, dt=F32):
        b = c.sbuf(name, shape, dt)
        MS(b[:], 0.0)
        return b
    H_LRU = zbuf("H_LRU", [128, 2, 8])
    HI_LRU = zbuf("HI_LRU", [128, 2, 8, 3])
    HI_SSD = zbuf("HI_SSD", [128, 2, 12, 3])
    HI_FFN = zbuf("HI_FFN", [128, 2, 44, 2])
    S_B = zbuf("S_B", [128, 2, 8, 128])
    S_C = zbuf("S_C", [128, 2, 16, 64])
    S_D = zbuf("S_D", [128, 2, 8, 128])

    TILE = 256 if T > 256 else T
    xP = c.sbuf("xP", [128, 16, TILE], F32)
    xS = c.sbuf("xS", [128, 16, NS], F32)
    hP = c.sbuf("hP", [128, 16, TILE], BF16)
    hS = c.sbuf("hS", [128, 16, NS], BF16)
    NTK = TILE + NS
    UBN = max(32 * NTK + 2 * 4 * 1024 + max(2 * 6 * (TILE + 3), 2 * 9 * 256 + 2 * 6 * 256 + 4 * (TILE + 3)) + 64, 44 * NTK + 2 * 2 * (TILE + 2) + 2 * 44 * 3 * NS)
    UB = c.sbuf("UB", [128, UBN], BF16)

    def carve(name, off, shape, dt):
        sz = 2 if dt in (F32, I32) else 1
        n = int(np.prod(shape[1:])) * sz
        a = UB.t[0:shape[0], off:off + n]
        if sz == 2:
            a = a.bitcast(dt)
        if len(shape) == 3:
            a = a.rearrange("p (a b) -> p a b", a=shape[1])
        elif len(shape) == 4:
            a = a.rearrange("p (a b c) -> p a b c", a=shape[1], b=shape[2])
        return Buf(c, a, name), off + n

    def barrier():
        engs = [c.pe, c.dve, c.act, c.pool, c.sp]
        for e in engs:
            for f in engs:
                if f is not e and f.cnt > 0:
                    e.need((f.sem, f.cnt))
                if f is not e:
                    for tok in f.old:
                        e.need(tok)
            for b in c.dma_bufs:
                e.need((b.dsem, b.dcnt))

    oneb = c.sbuf("oneb", [128, 1], F32); MS(oneb[:], 1.0)
    MACC = c.sbuf("MACC", [128, 2, TILE + NS], F32)
    RSB = c.sbuf("RSB", [128, 512], F32)
    LS_HS = c.sbuf("LS_HS", [128, 8, 3, NS], F32)
    LS_H0 = c.sbuf("LS_H0", [128, 8, NS], F32)
    LS_XA = c.sbuf("LS_XA", [128, 8, NS], F32)
    LS_HN = c.sbuf("LS_HN", [128, 8, NS], F32)
    LS_GY = c.sbuf("LS_GY", [128, 8, NS], F32)
    sl = lambda v, a, n_: V(v.buf, v.ap[:, a:a + n_])
    K_ = K()
    K_.__dict__.update(locals())
    return _program2(K_)

def _program2(k):
    c, nc, di, do, T = k.c, k.nc, k.di, k.do, k.T
    dve, act, pool, pe, sp = k.dve, k.act, k.pool, k.pe, k.sp
    TT, TS, STT, ACT, CP, MS = k.TT, k.TS, k.STT, k.ACT, k.CP, k.MS
    PS, PSB, WP, TA, TBp, TSm = k.PS, k.PSB, k.WP, k.TA, k.TBp, k.TSm
    ones, ident, identb, pv = k.ones, k.ident, k.identb, k.pv
    xP, xS, hP, hS, TILE, carve, barrier = k.xP, k.xS, k.hP, k.hS, k.TILE, k.carve, k.barrier
    tiles = []
    t0 = 0
    while t0 < T:
        n = min(TILE, T - t0)
        tiles.append((t0, n))
        t0 += n

    def wchunk(w2d, k0, kc, col0, ncols, q=None):
        wb = WP.get()
        v = wb.t[:, 0:kc * ncols].rearrange("p (k n) -> p k n", k=kc)
        src = w2d[k0 * 128:(k0 + kc) * 128, col0:col0 + ncols].rearrange("(k p) n -> p k n", p=128)
        c.dma(pool, wb.v(v), src)
        return wb, v

    def dense_fm(w2d, kc, c0, ncols, srcs, sink, cw=256):
        for col0 in range(c0, c0 + ncols, cw):
            n_ = min(cw, c0 + ncols - col0)
            wb, wv = wchunk(w2d, 0, kc, col0, n_)
            for j in range(0, n_, 128):
                m = min(128, n_ - j)
                for gi, (hv, nt) in enumerate(srcs):
                    ps = PS.get()
                    for kk in range(kc):
                        c.mm(ps[0:m, 0:nt], wb.v(wv[:, kk, j:j + m]), hv(kk), start=(kk == 0), stop=(kk == kc - 1))
                    sink(gi, col0 + j, m, ps)

    def dense_tm(w2d, kc, c0, ncols, srcs, sink, cw=256):
        for col0 in range(c0, c0 + ncols, cw):
            n_ = min(cw, c0 + ncols - col0)
            wb, wv = wchunk(w2d, 0, kc, col0, n_)
            for gi, (hv, nt) in enumerate(srcs):
                for tt in range(0, nt, 128):
                    P = min(128, nt - tt)
                    ps = PS.get()
                    for kk in range(kc):
                        c.mm(ps[0:P, 0:n_], hv(kk, tt, P), wb.v(wv[:, kk, :]), start=(kk == 0), stop=(kk == kc - 1))
                    sink(gi, tt, P, col0, n_, ps)

    def tm2fm(dst_fn, src, P, ncol, dt=F32):
        idn = ident if dt == F32 else identb
        for j in range(ncol // 128):
            if dt == F32:
                ps = PS.get()
                pv_ = ps[:, 0:P]
            else:
                ps = PSB.get()
                pv_ = ps[:, 0:P]
            c.tr(pv_, src(j * 128, 128), idn[0:P, 0:P])
            CP(dst_fn(j), pv_, eng=act)

    def rmsnorm(xv, hv, n, gname, gi):
        ps = PS.get()
        for cc in range(16):
            sq = TA.get()
            ACT(sq[:, 0:n], xv(cc), AF.Square)
            c.mm(ps[:, 0:n], ones[:, :], sq[:, 0:n], start=(cc == 0), stop=(cc == 15), inc=True)
        rs = TA.get()
        ACT(rs[:, 0:n], ps[:, 0:n], AF.Sqrt, scale=1.0 / D, bias=k.epsb[:, 0:1])
        c.op(dve, "reciprocal", rs[:, 0:n], [rs[:, 0:n]])
        for cc in range(16):
            STT(hv(cc), xv(cc), pv(gname, gi * 16 + cc), rs[:, 0:n], ALU.mult, ALU.mult)

    epsb = c.sbuf("epsb", [128, 1], F32); MS(epsb[:], EPS)
    k.epsb = epsb

    def conv_fm(out, taps, wname, bname, l, ntap, j, nj):
        TS(out, taps[ntap - 1], pv(wname, (l * ntap + ntap - 1) * nj + j), pv(bname, l * nj + j), ALU.mult, ALU.add)
        for i in range(ntap - 1):
            STT(out, taps[i], pv(wname, (l * ntap + i) * nj + j), out, ALU.mult, ALU.add)

    for ti, (tok0, n) in enumerate(tiles):
        has_s = (ti == 0)
        nblk = n // 128
        for b in range(nblk):
            for q4 in range(4):
                xr = TA.get()
                c.dma(sp, xr[:, :], di["x_prompt"][tok0 + b * 128: tok0 + (b + 1) * 128, q4 * 512:(q4 + 1) * 512])
                ps = PS.get()
                for jj in range(4):
                    c.tr(ps[:, jj * 128:(jj + 1) * 128], xr[:, jj * 128:(jj + 1) * 128], ident[:, :])
                CP(xP[:, q4 * 4:(q4 + 1) * 4, b * 128:(b + 1) * 128], ps.v(ps.t[:, :].rearrange("p (a b) -> p a b", a=4)))
        if has_s:
            for q4 in range(4):
                xr = TA.get()
                c.dma(sp, xr[0:NS, :], di["x_sample"][:, q4 * 512:(q4 + 1) * 512])
                ps = PS.get()
                for jj in range(4):
                    c.tr(ps[:, jj * NS:(jj + 1) * NS], xr[0:NS, jj * 128:(jj + 1) * 128], ident[0:NS, 0:NS])
                CP(xS[:, q4 * 4:(q4 + 1) * 4, :], ps.v(ps.t[:, 0:4 * NS].rearrange("p (a b) -> p a b", a=4)))

        for l in range(DEPTH):
            last = (ti == len(tiles) - 1)
            off = 0
            oTP, off = carve("oTP", off, [128, 32, n], BF16)
            oTS, off = carve("oTS", off, [128, 32, NS], BF16)
            PB, _ = carve("PB", off, [128, 4, 1024], F32)
            GYb, off = carve("GY", off, [128, 8, 512], F32)
            k.PBv8 = GYb
            off_fmb = off
            FMB, off = carve("FMB", off, [128, 6, n + 3], F32)
            rmsnorm(lambda cc: xP[:, cc, 0:n], lambda cc: hP[:, cc, 0:n], n, "g_mix", l)
            if has_s:
                rmsnorm(lambda cc: xS[:, cc, :], lambda cc: hS[:, cc, :], NS, "g_mix", l)
            srcs_fm = [(lambda kk: hP[:, kk, 0:n], n)] + ([(lambda kk: hS[:, kk, :], NS)] if has_s else [])
            srcs_tm = [(lambda kk, tt, P: hP[:, kk, tt:tt + P], n)] + ([(lambda kk, tt, P: hS[:, kk, tt:tt + P], NS)] if has_s else [])
            w_in = di["w_in"][l]
            MS(oTP[:, :, :], 0.0)
            if has_s:
                MS(oTS[:, :, :], 0.0)

            mixer_lru(k, l, ti, n, has_s, last, srcs_fm, oTP, oTS, dense_fm, conv_fm, tm2fm, FMB)
            barrier()
            DWb, _ = carve("DWb", off_fmb, [128, 9, 256], F32)
            mixer_B(k, l, ti, tok0, n, has_s, last, srcs_fm, srcs_tm, oTP, oTS, dense_fm, dense_tm, PB, DWb)
            barrier()
            mixer_C(k, l, ti, tok0, n, has_s, last, srcs_fm, srcs_tm, oTP, oTS, dense_fm, dense_tm, conv_fm, PB, off_fmb)
            barrier()
            DW, _ = carve("DW", off_fmb, [128, 9, 256], F32)
            mixer_D(k, l, ti, tok0, n, has_s, last, srcs_tm, oTP, oTS, dense_tm, PB, DW)

            barrier()
            MT, _ = carve("MT", 32 * (n + NS), [128, 16, n + NS], BF16)
            wbr = di["w_branch"][l]
            wg = di["w_gate"][l]
            groups = [(oTP, hP, n, 0)] + ([(oTS, hS, NS, n)] if has_s else [])
            for jb in range(0, 16, 2):
                accs = {}
                for br in range(4):
                    wbb, wbv = wchunk(wbr[br], 0, 8, jb * 128, 256)
                    wgb, wgv = wchunk(wg[:, br, :], 0, 16, jb * 128, 256)
                    for j2 in range(2):
                        for gi, (oT_, h_, nt, moff) in enumerate(groups):
                            psb_ = PS.get()
                            for kk in range(8):
                                c.mm(psb_[:, 0:nt], wbb.v(wbv[:, kk, j2 * 128:(j2 + 1) * 128]), oT_[:, br * 8 + kk, 0:nt], start=(kk == 0), stop=(kk == 7))
                            psg = PS.get()
                            for kk in range(16):
                                c.mm(psg[:, 0:nt], wgb.v(wgv[:, kk, j2 * 128:(j2 + 1) * 128]), h_[:, kk, 0:nt], start=(kk == 0), stop=(kk == 15))
                            sg = TA.get()
                            ACT(sg[:, 0:nt], psg[:, 0:nt], AF.Sigmoid)
                            mv = k.MACC[:, j2, moff:moff + nt]
                            if br == 0:
                                TT(mv, sg[:, 0:nt], psb_[:, 0:nt], ALU.mult)
                            else:
                                TT(sg[:, 0:nt], sg[:, 0:nt], psb_[:, 0:nt], ALU.mult)
                                TT(mv, mv, sg[:, 0:nt], ALU.add)
                            if br == 3:
                                CP(MT[:, jb + j2, moff:moff + nt], mv, eng=act)
            def sink_res(xg):
                def f(gi, col, m, ps):
                    xv = (xP[:, col // 128, 0:n] if gi == 0 else xS[:, col // 128, :])
                    nt = n if gi == 0 else NS
                    TT(xv, xv, ps[:, 0:nt], ALU.add)
                return f
            msrc = [(lambda kk: MT[:, kk, 0:n], n)] + ([(lambda kk: MT[:, kk, n:n + NS], NS)] if has_s else [])
            dense_fm(di["w_out"][l], 16, 0, D, msrc, sink_res(None))
            barrier()
            aT, _ = carve("aT", 0, [128, 44, n + NS], BF16)
            rmsnorm(lambda cc: xP[:, cc, 0:n], lambda cc: hP[:, cc, 0:n], n, "g_ffn", l)
            if has_s:
                rmsnorm(lambda cc: xS[:, cc, :], lambda cc: hS[:, cc, :], NS, "g_ffn", l)
            ffn_phase(k, l, ti, n, has_s, last, srcs_fm, aT, wchunk, conv_fm, tm2fm)
            asrc = [(lambda kk: aT[:, kk, 0:n], n)] + ([(lambda kk: aT[:, kk, n:n + NS], NS)] if has_s else [])
            wd = di["ffn_w_down"][l]
            for jb in range(16):
                pss = [PS.get() for _ in asrc]
                kgs = [(0, 16), (16, 16), (32, 12)]
                for (k0, kcn) in kgs:
                    wb, wv = wchunk(wd, k0, kcn, jb * 128, 128)
                    for gi, (av, nt) in enumerate(asrc):
                        for kk in range(kcn):
                            c.mm(pss[gi][:, 0:nt], wb.v(wv[:, kk, :]), av(k0 + kk), start=(k0 + kk == 0), stop=(k0 + kk == 43), inc=(kk == kcn - 1))
                for gi, (av, nt) in enumerate(asrc):
                    xv = (xP[:, jb, 0:n] if gi == 0 else xS[:, jb, :])
                    TT(xv, xv, pss[gi][:, 0:nt], ALU.add)
            barrier()

        def final(xv, nt, ydram):
            ps = PS.get()
            for cc in range(16):
                sq = TA.get()
                ACT(sq[:, 0:nt], xv(cc), AF.Square)
                c.mm(ps[:, 0:nt], ones[:, :], sq[:, 0:nt], start=(cc == 0), stop=(cc == 15), inc=True)
            rs = k.RSB
            ACT(rs[:, 0:nt], ps[:, 0:nt], AF.Sqrt, scale=1.0 / D, bias=epsb[:, 0:1])
            c.op(dve, "reciprocal", rs[:, 0:nt], [rs[:, 0:nt]])
            for tt in range(0, nt, 128):
                P = min(128, nt - tt)
                for q4 in range(4):
                    yt = TA.get()
                    pso = PS.get()
                    for jj in range(4):
                        cc = q4 * 4 + jj
                        yn = TA.get()
                        STT(yn[:, 0:P], k.sl(xv(cc), tt, P), pv("g_final", cc), rs[:, tt:tt + P], ALU.mult, ALU.mult)
                        c.tr(pso[0:P, jj * 128:(jj + 1) * 128], yn[:, 0:P], ident[:, :])
                    CP(yt[0:P, :], pso[0:P, :], eng=act)
                    c.dma(sp, ydram[tt:tt + P, q4 * 512:(q4 + 1) * 512], yt[0:P, :])
        final(lambda cc: xP[:, cc, 0:n], n, do["y_prompt"][tok0:tok0 + n, :])
        if has_s:
            final(lambda cc: xS[:, cc, :], NS, do["y_sample"])
        barrier()

def load_tm2fm(k, dram2d, ncol, dst_fn):
    c = k.c
    for c0 in range(0, ncol, 512):
        w = min(512, ncol - c0)
        xr = k.TA.get()
        c.dma(k.sp, xr[0:NS, 0:w], dram2d[:, c0:c0 + w])
        for jj in range(w // 128):
            ps = k.PS.get()
            c.tr(ps[:, 0:NS], xr[0:NS, jj * 128:(jj + 1) * 128], k.ident[0:NS, 0:NS])
            k.CP(dst_fn(c0 // 128 + jj), ps[:, 0:NS], eng=k.act)


def store_fm2tm(k, src_fn, nch, dram2d, P=NS):
    c = k.c
    for j0 in range(0, nch, 4):
        m = min(4, nch - j0)
        ps = k.PS.get()
        for jj in range(m):
            c.tr(ps[0:P, jj * 128:(jj + 1) * 128], src_fn(j0 + jj), k.ident[:, :])
        yt = k.TA.get()
        k.CP(yt[0:P, 0:m * 128], ps[0:P, 0:m * 128], eng=k.act)
        c.dma(k.sp, dram2d[:, j0 * 128:(j0 + m) * 128], yt[0:P, 0:m * 128])


def mixer_lru(k, l, ti, n, has_s, last, srcs_fm, oTP, oTS, dense_fm, conv_fm, tm2fm, FMB):
    c, di, do = k.c, k.di, k.do
    TT, TS, STT, ACT, CP, MS, TA, TBp, PS, pv = k.TT, k.TS, k.STT, k.ACT, k.CP, k.MS, k.TA, k.TBp, k.PS, k.pv
    w_in = di["w_in"][l]
    GY = k.PBv8
    if has_s:
        HS, H0S, XAS, HNS, GYS = k.LS_HS, k.LS_H0, k.LS_XA, k.LS_HN, k.LS_GY
        for i in range(3):
            load_tm2fm(k, di["state_lru_conv"][l, :, i, :], 1024, lambda j, i=i: HS[:, j, i, :])
        load_tm2fm(k, di["state_lru_h"][l], 1024, lambda j: H0S[:, j, :])

    def sink_y(gi, col, m, ps):
        j = (col - O_YA) // 128
        if gi == 0:
            ACT(GY[:, j, 0:n], ps[:, 0:n], AF.Gelu_apprx_tanh)
        else:
            ACT(GYS[:, j, :], ps[:, 0:NS], AF.Gelu_apprx_tanh)
    dense_fm(w_in, 16, O_YA, 1024, srcs_fm, sink_y)

    def lru_core(j, xc, nt, init, hs_out):
        xcb = TBp.get()
        CP(xcb[:, 0:nt], xc, eng=k.act)
        ps_r = PS.get()
        c.mm(ps_r[:, 0:nt], k.LWA[:, l, j, :], xcb[:, 0:nt])
        r = TA.get()
        ACT(r[:, 0:nt], ps_r[:, 0:nt], AF.Sigmoid, bias=pv("lru_ba", l * 8 + j))
        ps_i = PS.get()
        c.mm(ps_i[:, 0:nt], k.LWX[:, l, j, :], xcb[:, 0:nt])
        a = TA.get()
        ACT(a[:, 0:nt], r[:, 0:nt], AF.Exp, scale=k.LCC[:, l * 8 + j:l * 8 + j + 1])
        ACT(r[:, 0:nt], r[:, 0:nt], AF.Exp, scale=k.LCC2[:, l * 8 + j:l * 8 + j + 1])
        ACT(r[:, 0:nt], r[:, 0:nt], AF.Sqrt, scale=-1.0, bias=k.oneb[:, 0:1])
        ig = TA.get()
        ACT(ig[:, 0:nt], ps_i[:, 0:nt], AF.Sigmoid, bias=pv("lru_bx", l * 8 + j))
        TT(ig[:, 0:nt], ig[:, 0:nt], xc, ALU.mult)
        TT(ig[:, 0:nt], ig[:, 0:nt], r[:, 0:nt], ALU.mult)
        return a, ig

    def sink_x(gi, col, m, ps):
        j = (col - O_XA) // 128
        if gi == 0:
            xpad = FMB[:, j % 6, 0:n + 3]
            CP(FMB[:, j % 6, 0:3], k.HI_LRU[:, l, j, :])
            CP(FMB[:, j % 6, 3:n + 3], ps[:, 0:n], eng=k.act)
            xc = TA.get()
            conv_fm(xc[:, 0:n], [FMB[:, j % 6, i:i + n] for i in range(4)], "lru_conv_w", "lru_conv_b", l, 4, j, 8)
            CP(k.HI_LRU[:, l, j, :], FMB[:, j % 6, n:n + 3])
            a, u = lru_core(j, xc[:, 0:n], n, None, None)
            hs = TA.get()
            c.op(k.dve, "tensor_tensor_scan", hs[:, 0:n], [a[:, 0:n], u[:, 0:n], k.H_LRU[:, l, j:j + 1]], ALU.mult, ALU.add)
            CP(k.H_LRU[:, l, j:j + 1], hs[:, n - 1:n])
            TT(oTP[:, j, 0:n], hs[:, 0:n], GY[:, j, 0:n], ALU.mult)
        else:
            CP(XAS[:, j, :], ps[:, 0:NS], eng=k.act)
            xc = TA.get()
            conv_fm(xc[:, 0:NS], [HS[:, j, 0, :], HS[:, j, 1, :], HS[:, j, 2, :], XAS[:, j, :]], "lru_conv_w", "lru_conv_b", l, 4, j, 8)
            a, u = lru_core(j, xc[:, 0:NS], NS, None, None)
            hs = TA.get()
            TT(hs[:, 0:NS], a[:, 0:NS], H0S[:, j, :], ALU.mult)
            TT(HNS[:, j, :], hs[:, 0:NS], u[:, 0:NS], ALU.add)
            TT(oTS[:, j, :], HNS[:, j, :], GYS[:, j, :], ALU.mult)
    dense_fm(w_in, 16, O_XA, 1024, srcs_fm, sink_x)

    if has_s:
        store_fm2tm(k, lambda j: HNS[:, j, :], 8, do["lru_h_sample"][l])
        store_fm2tm(k, lambda j: XAS[:, j, :], 8, do["lru_conv_sample"][l, :, 2, :])
        for i in range(2):
            c.dma(k.sp, do["lru_conv_sample"][l, :, i, :], di["state_lru_conv"][l, :, i + 1, :])
    if last:
        store_fm2tm(k, lambda jj: k.H_LRU[:, l, :], 1, do["lru_h_prompt"][l].rearrange("(j p) -> j p", p=128), P=8)
        for i in range(3):
            store_fm2tm(k, lambda jj, i=i: k.HI_LRU[:, l, :, i], 1, do["lru_conv_prompt"][l, i].rearrange("(j p) -> j p", p=128), P=8)


def ffn_phase(k, l, ti, n, has_s, last, srcs_fm, aT, wchunk, conv_fm, tm2fm):
    c, di, do = k.c, k.di, k.do
    TT, TS, STT, ACT, CP, MS, TA, TBp, PS, pv = k.TT, k.TS, k.STT, k.ACT, k.CP, k.MS, k.TA, k.TBp, k.PS, k.pv
    off = 44 * (n + NS)
    UP, off = k.carve("UP", off, [128, 2, n + 2], F32)
    if has_s:
        HSF, off = k.carve("HSF", off, [128, 44, 2, NS], F32)
        US, off = k.carve("US", off, [128, 44, NS], F32)
        for i in range(2):
            load_tm2fm(k, di["state_ffn_conv"][l, :, i, :], DFF, lambda j, i=i: HSF[:, j, i, :])
    wu = di["ffn_w_up"][l]
    wv_ = di["ffn_w_val"][l]
    for j0 in range(0, 44, 2):
        wub, wuv = wchunk(wu, 0, 16, j0 * 128, 256)
        wvb, wvv = wchunk(wv_, 0, 16, j0 * 128, 256)
        for j2 in range(2):
            j = j0 + j2
            for gi, (hv, nt) in enumerate(srcs_fm):
                psu = PS.get()
                for kk in range(16):
                    c.mm(psu[:, 0:nt], wub.v(wuv[:, kk, j2 * 128:(j2 + 1) * 128]), hv(kk), start=(kk == 0), stop=(kk == 15))
                psv = PS.get()
                for kk in range(16):
                    c.mm(psv[:, 0:nt], wvb.v(wvv[:, kk, j2 * 128:(j2 + 1) * 128]), hv(kk), start=(kk == 0), stop=(kk == 15))
                uc = TA.get()
                if gi == 0:
                    s = j % 2
                    CP(UP[:, s, 0:2], k.HI_FFN[:, l, j, :])
                    CP(UP[:, s, 2:n + 2], psu[:, 0:n], eng=k.act)
                    conv_fm(uc[:, 0:n], [UP[:, s, i:i + n] for i in range(3)], "ffn_conv_w", "ffn_conv_b", l, 3, j, 44)
                    CP(k.HI_FFN[:, l, j, :], UP[:, s, n:n + 2])
                    ACT(uc[:, 0:n], uc[:, 0:n], AF.Gelu_apprx_tanh)
                    TT(aT[:, j, 0:n], uc[:, 0:n], psv[:, 0:n], ALU.mult)
                else:
                    CP(US[:, j, :], psu[:, 0:NS], eng=k.act)
                    conv_fm(uc[:, 0:NS], [HSF[:, j, 0, :], HSF[:, j, 1, :], US[:, j, :]], "ffn_conv_w", "ffn_conv_b", l, 3, j, 44)
                    ACT(uc[:, 0:NS], uc[:, 0:NS], AF.Gelu_apprx_tanh)
                    TT(aT[:, j, n:n + NS], uc[:, 0:NS], psv[:, 0:NS], ALU.mult)
    if has_s:
        store_fm2tm(k, lambda j: US[:, j, :], 44, do["ffn_conv_sample"][l, :, 1, :])
        c.dma(k.sp, do["ffn_conv_sample"][l, :, 0, :], di["state_ffn_conv"][l, :, 1, :])
    if last:
        for i in range(2):
            store_fm2tm(k, lambda jj, i=i: k.HI_FFN[:, l, :, i], 1, do["ffn_conv_prompt"][l, i].rearrange("(j p) -> j p", p=128), P=44)

def rstd_from_ss(k, out, ss, P, denom):
    k.ACT(out, ss, AF.Ln, scale=1.0 / denom, bias=k.epsb[0:P, 0:1])
    k.ACT(out, out, AF.Exp, scale=-0.5)


def tr_bf(k, dst, src, P, ncol=128):
    ps = k.PSB.get()
    k.c.tr(ps[0:ncol, 0:P], src, k.identb[0:P, 0:P])
    k.CP(dst, ps[0:ncol, 0:P], eng=k.act)


def block_D(k, l, P, h0, qv, kv, vv, gv, cosv, sinv, Sv, grevv, dst_fn, DW):
    c = k.c
    TT, TS, STT, ACT, CP, PS, TBp = k.TT, k.TS, k.STT, k.ACT, k.CP, k.PS, k.TBp

    def w(i):
        return DW[0:P, i, :]

    def v4(x):
        return V(x.buf, x.ap.rearrange("p (h a d) -> p h a d", h=2, a=2))

    def bc4(t):
        a = t.ap
        return V(t.buf, bass.AP(a.tensor, a.offset, [list(a.ap[0]), [0, 2], [0, 2], [1, 64]]))

    def bc3(t):
        a = t.ap
        return V(t.buf, bass.AP(a.tensor, a.offset, [list(a.ap[0]), [0, 2], [1, 64]]))

    def half(x, a_):
        x4 = x.ap.rearrange("p (h a d) -> p h a d", h=2, a=2)
        return V(x.buf, x4[:, :, a_, :])

    def rope(dst, src):
        TT(v4(w(0)), v4(src), bc4(cosv), ALU.mult)
        TT(half(w(1), 0), half(src, 1), bc3(sinv), ALU.mult)
        TT(half(w(1), 1), half(src, 0), bc3(sinv), ALU.mult)
        TT(half(dst, 0), half(w(0), 0), half(w(1), 0), ALU.subtract)
        TT(half(dst, 1), half(w(0), 1), half(w(1), 1), ALU.add)

    rope(w(2), qv)
    rope(w(3), kv)
    A = TBp.get(); B = TBp.get(); Cb = TBp.get()
    qr_b, kr_b = A[0:P, 0:256], A[0:P, 256:512]
    qh_b, v_b = B[0:P, 0:256], B[0:P, 256:512]
    vh_b, ob_b = Cb[0:P, 0:256], Cb[0:P, 256:512]
    CP(qr_b, w(2)); CP(kr_b, w(3), eng=k.act)
    for hh in range(2):
        h = h0 + hh
        TS(k.sl(qh_b, hh * 128, 128), k.sl(w(2), hh * 128, 128), k.gpow[0:P, h:h + 1], None, ALU.mult)
        TS(k.sl(vh_b, hh * 128, 128), k.sl(vv, hh * 128, 128), grevv[0:P, h:h + 1], None, ALU.mult)
    CP(v_b, vv, eng=k.act)
    ACT(w(6), gv, AF.Silu)
    for hh in range(2):
        h = h0 + hh
        cs = slice(hh * 128, (hh + 1) * 128)
        Tt = k.TTp.get()
        qT, kT, qhT = Tt[:, 0:P], Tt[:, 128:128 + P], Tt[:, 256:256 + P]
        tr_bf(k, qT, k.sl(qr_b, hh * 128, 128), P)
        tr_bf(k, kT, k.sl(kr_b, hh * 128, 128), P)
        tr_bf(k, qhT, k.sl(qh_b, hh * 128, 128), P)
        S = Sv(hh)
        Sb = k.SBFp.get()
        CP(Sb[:, :], S, eng=k.act)
        ps_sc = PS.get()
        c.mm(ps_sc[0:P, 0:P], kT, qT)
        Pm = k.PMp.get()
        TT(Pm[0:P, 0:P], ps_sc[0:P, 0:P], k.GM[0:P, h, 0:P], ALU.mult)
        ps_o = PS.get()
        c.mm(ps_o[0:P, 0:128], Pm[0:P, 0:P], k.sl(v_b, hh * 128, 128), start=True, stop=False)
        c.mm(ps_o[0:P, 0:128], qhT, Sb[:, :], start=False, stop=True)
        ps_u = PS.get()
        c.mm(ps_u[:, 0:128], k.sl(kr_b, hh * 128, 128), k.sl(vh_b, hh * 128, 128))
        STT(S, S, math.exp(LOG_GAMMA[h] * P), ps_u[:, 0:128], ALU.mult, ALU.add)
        sm = k.TSm.get()
        ACT(k.sl(w(7), 0, 128), ps_o[0:P, 0:128], AF.Square, accum_out=sm[0:P, 0:1])
        rstd_from_ss(k, sm[0:P, 1:2], sm[0:P, 0:1], P, 128.0)
        STT(k.sl(ob_b, hh * 128, 128), ps_o[0:P, 0:128], sm[0:P, 1:2], k.sl(w(6), hh * 128, 128), ALU.mult, ALU.mult)
        tr_bf(k, dst_fn(h), k.sl(ob_b, hh * 128, 128), P)


def mixer_D(k, l, ti, tok0, n, has_s, last, srcs_tm, oTP, oTS, dense_tm, PB, DW):
    c, di, do = k.c, k.di, k.do
    ACT, CP, TS = k.ACT, k.CP, k.TS
    w_in = di["w_in"][l]
    nblk = n // 128
    k.make_rope(tok0, nblk)
    for hp in range(4):
        h0 = 2 * hp
        offs = [O_RQ + 256 * hp, O_RK + 256 * hp, O_RV + 256 * hp, O_RG + 256 * hp]
        for wi, o_ in enumerate(offs):
            def sink(gi, tt, P, col0, n_, ps, wi=wi):
                dst = PB[0:P, tt // 128, wi * 256:(wi + 1) * 256] if gi == 0 else k.SPB[0:P, wi * 256:(wi + 1) * 256]
                if wi == 1:
                    ACT(dst, ps[0:P, 0:256], AF.Copy, scale=128.0 ** -0.5)
                else:
                    CP(dst, ps[0:P, 0:256], eng=k.act)
            dense_tm(w_in, 16, o_, 256, srcs_tm, sink)
        for b in range(nblk):
            block_D(k, l, 128, h0,
                    PB[:, b, 0:256], PB[:, b, 256:512], PB[:, b, 512:768], PB[:, b, 768:1024],
                    k.cosT[:, b, :], k.sinT[:, b, :],
                    lambda hh: k.S_D[:, l, h0 + hh, :], k.grev,
                    lambda h, b=b: oTP[:, 24 + h, b * 128:(b + 1) * 128], DW)
        if has_s:
            for j in range(NS):
                sp0 = k.SP0p.get()
                c.dma(k.sp, sp0[0:1, :], k.SPB[j:j + 1, :])
                ss = k.SSp.get()
                c.dma(k.sp, ss[:, 0:2, :], di["state_ret"][l, j, h0:h0 + 2].rearrange("h k v -> k h v"))
                block_D(k, l, 1, h0,
                        sp0[0:1, 0:256], sp0[0:1, 256:512], sp0[0:1, 512:768], sp0[0:1, 768:1024],
                        k.cosS[0:1, 0, :], k.sinS[0:1, 0, :],
                        lambda hh, ss=ss: ss[:, hh, :], k.ones,
                        lambda h, j=j: oTS[:, 24 + h, j:j + 1], DW)
                c.dma(k.sp, do["ret_sample"][l, j, h0:h0 + 2].rearrange("h k v -> k h v"), ss[:, 0:2, :])
    if last:
        c.dma(k.sp, do["ret_prompt"][l].rearrange("h k v -> k h v"), k.S_D[:, l, :, :])

def block_C(k, l, P, g, xc_fn, zs_v, sdt_v, Sg, dst_fn, DW):
    c = k.c
    TT, TS, STT, ACT, CP, PS, TBp = k.TT, k.TS, k.STT, k.ACT, k.CP, k.PS, k.TBp
    h0 = 8 * g
    xs = V(DW.ap.tensor and DW.buf if False else DW.buf, DW.t[0:P, 0:2, :].rearrange("p a b -> p (a b)")) if False else None
    xs = DW.v(DW.t[0:P, 0:2, :].rearrange("p a b -> p (a b)"))
    y = DW.v(DW.t[0:P, 2:4, :].rearrange("p a b -> p (a b)"))
    sc_sb = DW[0:P, 5, 0:P]
    LBh = DW[0:P, 6, 0:128]
    tmp = DW[0:P, 7, 0:P]
    tmp2 = DW[:, 8, 0:P]
    for i in range(4):
        ps = PS.get()
        c.tr(ps[0:P, 0:128], xc_fn(i), k.ident[:, :])
        CP(k.sl(xs, i * 128, 128), ps[0:P, 0:128], eng=k.act)
    Tt = k.TTp.get()
    B_fm, C_fm = Tt[:, 0:P], Tt[:, 128:128 + P]
    CP(B_fm, xc_fn(4)); CP(C_fm, xc_fn(5), eng=k.act)
    Bt = k.BTp.get()
    B_tm = Bt[0:P, :]
    psb = k.PSB.get()
    c.tr(psb[0:P, 0:128], B_fm, k.identb[:, :])
    CP(B_tm, psb[0:P, 0:128], eng=k.act)
    sm = k.TSm.get()
    dt, logd, b_sb, erev, dtr = sm[0:P, 0:8], sm[0:P, 8:16], sm[0:P, 16:24], sm[0:P, 24:32], sm[0:P, 32:40]
    sm2 = k.TSm.get()
    dS = sm2[:, 0:8]
    TT(dt, sdt_v, k.DTB[l][0:P, h0:h0 + 8], ALU.add)
    ACT(dt, dt, AF.Exp)
    ACT(dt, dt, AF.Ln, bias=k.oneb[0:P, 0:1])
    TT(logd, dt, k.NEGA[l][0:P, h0:h0 + 8], ALU.mult)
    ps1 = PS.get()
    c.mm(ps1[0:P, 0:8], k.causal[0:P, 0:P], logd)
    CP(b_sb, ps1[0:P, 0:8])
    ps2 = PS.get()
    c.mm(ps2[0:P, 0:8], k.trirev[0:P, 0:P], logd)
    ACT(erev, ps2[0:P, 0:8], AF.Exp)
    ps3 = PS.get()
    c.mm(ps3[:, 0:8], k.ones[0:P, 0:128], logd)
    ACT(dS, ps3[:, 0:8], AF.Exp)
    TT(dtr, dt, erev, ALU.mult)

    def bc64(t):
        a = t.ap
        return V(t.buf, bass.AP(a.tensor, a.offset, [list(a.ap[0]), [1, 8], [0, 64]]))

    def r3(t):
        return V(t.buf, t.ap.rearrange("p (h d) -> p h d", h=8))
    vb_ = TBp.get(); vh_ = TBp.get(); yn_ = TBp.get(); Sb_ = TBp.get()
    v_b, vh_b, yn_b = vb_[0:P, :], vh_[0:P, :], yn_[0:P, :]
    TT(r3(v_b), r3(xs), bc64(dt), ALU.mult)
    TT(r3(vh_b), r3(xs), bc64(dtr), ALU.mult)
    Sb = Sb_.v(Sb_.t[:, :].rearrange("p (h d) -> p h d", h=8))
    CP(Sb, Sg, eng=k.act)
    ps_sc = PS.get()
    c.mm(ps_sc[0:P, 0:P], B_fm, C_fm)
    CP(sc_sb, ps_sc[0:P, 0:P])
    for hh in range(8):
        h = h0 + hh
        a = logd.ap
        CP(LBh, V(logd.buf, bass.AP(a.tensor, a.offset + hh, [list(a.ap[0]), [0, 128]])))
        ps_bt = PS.get()
        c.mm(ps_bt[:, 0:P], LBh, k.causal[0:P, 0:P])
        STT(tmp, ps_bt[0:P, 0:P], b_sb[:, hh:hh + 1], k.negm[0:P, 0:P], ALU.subtract, ALU.add)
        TS(tmp, tmp, 0.0, None, ALU.min)
        ACT(tmp, tmp, AF.Exp)
        Pm = k.PMp.get()
        TT(Pm[0:P, 0:P], tmp, sc_sb, ALU.mult)
        ACT(tmp2, ps_bt[:, 0:P], AF.Exp)
        qh = k.QHp.get()
        TT(qh[:, 0:P], xc_fn(5), tmp2, ALU.mult)
        ps_o = PS.get()
        c.mm(ps_o[0:P, 0:64], Pm[0:P, 0:P], k.sl(v_b, hh * 64, 64), start=True, stop=False)
        c.mm(ps_o[0:P, 0:64], qh[:, 0:P], Sb[:, hh, :], start=False, stop=True)
        STT(k.sl(y, hh * 64, 64), k.sl(xs, hh * 64, 64), k.SSD_D[l][0:P, h:h + 1], ps_o[0:P, 0:64], ALU.mult, ALU.add)
    ps_u = PS.get()
    c.mm(ps_u[:, 0:512], B_tm, vh_b)
    a = dS.ap
    TT(Sg, Sg, V(dS.buf, bass.AP(a.tensor, a.offset, [list(a.ap[0]), [1, 8], [0, 64]])), ALU.mult)
    TT(Sg, Sg, ps_u.v(ps_u.t[:, 0:512].rearrange("p (h d) -> p h d", h=8)), ALU.add)
    TT(y, y, zs_v, ALU.mult)
    sm3 = k.TSm.get()
    ACT(k.sl(xs, 0, 512), y, AF.Square, accum_out=sm3[0:P, 0:1])
    rstd_from_ss(k, sm3[0:P, 1:2], sm3[0:P, 0:1], P, 512.0)
    TS(yn_b, y, sm3[0:P, 1:2], None, ALU.mult)
    for i in range(4):
        psb = k.PSB.get()
        c.tr(psb[:, 0:P], k.sl(yn_b, i * 128, 128), k.identb[0:P, 0:P])
        TS(dst_fn(4 * g + i), psb[:, 0:P], k.pv("ssd_norm_w", l * 8 + 4 * g + i), None, ALU.mult)


def mixer_C(k, l, ti, tok0, n, has_s, last, srcs_fm, srcs_tm, oTP, oTS, dense_fm, dense_tm, conv_fm, PB, off_fmb):
    c, di, do = k.c, k.di, k.do
    ACT, CP, TS, PS = k.ACT, k.CP, k.TS, k.PS
    w_in = di["w_in"][l]
    nblk = n // 128
    off = off_fmb
    DW, off = k.carve("DWc", off, [128, 9, 256], F32)
    XC, off = k.carve("XC", off, [128, 6, 256], F32)
    XP2, off = k.carve("XP2", off, [128, 2, n + 3], F32)
    if has_s:
        HSC, XSC, XCS = k.C_HSC, k.C_XSC, k.C_XCS
        for i in range(3):
            load_tm2fm(k, di["state_ssd_conv"][l, :, i, :], 1536, lambda j, i=i: HSC[:, j, i, :])
    for g in range(2):
        chunks = [4 * g, 4 * g + 1, 4 * g + 2, 4 * g + 3, 8 + g, 10 + g]
        for slot, ch in enumerate(chunks):
            def sink(gi, col, m, ps, slot=slot, ch=ch):
                if gi == 0:
                    s2 = slot % 2
                    CP(XP2[:, s2, 0:3], k.HI_SSD[:, l, ch, :])
                    CP(XP2[:, s2, 3:n + 3], ps[:, 0:n], eng=k.act)
                    conv_fm(XC[:, slot, 0:n], [XP2[:, s2, i:i + n] for i in range(4)], "ssd_conv_w", "ssd_conv_b", l, 4, ch, 12)
                    CP(k.HI_SSD[:, l, ch, :], XP2[:, s2, n:n + 3])
                    ACT(XC[:, slot, 0:n], XC[:, slot, 0:n], AF.Silu)
                else:
                    CP(XSC[:, ch, :], ps[:, 0:NS], eng=k.act)
                    conv_fm(XCS[:, slot, :], [HSC[:, ch, 0, :], HSC[:, ch, 1, :], HSC[:, ch, 2, :], XSC[:, ch, :]], "ssd_conv_w", "ssd_conv_b", l, 4, ch, 12)
                    ACT(XCS[:, slot, :], XCS[:, slot, :], AF.Silu)
            dense_fm(w_in, 16, O_XBC + 128 * ch, 128, srcs_fm, sink, cw=128)

        def sink_z(gi, tt, P, col0, n_, ps):
            cc = col0 - (O_SZ + 512 * g)
            dst = PB[0:P, tt // 128, cc:cc + n_] if gi == 0 else k.SPB[0:P, cc:cc + n_]
            ACT(dst, ps[0:P, 0:n_], AF.Silu)
        dense_tm(w_in, 16, O_SZ + 512 * g, 512, srcs_tm, sink_z)

        def sink_dt(gi, tt, P, col0, n_, ps):
            dst = PB[0:P, tt // 128, 512:520] if gi == 0 else k.SPB[0:P, 512:520]
            CP(dst, ps[0:P, 0:8], eng=k.act)
        dense_tm(w_in, 16, O_DT + 8 * g, 8, srcs_tm, sink_dt)
        for b in range(nblk):
            block_C(k, l, 128, g, lambda i, b=b: XC[:, i, b * 128:(b + 1) * 128], PB[:, b, 0:512], PB[:, b, 512:520],
                    k.S_C[:, l, 8 * g:8 * g + 8, :], lambda j, b=b: oTP[:, 16 + j, b * 128:(b + 1) * 128], DW)
        if has_s:
            for j in range(NS):
                sp0 = k.SP0p.get()
                c.dma(k.sp, sp0[0:1, 0:520], k.SPB[j:j + 1, 0:520])
                ss = k.SSp.get()
                ssv = ss.v(ss.t[:, :, :].rearrange("p a b -> p (a b)")[:, 0:512].rearrange("p (h d) -> p h d", h=8))
                c.dma(k.sp, ssv, di["state_ssd"][l, j, 8 * g:8 * g + 8].rearrange("h n v -> n h v"))
                block_C(k, l, 1, g, lambda i, j=j: XCS[:, i, j:j + 1], sp0[0:1, 0:512], sp0[0:1, 512:520],
                        ssv, lambda jj, j=j: oTS[:, 16 + jj, j:j + 1], DW)
                c.dma(k.sp, do["ssd_sample"][l, j, 8 * g:8 * g + 8].rearrange("h n v -> n h v"), ssv)
    if has_s:
        store_fm2tm(k, lambda j: XSC[:, j, :], 12, do["ssd_conv_sample"][l, :, 2, :])
        for i in range(2):
            c.dma(k.sp, do["ssd_conv_sample"][l, :, i, :], di["state_ssd_conv"][l, :, i + 1, :])
    if last:
        c.dma(k.sp, do["ssd_prompt"][l].rearrange("h n v -> n h v"), k.S_C[:, l, :, :])
        for i in range(3):
            store_fm2tm(k, lambda jj, i=i: k.HI_SSD[:, l, :, i], 1, do["ssd_conv_prompt"][l, i].rearrange("(j p) -> j p", p=128), P=12)

def block_B(k, l, P, h0, hq_v, hf_v, hi_v, sgh_fn, S_fn, dst_fn, DW):
    c = k.c
    TT, TS, STT, ACT, CP, PS, TBp = k.TT, k.TS, k.STT, k.ACT, k.CP, k.PS, k.TBp
    nch = max(1, P // 32)
    cs = min(32, P)

    def w(i):
        return DW[0:P, i, :]
    cols = slice(h0 * 128, h0 * 128 + 256)
    ACT(w(0), hq_v, AF.Silu)
    ACT(w(1), hf_v, AF.Sigmoid)
    ACT(w(2), hf_v, AF.Sigmoid, scale=-1.0)
    if l == 1:
        TT(w(1), w(1), k.OML1[0:P, cols], ALU.mult)
        TT(w(1), w(1), k.LB1[0:P, cols], ALU.add)
        TT(w(2), w(2), k.OML1[0:P, cols], ALU.mult)
    ACT(w(1), w(1), AF.Ln)
    ps_b = PS.get()
    c.mm(ps_b[0:P, 0:256], k.tri32[0:P, 0:P], w(1))
    ps_r = PS.get()
    c.mm(ps_r[0:P, 0:256], k.rev32[0:P, 0:P], w(1))
    A = TBp.get(); B = TBp.get()
    qt_b, kt_b = A[0:P, 0:256], A[0:P, 256:512]
    v_b, khc = B[0:P, 0:256], B[0:P, 256:384]
    ACT(w(4), ps_b[0:P, 0:256], AF.Exp)
    TT(qt_b, w(0), w(4), ALU.mult)
    ACT(w(4), ps_b[0:P, 0:256], AF.Exp, scale=-1.0)
    TT(kt_b, w(2), w(4), ALU.mult)
    ACT(w(4), ps_r[0:P, 0:256], AF.Exp)
    TT(w(3), w(2), w(4), ALU.mult)
    CP(v_b, hi_v, eng=k.act)
    for hh in range(2):
        h = h0 + hh
        hs = slice(hh * 128, (hh + 1) * 128)
        S = S_fn(hh)
        ps_d = PS.get()
        c.mm(ps_d[:, 0:nch], w(1)[:, hs], k.ind[0:P, 0:nch])
        sm = k.TSm.get()
        ACT(sm[:, 0:nch], ps_d[:, 0:nch], AF.Exp)
        Tt = k.TTp.get()
        qT, kT = Tt[:, 0:P], Tt[:, 128:128 + P]
        tr_bf(k, qT, qt_b[:, hs], P)
        tr_bf(k, kT, kt_b[:, hs], P)
        Sb = k.SB4p.get()
        for cc in range(nch):
            CP(Sb[:, cc * 128:(cc + 1) * 128], S, eng=k.act)
            TS(khc, w(3)[:, hs], k.ind[0:P, cc:cc + 1], None, ALU.mult)
            ps_u = PS.get()
            c.mm(ps_u[:, 0:128], khc, v_b[:, hs])
            STT(S, S, sm[:, cc:cc + 1], ps_u[:, 0:128], ALU.mult, ALU.add)
        ps_sc = PS.get()
        c.mm(ps_sc[0:P, 0:P], kT, qT)
        Pm = k.PMp.get()
        TT(Pm[0:P, 0:P], ps_sc[0:P, 0:P], k.tri32[0:P, 0:P], ALU.mult)
        ps_oi = PS.get()
        c.mm(ps_oi[:, 0:P], v_b[:, hs], Pm[0:P, 0:P])
        ps_oc = PS.get()
        for cc in range(nch):
            c.mm(ps_oc[:, cc * 32:cc * 32 + cs], Sb[:, cc * 128:(cc + 1) * 128], qT[:, cc * 32:cc * 32 + cs])
        o = DW[:, 5, 0:P]
        sq = DW[:, 6, 0:P]
        rs = DW[:, 7, 0:P]
        CP(o, ps_oi[:, 0:P], eng=k.act)
        TT(o, o, ps_oc[:, 0:P], ALU.add)
        ACT(sq, o, AF.Square)
        ps_ss = PS.get()
        c.mm(ps_ss[:, 0:P], k.ones[:, :], sq)
        ACT(rs, ps_ss[:, 0:P], AF.Ln, scale=1.0 / 128.0, bias=k.epsb[:, 0:1])
        ACT(rs, rs, AF.Exp, scale=-0.5)
        STT(sq, o, k.pv("hg_norm_w", l), rs, ALU.mult, ALU.mult)
        TT(dst_fn(h), sq, sgh_fn(hh), ALU.mult)


def mixer_B(k, l, ti, tok0, n, has_s, last, srcs_fm, srcs_tm, oTP, oTS, dense_fm, dense_tm, PB, DW):
    c, di, do = k.c, k.di, k.do
    ACT, CP = k.ACT, k.CP
    w_in = di["w_in"][l]
    nblk = n // 128
    for hp in range(4):
        h0 = 2 * hp
        for wi, o_ in enumerate([O_HQ + 256 * hp, O_HF + 256 * hp, O_HI + 256 * hp]):
            def sink(gi, tt, P, col0, n_, ps, wi=wi):
                dst = PB[0:P, tt // 128, wi * 256:(wi + 1) * 256] if gi == 0 else k.SPB[0:P, wi * 256:(wi + 1) * 256]
                CP(dst, ps[0:P, 0:256], eng=k.act)
            dense_tm(w_in, 16, o_, 256, srcs_tm, sink)

        def sink_g(gi, col, m, ps):
            hh = (col - (O_HG + 256 * hp)) // 128
            if gi == 0:
                ACT(PB[:, 2, hh * 256:hh * 256 + n], ps[:, 0:n], AF.Silu)
            else:
                ACT(PB[:, 3, hh * NS:(hh + 1) * NS], ps[:, 0:NS], AF.Silu)
        dense_fm(w_in, 16, O_HG + 256 * hp, 256, srcs_fm, sink_g)
        for b in range(nblk):
            block_B(k, l, 128, h0, PB[:, b, 0:256], PB[:, b, 256:512], PB[:, b, 512:768],
                    lambda hh, b=b: PB[:, 2, hh * 256 + b * 128:hh * 256 + (b + 1) * 128],
                    lambda hh: k.S_B[:, l, h0 + hh, :],
                    lambda h, b=b: oTP[:, 8 + h, b * 128:(b + 1) * 128], DW)
        if has_s:
            for j in range(NS):
                sp0 = k.SP0p.get()
                c.dma(k.sp, sp0[0:1, 0:768], k.SPB[j:j + 1, 0:768])
                ss = k.SSp.get()
                c.dma(k.sp, ss[:, 0:2, :], di["state_hgrn"][l, j, h0:h0 + 2].rearrange("h k v -> k h v"))
                block_B(k, l, 1, h0, sp0[0:1, 0:256], sp0[0:1, 256:512], sp0[0:1, 512:768],
                        lambda hh, j=j: PB[:, 3, hh * NS + j:hh * NS + j + 1],
                        lambda hh, ss=ss: ss[:, hh, :],
                        lambda h, j=j: oTS[:, 8 + h, j:j + 1], DW)
                c.dma(k.sp, do["hgrn_sample"][l, j, h0:h0 + 2].rearrange("h k v -> k h v"), ss[:, 0:2, :])
    if last:
        c.dma(k.sp, do["hgrn_prompt"][l].rearrange("h k v -> k h v"), k.S_B[:, l, :, :])


def _shard_inputs(inputs, ci):
    b = ci % 4
    s0 = ci * NS
    m = {}
    for name, a in inputs.items():
        a = np.asarray(a)
        if name == "x_prompt":
            m[name] = np.ascontiguousarray(a[b])
        elif name == "x_sample":
            m[name] = np.ascontiguousarray(a[s0:s0 + NS, 0, :])
        elif name.startswith("state_"):
            m[name] = np.ascontiguousarray(a[:, s0:s0 + NS])
        else:
            m[name] = np.ascontiguousarray(a)
    return m


_NC_CACHE = {}


def kernel(**inputs):
    T = int(np.asarray(inputs["x_prompt"]).shape[1])
    B = int(np.asarray(inputs["x_prompt"]).shape[0])
    if T not in _NC_CACHE:
        _NC_CACHE[T] = build(T)
    nc = _NC_CACHE[T]
    in_maps = [_shard_inputs(inputs, ci) for ci in range(8)]
    res = run_bass_kernel_spmd(nc, in_maps, core_ids=list(range(8)))
    R = res.results
    f = np.float32
    y_prompt = np.stack([R[b]["y_prompt"] for b in range(B)], 0).astype(f)
    y_sample = np.concatenate([R[ci]["y_sample"] for ci in range(8)], 0)[:, None, :].astype(f)
    outs = [y_prompt, y_sample]
    for nm in ["lru_h", "lru_conv", "hgrn", "ssd", "ssd_conv", "ret", "ffn_conv"]:
        outs.append(np.stack([R[b][nm + "_prompt"] for b in range(B)], 1).astype(f))
        outs.append(np.concatenate([R[ci][nm + "_sample"] for ci in range(8)], 1).astype(f))
    return tuple(outs)
```

```python
import math
from contextlib import ExitStack
from concourse.bass_utils import run_bass_kernel_spmd
import numpy as np
import concourse.bass as bass
import concourse.mybir as mybir

F32 = mybir.dt.float32
BF16 = mybir.dt.bfloat16
I32 = mybir.dt.int32
AF = mybir.ActivationFunctionType
ALU = mybir.AluOpType
AX = mybir.AxisListType


class V:
    __slots__ = ("buf", "ap")

    def __init__(self, buf, ap):
        self.buf = buf
        self.ap = ap

    def __getitem__(self, idx):
        return V(self.buf, self.ap[idx])


class Buf:
    def __init__(self, ctx, tensor, name):
        self.ctx = ctx
        self.t = tensor
        self.name = name
        self.w = None
        self.r = []
        self.dsem = None
        self.dcnt = 0

    def __getitem__(self, idx):
        return V(self, self.t[idx])

    def v(self, ap):
        return V(self, ap)


class Eng:
    def __init__(self, ctx, e, name):
        self.ctx = ctx
        self.e = e
        self.name = name
        self.sem = ctx.new_sem("c_" + name)
        self.cnt = 0
        self.waited = {}
        self.pend_r = []
        self.pend_w = []
        self.old = []

    def need(self, tok):
        if tok is None:
            return
        sem, val = tok
        k = id(sem)
        if self.waited.get(k, 0) >= val:
            return
        self.e.wait_ge(sem, val)
        self.waited[k] = val

    def deps(self, reads, writes):
        for b in reads:
            self.need(b.w)
        for b in writes:
            self.need(b.w)
            for t in b.r:
                self.need(t)

    def done(self, inst, reads, writes, inc=True):
        self.pend_r += reads
        self.pend_w += writes
        if inc:
            if self.cnt >= 30000:
                self.old.append((self.sem, self.cnt))
                self.sem = self.ctx.new_sem("c_" + self.name + str(self.ctx.nsem))
                self.cnt = 0
            inst.then_inc(self.sem, 1)
            self.cnt += 1
            tok = (self.sem, self.cnt)
            for b in self.pend_w:
                b.w = tok
                b.r = []
            for b in self.pend_r:
                if b.w is not tok:
                    b.r.append(tok)
                    if len(b.r) > 6:
                        b.r = b.r[-6:] if False else b.r
            self.pend_r = []
            self.pend_w = []


def _bufs(views):
    out = []
    for v in views:
        if isinstance(v, V) and v.buf is not None and v.buf not in out:
            out.append(v.buf)
    return out


def _ap(x):
    return x.ap if isinstance(x, V) else x


class Ctx:
    def __init__(self, nc, stack):
        self.nc = nc
        self.stack = stack
        self.nsem = 0
        self.pe = Eng(self, nc.tensor, "pe")
        self.dve = Eng(self, nc.vector, "dve")
        self.act = Eng(self, nc.scalar, "act")
        self.pool = Eng(self, nc.gpsimd, "pool")
        self.sp = Eng(self, nc.sync, "sp")
        self.dma_bufs = []
        self.drambuf = Buf(self, None, "dram")
        self.uid = 0

    def new_sem(self, name):
        self.nsem += 1
        return self.stack.enter_context(self.nc.semaphore(name))

    def sbuf(self, name, shape, dt=F32):
        t = self.stack.enter_context(self.nc.sbuf_tensor(name, list(shape), dt))
        return Buf(self, t, name)

    def psum(self, name, shape, dt=F32):
        t = self.stack.enter_context(self.nc.psum_tensor(name, list(shape), dt))
        return Buf(self, t, name)

    def op(self, eng, fn, out, ins, *args, **kw):
        extra_r = kw.pop("_reads", [])
        rb = _bufs(list(ins) + list(extra_r) + [v for v in kw.values() if isinstance(v, V)])
        wb = _bufs([out] + ([kw["accum_out"]] if "accum_out" in kw else []))
        eng.deps(rb, wb)
        kw2 = {k: _ap(v) for k, v in kw.items()}
        inst = getattr(eng.e, fn)(_ap(out), *[_ap(i) for i in ins], *args, **kw2)
        eng.done(inst, rb, wb)
        return inst

    def mm(self, out, lhsT, rhs, start=True, stop=True, inc=None, **kw):
        pe = self.pe
        rb = _bufs([lhsT, rhs])
        wb = _bufs([out])
        pe.deps(rb, wb if start else [])
        inst = pe.e.matmul(_ap(out), _ap(lhsT), _ap(rhs), start=start, stop=stop, **kw)
        pe.done(inst, rb, wb, inc=(stop if inc is None else inc))
        return inst

    def tr(self, out, in_, ident):
        pe = self.pe
        rb = _bufs([in_, ident])
        wb = _bufs([out])
        pe.deps(rb, wb)
        inst = pe.e.transpose(_ap(out), _ap(in_), _ap(ident))
        pe.done(inst, rb, wb, inc=True)
        return inst

    def dma(self, q, out, in_, **kw):
        rb = _bufs([in_])
        wb = _bufs([out])
        q.deps(rb, wb)
        b = (wb + rb)[0] if (wb + rb) else self.drambuf
        if b.dsem is None:
            b.dsem = self.new_sem("d_" + b.name)
            self.dma_bufs.append(b)
        inst = q.e.dma_start(out=_ap(out), in_=_ap(in_), **kw)
        inst.then_inc(b.dsem, 16)
        b.dcnt += 16
        tok = (b.dsem, b.dcnt)
        for x in wb:
            x.w = tok
            x.r = []
        for x in rb:
            x.r.append(tok)
        return inst

    def finish(self):
        for b in self.dma_bufs:
            self.sp.need((b.dsem, b.dcnt))


class Pool:
    def __init__(self, ctx, name, n, shape, dt=F32, psum=False):
        self.bufs = [(ctx.psum if psum else ctx.sbuf)(f"{name}{i}", shape, dt) for i in range(n)]
        self.i = 0

    def get(self):
        b = self.bufs[self.i % len(self.bufs)]
        self.i += 1
        return b


def bc(v, shape_ap):
    a = _ap(v)
    ap = bass.AP(a.tensor, a.offset, [list(a.ap[0])] + [list(x) for x in shape_ap])
    return V(v.buf, ap) if isinstance(v, V) else ap

D = 2048
KC = 16
DEPTH = 2
NS = 16
LRU_W = 1024
DFF = 5632
FC = 44
N_IN = 12816
EPS = 1e-6
O_XA, O_YA = 0, 1024
O_HQ, O_HF, O_HI, O_HG = 2048, 3072, 4096, 5120
O_SZ, O_XBC, O_DT = 6144, 7168, 8704
O_RQ, O_RK, O_RV, O_RG = 8720, 9744, 10768, 11792
LOG_GAMMA = [math.log1p(-2.0 ** (-5.0 - h)) for h in range(8)]


class K:
    pass


_PLANS = {}


def build(T):
    if T not in _PLANS:
        rec = []
        _build(T, None, rec)
        seen = {}
        off = [0, 0]
        for key in rec:
            if key not in seen:
                seen[key] = off[key[1]]
                off[key[1]] += 128 * key[4] * key[6]
        _PLANS[T] = (seen, off)
    return _build(T, _PLANS[T], None)


def _build(T, plan, rec):
    NT = T // 512
    nc = bass.Bass("TRN2", target_bir_lowering=False)
    di = {}
    do = {}

    def din(name, shape):
        di[name] = nc.dram_tensor(name, list(shape), F32, kind="ExternalInput").ap()

    def dout(name, shape):
        do[name] = nc.dram_tensor(name, list(shape), F32, kind="ExternalOutput").ap()

    din("x_prompt", [T, D]); din("x_sample", [NS, D])
    din("state_lru_h", [2, NS, 1024]); din("state_lru_conv", [2, NS, 3, 1024])
    din("state_hgrn", [2, NS, 8, 128, 128]); din("state_ssd", [2, NS, 16, 128, 64])
    din("state_ssd_conv", [2, NS, 3, 1536]); din("state_ret", [2, NS, 8, 128, 128])
    din("state_ffn_conv", [2, NS, 2, DFF])
    din("g_mix", [2, D]); din("g_ffn", [2, D]); din("w_in", [2, D, N_IN])
    din("lru_conv_w", [2, 4, 1024]); din("lru_conv_b", [2, 1024]); din("lru_wa", [2, 8, 128, 128])
    din("lru_ba", [2, 8, 128]); din("lru_wx", [2, 8, 128, 128]); din("lru_bx", [2, 8, 128])
    din("lru_lambda", [2, 1024]); din("hg_lb_logits", [2, 1024]); din("hg_norm_w", [2, 128])
    din("ssd_conv_w", [2, 4, 1536]); din("ssd_conv_b", [2, 1536]); din("ssd_dt_bias", [2, 16])
    din("ssd_a_log", [2, 16]); din("ssd_d", [2, 16]); din("ssd_norm_w", [2, 1024])
    din("w_branch", [2, 4, 1024, D]); din("w_gate", [2, D, 4, D]); din("w_out", [2, D, D])
    din("ffn_w_up", [2, D, DFF]); din("ffn_w_val", [2, D, DFF]); din("ffn_conv_w", [2, 3, DFF])
    din("ffn_conv_b", [2, DFF]); din("ffn_w_down", [2, DFF, D]); din("g_final", [D])
    dout("y_prompt", [T, D]); dout("y_sample", [NS, D])
    dout("lru_h_prompt", [2, 1024]); dout("lru_h_sample", [2, NS, 1024])
    dout("lru_conv_prompt", [2, 3, 1024]); dout("lru_conv_sample", [2, NS, 3, 1024])
    dout("hgrn_prompt", [2, 8, 128, 128]); dout("hgrn_sample", [2, NS, 8, 128, 128])
    dout("ssd_prompt", [2, 16, 128, 64]); dout("ssd_sample", [2, NS, 16, 128, 64])
    dout("ssd_conv_prompt", [2, 3, 1536]); dout("ssd_conv_sample", [2, NS, 3, 1536])
    dout("ret_prompt", [2, 8, 128, 128]); dout("ret_sample", [2, NS, 8, 128, 128])
    dout("ffn_conv_prompt", [2, 2, DFF]); dout("ffn_conv_sample", [2, NS, 2, DFF])

    bfw = [nc.dram_tensor(f"bfw{l}", [plan[1][l] if plan else 128], BF16, kind="Internal").ap() for l in range(2)]
    with ExitStack() as st:
        c = Ctx(nc, st)
        c.plan = plan
        c.rec = rec
        c.bfw = bfw
        _program(c, nc, di, do, T, NT)
        c.finish()
    return nc

def _program(c, nc, di, do, T, NT):
    dve, act, pool, pe, sp = c.dve, c.act, c.pool, c.pe, c.sp
    PI = math.pi
    NB = T // 128

    def OP(eng, fn, out, ins, *a, **k):
        return c.op(eng, fn, out, ins, *a, **k)

    def TT(out, a, b, op, eng=None):
        return c.op(eng or dve, "tensor_tensor", out, [a, b], op)

    def TS(out, a, s1, s2, op0, op1=ALU.bypass, eng=None):
        return c.op(eng or dve, "tensor_scalar", out, [a, s1, s2], op0, op1)

    def STT(out, a, s, b, op0, op1):
        return c.op(dve, "scalar_tensor_tensor", out, [a, s, b], op0, op1)

    def ACT(out, a, f, **k):
        return c.op(act, "activation", out, [a], f, **k)

    def CP(out, a, eng=None):
        e = eng or dve
        if e is act:
            return c.op(act, "activation", out, [a], AF.Copy)
        return c.op(e, "tensor_copy", out, [a])

    def MS(buf_view, val, eng=None):
        return c.op(eng or dve, "memset", buf_view, [], val)

    def ASEL(out, in_, pattern, cmp, base, cm):
        return c.op(pool, "affine_select", out, [in_], pattern=pattern, compare_op=cmp, fill=0.0,
                    base=base, channel_multiplier=cm)

    def wsrc(nm, l, sub):
        a_ = di[nm][l]
        if nm == "w_branch":
            a_ = a_[sub]
        elif nm == "w_gate":
            a_ = a_[:, sub, :]
        return a_
    CV = {}
    if c.plan is not None:
        for key, off_ in c.plan[0].items():
            nm, l_, sub, k0, kc, col0, ncols = key
            if (nm, l_) not in CV:
                CV[(nm, l_)] = Buf(c, None, f"cv_{nm}{l_}")
            src = wsrc(nm, l_, sub)[k0 * 128:(k0 + kc) * 128, col0:col0 + ncols].rearrange("(k p) n -> p k n", p=128)
            dst = c.bfw[l_][off_:off_ + 128 * kc * ncols].rearrange("(p k n) -> p k n", p=128, k=kc)
            c.dma(pool, V(CV[(nm, l_)], dst), src)
    PS = Pool(c, "ps", 6, [128, 512], F32, psum=True)
    PSB = Pool(c, "psb", 2, [128, 1024], BF16, psum=True)
    WP = Pool(c, "wbuf", 3, [128, 4096], BF16)
    TA = Pool(c, "ta", 5, [128, 512], F32)
    TBp = Pool(c, "tb", 4, [128, 512], BF16)
    TSm = Pool(c, "tsm", 8, [128, 64], F32)

    ones = c.sbuf("ones", [128, 128], F32); MS(ones[:], 1.0)
    onesb = c.sbuf("onesb", [128, 128], BF16); MS(onesb[:], 1.0)
    ident = c.sbuf("ident", [128, 128], F32)
    ASEL(ident[:], ones[:], [[-1, 128]], ALU.is_equal, 0, 1)
    identb = c.sbuf("identb", [128, 128], BF16); CP(identb[:], ident[:])
    causal = c.sbuf("causal", [128, 128], F32)
    ASEL(causal[:], ones[:], [[1, 128]], ALU.is_ge, 0, -1)
    trirev = c.sbuf("trirev", [128, 128], F32)
    ASEL(trirev[:], ones[:], [[-1, 128]], ALU.is_gt, 0, 1)
    same = c.sbuf("same", [128, 4, 32], F32)
    tmpc = c.sbuf("tmpc", [128, 4, 32], F32)
    ASEL(tmpc[:], bc(ones[:, 0:1], [[0, 4], [0, 32]]), [[-32, 4], [0, 32]], ALU.is_ge, 0, 1)
    ASEL(same[:], tmpc[:], [[32, 4], [0, 32]], ALU.is_ge, 31, -1)
    samef = same.v(same.t[:].rearrange("p c j -> p (c j)"))
    tri32 = c.sbuf("tri32", [128, 128], F32); TT(tri32[:], causal[:], samef, ALU.mult)
    rev32 = c.sbuf("rev32", [128, 128], F32); TT(rev32[:], trirev[:], samef, ALU.mult)
    mbd = tri32
    ind = c.sbuf("ind", [128, 4], F32); CP(ind[:], same[:, :, 0])
    negm = c.sbuf("negm", [128, 128], F32)
    TS(negm[:], causal[:], 30000.0, -30000.0, ALU.mult, ALU.add)
    dti = c.sbuf("dti", [128, 128], I32)
    c.op(pool, "iota", dti[:], [], pattern=[[1, 128]], base=0, channel_multiplier=-1)
    dtf = c.sbuf("dtf", [128, 128], F32); CP(dtf[:], dti[:])
    GM = c.sbuf("GM", [128, 8, 128], F32)
    pidx_i = c.sbuf("pidx_i", [128, 1], I32)
    c.op(pool, "iota", pidx_i[:], [], pattern=[[0, 1]], base=0, channel_multiplier=1)
    pidx = c.sbuf("pidx", [128, 1], F32); CP(pidx[:], pidx_i[:])
    pp1 = c.sbuf("pp1", [128, 1], F32); TS(pp1[:], pidx[:], 1.0, None, ALU.add)
    prv = c.sbuf("prv", [128, 1], F32); TS(prv[:], pidx[:], -1.0, 127.0, ALU.mult, ALU.add)
    gpow = c.sbuf("gpow", [128, 8], F32)
    grev = c.sbuf("grev", [128, 8], F32)
    for h in range(8):
        ACT(GM[:, h, :], dtf[:], AF.Exp, scale=LOG_GAMMA[h])
        TT(GM[:, h, :], GM[:, h, :], causal[:], ALU.mult)
        ACT(gpow[:, h:h + 1], pp1[:], AF.Exp, scale=LOG_GAMMA[h])
        ACT(grev[:, h:h + 1], prv[:], AF.Exp, scale=LOG_GAMMA[h])
    fi = c.sbuf("fi", [128, 64], I32)
    c.op(pool, "iota", fi[:], [], pattern=[[1, 64]], base=0, channel_multiplier=0)
    FR = c.sbuf("FR", [128, 64], F32); CP(FR[:], fi[:])
    ACT(FR[:], FR[:], AF.Exp, scale=-math.log(10000.0) / 64.0)
    RB = 2
    cosT = c.sbuf("cosT", [128, RB, 64], F32)
    sinT = c.sbuf("sinT", [128, RB, 64], F32)
    angb = c.sbuf("angb", [128, RB, 64], F32)
    ang2 = c.sbuf("ang2", [128, RB, 64], F32)
    angi = c.sbuf("angi", [128, RB, 64], I32)
    angm = c.sbuf("angm", [128, RB, 64], F32)
    posf = c.sbuf("posf", [128, RB], F32)

    def sin_of(out, src, P, nb):
        a2 = ang2[0:P, 0:nb, :]; ai = angi[0:P, 0:nb, :]; am = angm[0:P, 0:nb, :]
        TS(a2, src, 1.0 / (2 * PI), None, ALU.mult)
        CP(ai, a2)
        CP(a2, ai)
        STT(a2, a2, -2 * PI, src, ALU.mult, ALU.add)
        TS(am, a2, PI, None, ALU.is_gt)
        STT(a2, am, -2 * PI, a2, ALU.mult, ALU.add)
        TS(am, a2, -PI, None, ALU.is_lt)
        STT(a2, am, 2 * PI, a2, ALU.mult, ALU.add)
        TS(a2, a2, PI, -PI, ALU.min, ALU.max)
        ACT(out, a2, AF.Sin)

    def make_rope(tok0, nb):
        for b_ in range(nb):
            TS(posf[:, b_:b_ + 1], pidx[:, 0:1], float(tok0 + 128 * b_), None, ALU.add)
        TT(angb[:, 0:nb, :], bc(posf[:, 0:1], [[1, nb], [0, 64]]), bc(FR[:, 0:1], [[0, nb], [1, 64]]), ALU.mult)
        sin_of(sinT[:, 0:nb, :], angb[:, 0:nb, :], 128, nb)
        TS(angb[:, 0:nb, :], angb[:, 0:nb, :], PI / 2, None, ALU.add)
        sin_of(cosT[:, 0:nb, :], angb[:, 0:nb, :], 128, nb)

    cosS = c.sbuf("cosS", [1, 1, 64], F32)
    sinS = c.sbuf("sinS", [1, 1, 64], F32)
    TS(angb[0:1, 0, :], FR[0:1, :], 16384.0, None, ALU.mult)
    sin_of(sinS[:, :, :], angb[0:1, 0:1, :], 1, 1)
    TS(angb[0:1, 0:1, :], angb[0:1, 0:1, :], PI / 2, None, ALU.add)
    sin_of(cosS[:, :, :], angb[0:1, 0:1, :], 1, 1)
    TTp = Pool(c, "ttp", 2, [128, 384], BF16)
    PMp = Pool(c, "pmp", 2, [128, 128], BF16)
    SBFp = Pool(c, "sbfp", 2, [128, 128], BF16)
    SP0p = Pool(c, "sp0p", 2, [1, 1024], F32)
    SSp = Pool(c, "ssp", 2, [128, 4, 128], F32)
    BTp = Pool(c, "btp", 2, [128, 128], BF16)
    SB4p = Pool(c, "sb4p", 2, [128, 512], BF16)
    QHp = Pool(c, "qhp", 2, [128, 128], BF16)
    C_HSC = c.sbuf("C_HSC", [128, 12, 3, NS], F32)
    C_XSC = c.sbuf("C_XSC", [128, 12, NS], F32)
    C_XCS = c.sbuf("C_XCS", [128, 6, NS], F32)
    SPB = c.sbuf("SPB", [NS, 1024], F32)

    plist = [("g_mix", di["g_mix"].rearrange("l (c p) -> (l c) p", p=128)),
             ("g_ffn", di["g_ffn"].rearrange("l (c p) -> (l c) p", p=128)),
             ("g_final", di["g_final"].rearrange("(c p) -> c p", p=128)),
             ("lru_conv_w", di["lru_conv_w"].rearrange("l i (j p) -> (l i j) p", p=128)),
             ("lru_conv_b", di["lru_conv_b"].rearrange("l (j p) -> (l j) p", p=128)),
             ("lru_ba", di["lru_ba"].rearrange("l j p -> (l j) p")),
             ("lru_bx", di["lru_bx"].rearrange("l j p -> (l j) p")),
             ("lru_lambda", di["lru_lambda"].rearrange("l (j p) -> (l j) p", p=128)),
             ("hg_norm_w", di["hg_norm_w"]),
             ("ssd_conv_w", di["ssd_conv_w"].rearrange("l i (j p) -> (l i j) p", p=128)),
             ("ssd_conv_b", di["ssd_conv_b"].rearrange("l (j p) -> (l j) p", p=128)),
             ("ffn_conv_w", di["ffn_conv_w"].rearrange("l i (j p) -> (l i j) p", p=128)),
             ("ffn_conv_b", di["ffn_conv_b"].rearrange("l (j p) -> (l j) p", p=128)),
             ("ssd_norm_w", di["ssd_norm_w"].rearrange("l (j p) -> (l j) p", p=128))]
    tot = sum(a.shape[0] for _, a in plist)
    ntile = (tot + 127) // 128
    PVT = c.sbuf("PVT", [128, ntile * 128], F32)
    prow = c.sbuf("prow", [128, 128], F32)
    poff = {}
    g = 0
    segs = []
    for name, a in plist:
        poff[name] = g
        R = a.shape[0]
        r = 0
        while r < R:
            ti, ro = divmod(g + r, 128)
            m = min(R - r, 128 - ro)
            segs.append((ti, ro, a[r:r + m, :], m))
            r += m
        g += R
    for ti in range(ntile):
        MS(prow[:], 0.0)
        for (t2, ro, ap_, m) in segs:
            if t2 == ti:
                c.dma(sp, prow[ro:ro + m, :], ap_)
        pst = PS.get()
        c.tr(pst[:, 0:128], prow[:], ident[:])
        CP(PVT[:, ti * 128:(ti + 1) * 128], pst[:, 0:128])

    def pv(name, idx, n=1):
        o = poff[name] + idx
        return PVT[:, o:o + n]

    def bload(name, src2d_row, ncol):
        b = c.sbuf(name, [128, ncol], F32)
        a = src2d_row
        c.dma(sp, b[:], bass.AP(a.tensor, a.offset, [[0, 128], [1, ncol]]))
        return b

    LB1 = bload("lb1", di["hg_lb_logits"][1, :], 1024)
    OML1 = bload("oml1", di["hg_lb_logits"][0, :], 1024)
    TT(OML1[:], LB1[:], OML1[:], ALU.subtract)
    ACT(LB1[:], OML1[:], AF.Sigmoid)
    ACT(OML1[:], OML1[:], AF.Sigmoid, scale=-1.0)
    DTB = [bload(f"dtb{l}", di["ssd_dt_bias"][l, :], 16) for l in range(2)]
    NEGA = [bload(f"nega{l}", di["ssd_a_log"][l, :], 16) for l in range(2)]
    SSD_D = [bload(f"ssdd{l}", di["ssd_d"][l, :], 16) for l in range(2)]
    for l in range(2):
        ACT(NEGA[l][:], NEGA[l][:], AF.Exp)
        TS(NEGA[l][:], NEGA[l][:], -1.0, None, ALU.mult)
    LCC = c.sbuf("lcc", [128, 16], F32)
    LCC2 = c.sbuf("lcc2", [128, 16], F32)
    ACT(LCC[:], pv("lru_lambda", 0, 16), AF.Exp, scale=-1.0)
    ACT(LCC[:], LCC[:], AF.Ln, bias=1.0)
    TS(LCC2[:], LCC[:], -16.0, None, ALU.mult)
    TS(LCC[:], LCC[:], -8.0, None, ALU.mult)
    LWA = c.sbuf("lwa", [128, 2, 8, 128], BF16)
    LWX = c.sbuf("lwx", [128, 2, 8, 128], BF16)
    c.dma(pool, LWA[:], di["lru_wa"].rearrange("l n c d -> c l n d"))
    c.dma(pool, LWX[:], di["lru_wx"].rearrange("l n c d -> c l n d"))

    def zbuf(name, shape, dt=F32):
        b = c.sbuf(name, shape, dt)
        MS(b[:], 0.0)
        return b
    H_LRU = zbuf("H_LRU", [128, 2, 8])
    HI_LRU = zbuf("HI_LRU", [128, 2, 8, 3])
    HI_SSD = zbuf("HI_SSD", [128, 2, 12, 3])
    HI_FFN = zbuf("HI_FFN", [128, 2, 44, 2])
    S_B = zbuf("S_B", [128, 2, 8, 128])
    S_C = zbuf("S_C", [128, 2, 16, 64])
    S_D = zbuf("S_D", [128, 2, 8, 128])

    TILE = 256 if T > 256 else T
    xP = c.sbuf("xP", [128, 16, TILE], F32)
    xS = c.sbuf("xS", [128, 16, NS], F32)
    hP = c.sbuf("hP", [128, 16, TILE], BF16)
    hS = c.sbuf("hS", [128, 16, NS], BF16)
    NTK = TILE + NS
    UBN = max(32 * NTK + 2 * 4 * 1024 + max(2 * 6 * (TILE + 3), 2 * 9 * 256 + 2 * 6 * 256 + 4 * (TILE + 3)) + 64, 44 * NTK + 2 * 2 * (TILE + 2) + 2 * 44 * 3 * NS)
    UB = c.sbuf("UB", [128, UBN], BF16)

    def carve(name, off, shape, dt):
        sz = 2 if dt in (F32, I32) else 1
        n = int(np.prod(shape[1:])) * sz
        a = UB.t[0:shape[0], off:off + n]
        if sz == 2:
            a = a.bitcast(dt)
        if len(shape) == 3:
            a = a.rearrange("p (a b) -> p a b", a=shape[1])
        elif len(shape) == 4:
            a = a.rearrange("p (a b c) -> p a b c", a=shape[1], b=shape[2])
        return Buf(c, a, name), off + n

    def barrier():
        engs = [c.pe, c.dve, c.act, c.pool, c.sp]
        for e in engs:
            for f in engs:
                if f is not e and f.cnt > 0:
                    e.need((f.sem, f.cnt))
                if f is not e:
                    for tok in f.old:
                        e.need(tok)
            for b in c.dma_bufs:
                e.need((b.dsem, b.dcnt))

    oneb = c.sbuf("oneb", [128, 1], F32); MS(oneb[:], 1.0)
    MACC = c.sbuf("MACC", [128, 2, TILE + NS], F32)
    RSB = c.sbuf("RSB", [128, 512], F32)
    LS_HS = c.sbuf("LS_HS", [128, 8, 3, NS], F32)
    LS_H0 = c.sbuf("LS_H0", [128, 8, NS], F32)
    LS_XA = c.sbuf("LS_XA", [128, 8, NS], F32)
    LS_HN = c.sbuf("LS_HN", [128, 8, NS], F32)
    LS_GY = c.sbuf("LS_GY", [128, 8, NS], F32)
    sl = lambda v, a, n_: V(v.buf, v.ap[:, a:a + n_])
    K_ = K()
    K_.__dict__.update(locals())
    return _program2(K_)

def _program2(k):
    c, nc, di, do, T = k.c, k.nc, k.di, k.do, k.T
    dve, act, pool, pe, sp = k.dve, k.act, k.pool, k.pe, k.sp
    TT, TS, STT, ACT, CP, MS = k.TT, k.TS, k.STT, k.ACT, k.CP, k.MS
    PS, PSB, WP, TA, TBp, TSm = k.PS, k.PSB, k.WP, k.TA, k.TBp, k.TSm
    ones, ident, identb, pv = k.ones, k.ident, k.identb, k.pv
    xP, xS, hP, hS, TILE, carve, barrier = k.xP, k.xS, k.hP, k.hS, k.TILE, k.carve, k.barrier
    tiles = []
    t0 = 0
    while t0 < T:
        n = min(TILE, T - t0)
        tiles.append((t0, n))
        t0 += n

    def wchunk(W, k0, kc, col0, ncols, q=None):
        nm, l_, sub = W
        key = (nm, l_, sub, k0, kc, col0, ncols)
        wb = WP.get()
        v = wb.t[:, 0:kc * ncols].rearrange("p (k n) -> p k n", k=kc)
        if c.plan is None:
            c.rec.append(key)
            src = k.wsrc(nm, l_, sub)[k0 * 128:(k0 + kc) * 128, col0:col0 + ncols].rearrange("(k p) n -> p k n", p=128)
            c.dma(pool, wb.v(v), src)
        else:
            off_ = c.plan[0][key]
            src = c.bfw[l_][off_:off_ + 128 * kc * ncols].rearrange("(p m) -> p m", p=128)
            c.dma(sp, wb[:, 0:kc * ncols], V(k.CV[(nm, l_)], src))
        return wb, v

    def WW(nm, l, sub=None):
        return (nm, l, sub)

    def dense_fm(w2d, kc, c0, ncols, srcs, sink, cw=256):
        for col0 in range(c0, c0 + ncols, cw):
            n_ = min(cw, c0 + ncols - col0)
            wb, wv = wchunk(w2d, 0, kc, col0, n_)
            for j in range(0, n_, 128):
                m = min(128, n_ - j)
                for gi, (hv, nt) in enumerate(srcs):
                    ps = PS.get()
                    for kk in range(kc):
                        c.mm(ps[0:m, 0:nt], wb.v(wv[:, kk, j:j + m]), hv(kk), start=(kk == 0), stop=(kk == kc - 1))
                    sink(gi, col0 + j, m, ps)

    def dense_tm(w2d, kc, c0, ncols, srcs, sink, cw=256):
        for col0 in range(c0, c0 + ncols, cw):
            n_ = min(cw, c0 + ncols - col0)
            wb, wv = wchunk(w2d, 0, kc, col0, n_)
            for gi, (hv, nt) in enumerate(srcs):
                for tt in range(0, nt, 128):
                    P = min(128, nt - tt)
                    ps = PS.get()
                    for kk in range(kc):
                        c.mm(ps[0:P, 0:n_], hv(kk, tt, P), wb.v(wv[:, kk, :]), start=(kk == 0), stop=(kk == kc - 1))
                    sink(gi, tt, P, col0, n_, ps)

    def tm2fm(dst_fn, src, P, ncol, dt=F32):
        idn = ident if dt == F32 else identb
        for j in range(ncol // 128):
            if dt == F32:
                ps = PS.get()
                pv_ = ps[:, 0:P]
            else:
                ps = PSB.get()
                pv_ = ps[:, 0:P]
            c.tr(pv_, src(j * 128, 128), idn[0:P, 0:P])
            CP(dst_fn(j), pv_, eng=act)

    def rmsnorm(xv, hv, n, gname, gi):
        ps = PS.get()
        for cc in range(16):
            sq = TA.get()
            ACT(sq[:, 0:n], xv(cc), AF.Square)
            c.mm(ps[:, 0:n], ones[:, :], sq[:, 0:n], start=(cc == 0), stop=(cc == 15), inc=True)
        rs = TA.get()
        ACT(rs[:, 0:n], ps[:, 0:n], AF.Sqrt, scale=1.0 / D, bias=k.epsb[:, 0:1])
        c.op(dve, "reciprocal", rs[:, 0:n], [rs[:, 0:n]])
        for cc in range(16):
            STT(hv(cc), xv(cc), pv(gname, gi * 16 + cc), rs[:, 0:n], ALU.mult, ALU.mult)

    epsb = c.sbuf("epsb", [128, 1], F32); MS(epsb[:], EPS)
    k.epsb = epsb

    def conv_fm(out, taps, wname, bname, l, ntap, j, nj):
        TS(out, taps[ntap - 1], pv(wname, (l * ntap + ntap - 1) * nj + j), pv(bname, l * nj + j), ALU.mult, ALU.add)
        for i in range(ntap - 1):
            STT(out, taps[i], pv(wname, (l * ntap + i) * nj + j), out, ALU.mult, ALU.add)

    for ti, (tok0, n) in enumerate(tiles):
        has_s = (ti == 0)
        nblk = n // 128
        for b in range(nblk):
            for q4 in range(4):
                xr = TA.get()
                c.dma(sp, xr[:, :], di["x_prompt"][tok0 + b * 128: tok0 + (b + 1) * 128, q4 * 512:(q4 + 1) * 512])
                ps = PS.get()
                for jj in range(4):
                    c.tr(ps[:, jj * 128:(jj + 1) * 128], xr[:, jj * 128:(jj + 1) * 128], ident[:, :])
                CP(xP[:, q4 * 4:(q4 + 1) * 4, b * 128:(b + 1) * 128], ps.v(ps.t[:, :].rearrange("p (a b) -> p a b", a=4)))
        if has_s:
            for q4 in range(4):
                xr = TA.get()
                c.dma(sp, xr[0:NS, :], di["x_sample"][:, q4 * 512:(q4 + 1) * 512])
                ps = PS.get()
                for jj in range(4):
                    c.tr(ps[:, jj * NS:(jj + 1) * NS], xr[0:NS, jj * 128:(jj + 1) * 128], ident[0:NS, 0:NS])
                CP(xS[:, q4 * 4:(q4 + 1) * 4, :], ps.v(ps.t[:, 0:4 * NS].rearrange("p (a b) -> p a b", a=4)))

        for l in range(DEPTH):
            last = (ti == len(tiles) - 1)
            off = 0
            oTP, off = carve("oTP", off, [128, 32, n], BF16)
            oTS, off = carve("oTS", off, [128, 32, NS], BF16)
            PB, _ = carve("PB", off, [128, 4, 1024], F32)
            GYb, off = carve("GY", off, [128, 8, 512], F32)
            k.PBv8 = GYb
            off_fmb = off
            FMB, off = carve("FMB", off, [128, 6, n + 3], F32)
            rmsnorm(lambda cc: xP[:, cc, 0:n], lambda cc: hP[:, cc, 0:n], n, "g_mix", l)
            if has_s:
                rmsnorm(lambda cc: xS[:, cc, :], lambda cc: hS[:, cc, :], NS, "g_mix", l)
            srcs_fm = [(lambda kk: hP[:, kk, 0:n], n)] + ([(lambda kk: hS[:, kk, :], NS)] if has_s else [])
            srcs_tm = [(lambda kk, tt, P: hP[:, kk, tt:tt + P], n)] + ([(lambda kk, tt, P: hS[:, kk, tt:tt + P], NS)] if has_s else [])
            w_in = WW("w_in", l)
            MS(oTP[:, :, :], 0.0)
            if has_s:
                MS(oTS[:, :, :], 0.0)

            k.WW = WW
            mixer_lru(k, l, ti, n, has_s, last, srcs_fm, oTP, oTS, dense_fm, conv_fm, tm2fm, FMB)
            barrier()
            DWb, _ = carve("DWb", off_fmb, [128, 9, 256], F32)
            mixer_B(k, l, ti, tok0, n, has_s, last, srcs_fm, srcs_tm, oTP, oTS, dense_fm, dense_tm, PB, DWb)
            barrier()
            mixer_C(k, l, ti, tok0, n, has_s, last, srcs_fm, srcs_tm, oTP, oTS, dense_fm, dense_tm, conv_fm, PB, off_fmb)
            barrier()
            DW, _ = carve("DW", off_fmb, [128, 9, 256], F32)
            mixer_D(k, l, ti, tok0, n, has_s, last, srcs_tm, oTP, oTS, dense_tm, PB, DW)

            barrier()
            MT, _ = carve("MT", 32 * (n + NS), [128, 16, n + NS], BF16)
            groups = [(oTP, hP, n, 0)] + ([(oTS, hS, NS, n)] if has_s else [])
            for jb in range(0, 16, 2):
                accs = {}
                for br in range(4):
                    wbb, wbv = wchunk(WW("w_branch", l, br), 0, 8, jb * 128, 256)
                    wgb, wgv = wchunk(WW("w_gate", l, br), 0, 16, jb * 128, 256)
                    for j2 in range(2):
                        for gi, (oT_, h_, nt, moff) in enumerate(groups):
                            psb_ = PS.get()
                            for kk in range(8):
                                c.mm(psb_[:, 0:nt], wbb.v(wbv[:, kk, j2 * 128:(j2 + 1) * 128]), oT_[:, br * 8 + kk, 0:nt], start=(kk == 0), stop=(kk == 7))
                            psg = PS.get()
                            for kk in range(16):
                                c.mm(psg[:, 0:nt], wgb.v(wgv[:, kk, j2 * 128:(j2 + 1) * 128]), h_[:, kk, 0:nt], start=(kk == 0), stop=(kk == 15))
                            sg = TA.get()
                            ACT(sg[:, 0:nt], psg[:, 0:nt], AF.Sigmoid)
                            mv = k.MACC[:, j2, moff:moff + nt]
                            if br == 0:
                                TT(mv, sg[:, 0:nt], psb_[:, 0:nt], ALU.mult)
                            else:
                                TT(sg[:, 0:nt], sg[:, 0:nt], psb_[:, 0:nt], ALU.mult)
                                TT(mv, mv, sg[:, 0:nt], ALU.add)
                            if br == 3:
                                CP(MT[:, jb + j2, moff:moff + nt], mv, eng=act)
            def sink_res(xg):
                def f(gi, col, m, ps):
                    xv = (xP[:, col // 128, 0:n] if gi == 0 else xS[:, col // 128, :])
                    nt = n if gi == 0 else NS
                    TT(xv, xv, ps[:, 0:nt], ALU.add)
                return f
            msrc = [(lambda kk: MT[:, kk, 0:n], n)] + ([(lambda kk: MT[:, kk, n:n + NS], NS)] if has_s else [])
            dense_fm(WW("w_out", l), 16, 0, D, msrc, sink_res(None))
            barrier()
            aT, _ = carve("aT", 0, [128, 44, n + NS], BF16)
            rmsnorm(lambda cc: xP[:, cc, 0:n], lambda cc: hP[:, cc, 0:n], n, "g_ffn", l)
            if has_s:
                rmsnorm(lambda cc: xS[:, cc, :], lambda cc: hS[:, cc, :], NS, "g_ffn", l)
            k.WW = WW
            ffn_phase(k, l, ti, n, has_s, last, srcs_fm, aT, wchunk, conv_fm, tm2fm)
            asrc = [(lambda kk: aT[:, kk, 0:n], n)] + ([(lambda kk: aT[:, kk, n:n + NS], NS)] if has_s else [])
            wd = WW("ffn_w_down", l)
            for jb in range(16):
                pss = [PS.get() for _ in asrc]
                kgs = [(0, 16), (16, 16), (32, 12)]
                for (k0, kcn) in kgs:
                    wb, wv = wchunk(wd, k0, kcn, jb * 128, 128)
                    for gi, (av, nt) in enumerate(asrc):
                        for kk in range(kcn):
                            c.mm(pss[gi][:, 0:nt], wb.v(wv[:, kk, :]), av(k0 + kk), start=(k0 + kk == 0), stop=(k0 + kk == 43), inc=(kk == kcn - 1))
                for gi, (av, nt) in enumerate(asrc):
                    xv = (xP[:, jb, 0:n] if gi == 0 else xS[:, jb, :])
                    TT(xv, xv, pss[gi][:, 0:nt], ALU.add)
            barrier()

        def final(xv, nt, ydram):
            ps = PS.get()
            for cc in range(16):
                sq = TA.get()
                ACT(sq[:, 0:nt], xv(cc), AF.Square)
                c.mm(ps[:, 0:nt], ones[:, :], sq[:, 0:nt], start=(cc == 0), stop=(cc == 15), inc=True)
            rs = k.RSB
            ACT(rs[:, 0:nt], ps[:, 0:nt], AF.Sqrt, scale=1.0 / D, bias=epsb[:, 0:1])
            c.op(dve, "reciprocal", rs[:, 0:nt], [rs[:, 0:nt]])
            for tt in range(0, nt, 128):
                P = min(128, nt - tt)
                for q4 in range(4):
                    yt = TA.get()
                    pso = PS.get()
                    for jj in range(4):
                        cc = q4 * 4 + jj
                        yn = TA.get()
                        STT(yn[:, 0:P], k.sl(xv(cc), tt, P), pv("g_final", cc), rs[:, tt:tt + P], ALU.mult, ALU.mult)
                        c.tr(pso[0:P, jj * 128:(jj + 1) * 128], yn[:, 0:P], ident[:, :])
                    CP(yt[0:P, :], pso[0:P, :], eng=act)
                    c.dma(sp, ydram[tt:tt + P, q4 * 512:(q4 + 1) * 512], yt[0:P, :])
        final(lambda cc: xP[:, cc, 0:n], n, do["y_prompt"][tok0:tok0 + n, :])
        if has_s:
            final(lambda cc: xS[:, cc, :], NS, do["y_sample"])
        barrier()

def load_tm2fm(k, dram2d, ncol, dst_fn):
    c = k.c
    for c0 in range(0, ncol, 512):
        w = min(512, ncol - c0)
        xr = k.TA.get()
        c.dma(k.sp, xr[0:NS, 0:w], dram2d[:, c0:c0 + w])
        for jj in range(w // 128):
            ps = k.PS.get()
            c.tr(ps[:, 0:NS], xr[0:NS, jj * 128:(jj + 1) * 128], k.ident[0:NS, 0:NS])
            k.CP(dst_fn(c0 // 128 + jj), ps[:, 0:NS], eng=k.act)


def store_fm2tm(k, src_fn, nch, dram2d, P=NS):
    c = k.c
    for j0 in range(0, nch, 4):
        m = min(4, nch - j0)
        ps = k.PS.get()
        for jj in range(m):
            c.tr(ps[0:P, jj * 128:(jj + 1) * 128], src_fn(j0 + jj), k.ident[:, :])
        yt = k.TA.get()
        k.CP(yt[0:P, 0:m * 128], ps[0:P, 0:m * 128], eng=k.act)
        c.dma(k.sp, dram2d[:, j0 * 128:(j0 + m) * 128], yt[0:P, 0:m * 128])


def mixer_lru(k, l, ti, n, has_s, last, srcs_fm, oTP, oTS, dense_fm, conv_fm, tm2fm, FMB):
    c, di, do = k.c, k.di, k.do
    TT, TS, STT, ACT, CP, MS, TA, TBp, PS, pv = k.TT, k.TS, k.STT, k.ACT, k.CP, k.MS, k.TA, k.TBp, k.PS, k.pv
    w_in = k.WW("w_in", l)
    GY = k.PBv8
    if has_s:
        HS, H0S, XAS, HNS, GYS = k.LS_HS, k.LS_H0, k.LS_XA, k.LS_HN, k.LS_GY
        for i in range(3):
            load_tm2fm(k, di["state_lru_conv"][l, :, i, :], 1024, lambda j, i=i: HS[:, j, i, :])
        load_tm2fm(k, di["state_lru_h"][l], 1024, lambda j: H0S[:, j, :])

    def sink_y(gi, col, m, ps):
        j = (col - O_YA) // 128
        if gi == 0:
            ACT(GY[:, j, 0:n], ps[:, 0:n], AF.Gelu_apprx_tanh)
        else:
            ACT(GYS[:, j, :], ps[:, 0:NS], AF.Gelu_apprx_tanh)
    dense_fm(w_in, 16, O_YA, 1024, srcs_fm, sink_y)

    def lru_core(j, xc, nt, init, hs_out):
        xcb = TBp.get()
        CP(xcb[:, 0:nt], xc, eng=k.act)
        ps_r = PS.get()
        c.mm(ps_r[:, 0:nt], k.LWA[:, l, j, :], xcb[:, 0:nt])
        r = TA.get()
        ACT(r[:, 0:nt], ps_r[:, 0:nt], AF.Sigmoid, bias=pv("lru_ba", l * 8 + j))
        ps_i = PS.get()
        c.mm(ps_i[:, 0:nt], k.LWX[:, l, j, :], xcb[:, 0:nt])
        a = TA.get()
        ACT(a[:, 0:nt], r[:, 0:nt], AF.Exp, scale=k.LCC[:, l * 8 + j:l * 8 + j + 1])
        ACT(r[:, 0:nt], r[:, 0:nt], AF.Exp, scale=k.LCC2[:, l * 8 + j:l * 8 + j + 1])
        ACT(r[:, 0:nt], r[:, 0:nt], AF.Sqrt, scale=-1.0, bias=k.oneb[:, 0:1])
        ig = TA.get()
        ACT(ig[:, 0:nt], ps_i[:, 0:nt], AF.Sigmoid, bias=pv("lru_bx", l * 8 + j))
        TT(ig[:, 0:nt], ig[:, 0:nt], xc, ALU.mult)
        TT(ig[:, 0:nt], ig[:, 0:nt], r[:, 0:nt], ALU.mult)
        return a, ig

    def sink_x(gi, col, m, ps):
        j = (col - O_XA) // 128
        if gi == 0:
            xpad = FMB[:, j % 6, 0:n + 3]
            CP(FMB[:, j % 6, 0:3], k.HI_LRU[:, l, j, :])
            CP(FMB[:, j % 6, 3:n + 3], ps[:, 0:n], eng=k.act)
            xc = TA.get()
            conv_fm(xc[:, 0:n], [FMB[:, j % 6, i:i + n] for i in range(4)], "lru_conv_w", "lru_conv_b", l, 4, j, 8)
            CP(k.HI_LRU[:, l, j, :], FMB[:, j % 6, n:n + 3])
            a, u = lru_core(j, xc[:, 0:n], n, None, None)
            hs = TA.get()
            c.op(k.dve, "tensor_tensor_scan", hs[:, 0:n], [a[:, 0:n], u[:, 0:n], k.H_LRU[:, l, j:j + 1]], ALU.mult, ALU.add)
            CP(k.H_LRU[:, l, j:j + 1], hs[:, n - 1:n])
            TT(oTP[:, j, 0:n], hs[:, 0:n], GY[:, j, 0:n], ALU.mult)
        else:
            CP(XAS[:, j, :], ps[:, 0:NS], eng=k.act)
            xc = TA.get()
            conv_fm(xc[:, 0:NS], [HS[:, j, 0, :], HS[:, j, 1, :], HS[:, j, 2, :], XAS[:, j, :]], "lru_conv_w", "lru_conv_b", l, 4, j, 8)
            a, u = lru_core(j, xc[:, 0:NS], NS, None, None)
            hs = TA.get()
            TT(hs[:, 0:NS], a[:, 0:NS], H0S[:, j, :], ALU.mult)
            TT(HNS[:, j, :], hs[:, 0:NS], u[:, 0:NS], ALU.add)
            TT(oTS[:, j, :], HNS[:, j, :], GYS[:, j, :], ALU.mult)
    dense_fm(w_in, 16, O_XA, 1024, srcs_fm, sink_x)

    if has_s:
        store_fm2tm(k, lambda j: HNS[:, j, :], 8, do["lru_h_sample"][l])
        store_fm2tm(k, lambda j: XAS[:, j, :], 8, do["lru_conv_sample"][l, :, 2, :])
        for i in range(2):
            c.dma(k.sp, do["lru_conv_sample"][l, :, i, :], di["state_lru_conv"][l, :, i + 1, :])
    if last:
        store_fm2tm(k, lambda jj: k.H_LRU[:, l, :], 1, do["lru_h_prompt"][l].rearrange("(j p) -> j p", p=128), P=8)
        for i in range(3):
            store_fm2tm(k, lambda jj, i=i: k.HI_LRU[:, l, :, i], 1, do["lru_conv_prompt"][l, i].rearrange("(j p) -> j p", p=128), P=8)


def ffn_phase(k, l, ti, n, has_s, last, srcs_fm, aT, wchunk, conv_fm, tm2fm):
    c, di, do = k.c, k.di, k.do
    TT, TS, STT, ACT, CP, MS, TA, TBp, PS, pv = k.TT, k.TS, k.STT, k.ACT, k.CP, k.MS, k.TA, k.TBp, k.PS, k.pv
    off = 44 * (n + NS)
    UP, off = k.carve("UP", off, [128, 2, n + 2], F32)
    if has_s:
        HSF, off = k.carve("HSF", off, [128, 44, 2, NS], F32)
        US, off = k.carve("US", off, [128, 44, NS], F32)
        for i in range(2):
            load_tm2fm(k, di["state_ffn_conv"][l, :, i, :], DFF, lambda j, i=i: HSF[:, j, i, :])
    wu = k.WW("ffn_w_up", l)
    wv_ = k.WW("ffn_w_val", l)
    for j0 in range(0, 44, 2):
        wub, wuv = wchunk(wu, 0, 16, j0 * 128, 256)
        wvb, wvv = wchunk(wv_, 0, 16, j0 * 128, 256)
        for j2 in range(2):
            j = j0 + j2
            for gi, (hv, nt) in enumerate(srcs_fm):
                psu = PS.get()
                for kk in range(16):
                    c.mm(psu[:, 0:nt], wub.v(wuv[:, kk, j2 * 128:(j2 + 1) * 128]), hv(kk), start=(kk == 0), stop=(kk == 15))
                psv = PS.get()
                for kk in range(16):
                    c.mm(psv[:, 0:nt], wvb.v(wvv[:, kk, j2 * 128:(j2 + 1) * 128]), hv(kk), start=(kk == 0), stop=(kk == 15))
                uc = TA.get()
                if gi == 0:
                    s = j % 2
                    CP(UP[:, s, 0:2], k.HI_FFN[:, l, j, :])
                    CP(UP[:, s, 2:n + 2], psu[:, 0:n], eng=k.act)
                    conv_fm(uc[:, 0:n], [UP[:, s, i:i + n] for i in range(3)], "ffn_conv_w", "ffn_conv_b", l, 3, j, 44)
                    CP(k.HI_FFN[:, l, j, :], UP[:, s, n:n + 2])
                    ACT(uc[:, 0:n], uc[:, 0:n], AF.Gelu_apprx_tanh)
                    TT(aT[:, j, 0:n], uc[:, 0:n], psv[:, 0:n], ALU.mult)
                else:
                    CP(US[:, j, :], psu[:, 0:NS], eng=k.act)
                    conv_fm(uc[:, 0:NS], [HSF[:, j, 0, :], HSF[:, j, 1, :], US[:, j, :]], "ffn_conv_w", "ffn_conv_b", l, 3, j, 44)
                    ACT(uc[:, 0:NS], uc[:, 0:NS], AF.Gelu_apprx_tanh)
                    TT(aT[:, j, n:n + NS], uc[:, 0:NS], psv[:, 0:NS], ALU.mult)
    if has_s:
        store_fm2tm(k, lambda j: US[:, j, :], 44, do["ffn_conv_sample"][l, :, 1, :])
        c.dma(k.sp, do["ffn_conv_sample"][l, :, 0, :], di["state_ffn_conv"][l, :, 1, :])
    if last:
        for i in range(2):
            store_fm2tm(k, lambda jj, i=i: k.HI_FFN[:, l, :, i], 1, do["ffn_conv_prompt"][l, i].rearrange("(j p) -> j p", p=128), P=44)

def rstd_from_ss(k, out, ss, P, denom):
    k.ACT(out, ss, AF.Ln, scale=1.0 / denom, bias=k.epsb[0:P, 0:1])
    k.ACT(out, out, AF.Exp, scale=-0.5)


def tr_bf(k, dst, src, P, ncol=128):
    ps = k.PSB.get()
    k.c.tr(ps[0:ncol, 0:P], src, k.identb[0:P, 0:P])
    k.CP(dst, ps[0:ncol, 0:P], eng=k.act)


def block_D(k, l, P, h0, qv, kv, vv, gv, cosv, sinv, Sv, grevv, dst_fn, DW):
    c = k.c
    TT, TS, STT, ACT, CP, PS, TBp = k.TT, k.TS, k.STT, k.ACT, k.CP, k.PS, k.TBp

    def w(i):
        return DW[0:P, i, :]

    def v4(x):
        return V(x.buf, x.ap.rearrange("p (h a d) -> p h a d", h=2, a=2))

    def bc4(t):
        a = t.ap
        return V(t.buf, bass.AP(a.tensor, a.offset, [list(a.ap[0]), [0, 2], [0, 2], [1, 64]]))

    def bc3(t):
        a = t.ap
        return V(t.buf, bass.AP(a.tensor, a.offset, [list(a.ap[0]), [0, 2], [1, 64]]))

    def half(x, a_):
        x4 = x.ap.rearrange("p (h a d) -> p h a d", h=2, a=2)
        return V(x.buf, x4[:, :, a_, :])

    def rope(dst, src):
        TT(v4(w(0)), v4(src), bc4(cosv), ALU.mult)
        TT(half(w(1), 0), half(src, 1), bc3(sinv), ALU.mult)
        TT(half(w(1), 1), half(src, 0), bc3(sinv), ALU.mult)
        TT(half(dst, 0), half(w(0), 0), half(w(1), 0), ALU.subtract)
        TT(half(dst, 1), half(w(0), 1), half(w(1), 1), ALU.add)

    rope(w(2), qv)
    rope(w(3), kv)
    A = TBp.get(); B = TBp.get(); Cb = TBp.get()
    qr_b, kr_b = A[0:P, 0:256], A[0:P, 256:512]
    qh_b, v_b = B[0:P, 0:256], B[0:P, 256:512]
    vh_b, ob_b = Cb[0:P, 0:256], Cb[0:P, 256:512]
    CP(qr_b, w(2)); CP(kr_b, w(3), eng=k.act)
    for hh in range(2):
        h = h0 + hh
        TS(k.sl(qh_b, hh * 128, 128), k.sl(w(2), hh * 128, 128), k.gpow[0:P, h:h + 1], None, ALU.mult)
        TS(k.sl(vh_b, hh * 128, 128), k.sl(vv, hh * 128, 128), grevv[0:P, h:h + 1], None, ALU.mult)
    CP(v_b, vv, eng=k.act)
    ACT(w(6), gv, AF.Silu)
    for hh in range(2):
        h = h0 + hh
        cs = slice(hh * 128, (hh + 1) * 128)
        Tt = k.TTp.get()
        qT, kT, qhT = Tt[:, 0:P], Tt[:, 128:128 + P], Tt[:, 256:256 + P]
        tr_bf(k, qT, k.sl(qr_b, hh * 128, 128), P)
        tr_bf(k, kT, k.sl(kr_b, hh * 128, 128), P)
        tr_bf(k, qhT, k.sl(qh_b, hh * 128, 128), P)
        S = Sv(hh)
        Sb = k.SBFp.get()
        CP(Sb[:, :], S, eng=k.act)
        ps_sc = PS.get()
        c.mm(ps_sc[0:P, 0:P], kT, qT)
        Pm = k.PMp.get()
        TT(Pm[0:P, 0:P], ps_sc[0:P, 0:P], k.GM[0:P, h, 0:P], ALU.mult)
        ps_o = PS.get()
        c.mm(ps_o[0:P, 0:128], Pm[0:P, 0:P], k.sl(v_b, hh * 128, 128), start=True, stop=False)
        c.mm(ps_o[0:P, 0:128], qhT, Sb[:, :], start=False, stop=True)
        ps_u = PS.get()
        c.mm(ps_u[:, 0:128], k.sl(kr_b, hh * 128, 128), k.sl(vh_b, hh * 128, 128))
        STT(S, S, math.exp(LOG_GAMMA[h] * P), ps_u[:, 0:128], ALU.mult, ALU.add)
        sm = k.TSm.get()
        ACT(k.sl(w(7), 0, 128), ps_o[0:P, 0:128], AF.Square, accum_out=sm[0:P, 0:1])
        rstd_from_ss(k, sm[0:P, 1:2], sm[0:P, 0:1], P, 128.0)
        STT(k.sl(ob_b, hh * 128, 128), ps_o[0:P, 0:128], sm[0:P, 1:2], k.sl(w(6), hh * 128, 128), ALU.mult, ALU.mult)
        tr_bf(k, dst_fn(h), k.sl(ob_b, hh * 128, 128), P)


def mixer_D(k, l, ti, tok0, n, has_s, last, srcs_tm, oTP, oTS, dense_tm, PB, DW):
    c, di, do = k.c, k.di, k.do
    ACT, CP, TS = k.ACT, k.CP, k.TS
    w_in = k.WW("w_in", l)
    nblk = n // 128
    k.make_rope(tok0, nblk)
    for hp in range(4):
        h0 = 2 * hp
        offs = [O_RQ + 256 * hp, O_RK + 256 * hp, O_RV + 256 * hp, O_RG + 256 * hp]
        for wi, o_ in enumerate(offs):
            def sink(gi, tt, P, col0, n_, ps, wi=wi):
                dst = PB[0:P, tt // 128, wi * 256:(wi + 1) * 256] if gi == 0 else k.SPB[0:P, wi * 256:(wi + 1) * 256]
                if wi == 1:
                    ACT(dst, ps[0:P, 0:256], AF.Copy, scale=128.0 ** -0.5)
                else:
                    CP(dst, ps[0:P, 0:256], eng=k.act)
            dense_tm(w_in, 16, o_, 256, srcs_tm, sink)
        for b in range(nblk):
            block_D(k, l, 128, h0,
                    PB[:, b, 0:256], PB[:, b, 256:512], PB[:, b, 512:768], PB[:, b, 768:1024],
                    k.cosT[:, b, :], k.sinT[:, b, :],
                    lambda hh: k.S_D[:, l, h0 + hh, :], k.grev,
                    lambda h, b=b: oTP[:, 24 + h, b * 128:(b + 1) * 128], DW)
        if has_s:
            for j in range(NS):
                sp0 = k.SP0p.get()
                c.dma(k.sp, sp0[0:1, :], k.SPB[j:j + 1, :])
                ss = k.SSp.get()
                c.dma(k.sp, ss[:, 0:2, :], di["state_ret"][l, j, h0:h0 + 2].rearrange("h k v -> k h v"))
                block_D(k, l, 1, h0,
                        sp0[0:1, 0:256], sp0[0:1, 256:512], sp0[0:1, 512:768], sp0[0:1, 768:1024],
                        k.cosS[0:1, 0, :], k.sinS[0:1, 0, :],
                        lambda hh, ss=ss: ss[:, hh, :], k.ones,
                        lambda h, j=j: oTS[:, 24 + h, j:j + 1], DW)
                c.dma(k.sp, do["ret_sample"][l, j, h0:h0 + 2].rearrange("h k v -> k h v"), ss[:, 0:2, :])
    if last:
        c.dma(k.sp, do["ret_prompt"][l].rearrange("h k v -> k h v"), k.S_D[:, l, :, :])

def block_C(k, l, P, g, xc_fn, zs_v, sdt_v, Sg, dst_fn, DW):
    c = k.c
    TT, TS, STT, ACT, CP, PS, TBp = k.TT, k.TS, k.STT, k.ACT, k.CP, k.PS, k.TBp
    h0 = 8 * g
    xs = V(DW.ap.tensor and DW.buf if False else DW.buf, DW.t[0:P, 0:2, :].rearrange("p a b -> p (a b)")) if False else None
    xs = DW.v(DW.t[0:P, 0:2, :].rearrange("p a b -> p (a b)"))
    y = DW.v(DW.t[0:P, 2:4, :].rearrange("p a b -> p (a b)"))
    sc_sb = DW[0:P, 5, 0:P]
    LBh = DW[0:P, 6, 0:128]
    tmp = DW[0:P, 7, 0:P]
    tmp2 = DW[:, 8, 0:P]
    for i in range(4):
        ps = PS.get()
        c.tr(ps[0:P, 0:128], xc_fn(i), k.ident[:, :])
        CP(k.sl(xs, i * 128, 128), ps[0:P, 0:128], eng=k.act)
    Tt = k.TTp.get()
    B_fm, C_fm = Tt[:, 0:P], Tt[:, 128:128 + P]
    CP(B_fm, xc_fn(4)); CP(C_fm, xc_fn(5), eng=k.act)
    Bt = k.BTp.get()
    B_tm = Bt[0:P, :]
    psb = k.PSB.get()
    c.tr(psb[0:P, 0:128], B_fm, k.identb[:, :])
    CP(B_tm, psb[0:P, 0:128], eng=k.act)
    sm = k.TSm.get()
    dt, logd, b_sb, erev, dtr = sm[0:P, 0:8], sm[0:P, 8:16], sm[0:P, 16:24], sm[0:P, 24:32], sm[0:P, 32:40]
    sm2 = k.TSm.get()
    dS = sm2[:, 0:8]
    TT(dt, sdt_v, k.DTB[l][0:P, h0:h0 + 8], ALU.add)
    ACT(dt, dt, AF.Exp)
    ACT(dt, dt, AF.Ln, bias=k.oneb[0:P, 0:1])
    TT(logd, dt, k.NEGA[l][0:P, h0:h0 + 8], ALU.mult)
    ps1 = PS.get()
    c.mm(ps1[0:P, 0:8], k.causal[0:P, 0:P], logd)
    CP(b_sb, ps1[0:P, 0:8])
    ps2 = PS.get()
    c.mm(ps2[0:P, 0:8], k.trirev[0:P, 0:P], logd)
    ACT(erev, ps2[0:P, 0:8], AF.Exp)
    ps3 = PS.get()
    c.mm(ps3[:, 0:8], k.ones[0:P, 0:128], logd)
    ACT(dS, ps3[:, 0:8], AF.Exp)
    TT(dtr, dt, erev, ALU.mult)

    def bc64(t):
        a = t.ap
        return V(t.buf, bass.AP(a.tensor, a.offset, [list(a.ap[0]), [1, 8], [0, 64]]))

    def r3(t):
        return V(t.buf, t.ap.rearrange("p (h d) -> p h d", h=8))
    vb_ = TBp.get(); vh_ = TBp.get(); yn_ = TBp.get(); Sb_ = TBp.get()
    v_b, vh_b, yn_b = vb_[0:P, :], vh_[0:P, :], yn_[0:P, :]
    TT(r3(v_b), r3(xs), bc64(dt), ALU.mult)
    TT(r3(vh_b), r3(xs), bc64(dtr), ALU.mult)
    Sb = Sb_.v(Sb_.t[:, :].rearrange("p (h d) -> p h d", h=8))
    CP(Sb, Sg, eng=k.act)
    ps_sc = PS.get()
    c.mm(ps_sc[0:P, 0:P], B_fm, C_fm)
    CP(sc_sb, ps_sc[0:P, 0:P])
    for hh in range(8):
        h = h0 + hh
        a = logd.ap
        CP(LBh, V(logd.buf, bass.AP(a.tensor, a.offset + hh, [list(a.ap[0]), [0, 128]])))
        ps_bt = PS.get()
        c.mm(ps_bt[:, 0:P], LBh, k.causal[0:P, 0:P])
        STT(tmp, ps_bt[0:P, 0:P], b_sb[:, hh:hh + 1], k.negm[0:P, 0:P], ALU.subtract, ALU.add)
        TS(tmp, tmp, 0.0, None, ALU.min)
        ACT(tmp, tmp, AF.Exp)
        Pm = k.PMp.get()
        TT(Pm[0:P, 0:P], tmp, sc_sb, ALU.mult)
        ACT(tmp2, ps_bt[:, 0:P], AF.Exp)
        qh = k.QHp.get()
        TT(qh[:, 0:P], xc_fn(5), tmp2, ALU.mult)
        ps_o = PS.get()
        c.mm(ps_o[0:P, 0:64], Pm[0:P, 0:P], k.sl(v_b, hh * 64, 64), start=True, stop=False)
        c.mm(ps_o[0:P, 0:64], qh[:, 0:P], Sb[:, hh, :], start=False, stop=True)
        STT(k.sl(y, hh * 64, 64), k.sl(xs, hh * 64, 64), k.SSD_D[l][0:P, h:h + 1], ps_o[0:P, 0:64], ALU.mult, ALU.add)
    ps_u = PS.get()
    c.mm(ps_u[:, 0:512], B_tm, vh_b)
    a = dS.ap
    TT(Sg, Sg, V(dS.buf, bass.AP(a.tensor, a.offset, [list(a.ap[0]), [1, 8], [0, 64]])), ALU.mult)
    TT(Sg, Sg, ps_u.v(ps_u.t[:, 0:512].rearrange("p (h d) -> p h d", h=8)), ALU.add)
    TT(y, y, zs_v, ALU.mult)
    sm3 = k.TSm.get()
    ACT(k.sl(xs, 0, 512), y, AF.Square, accum_out=sm3[0:P, 0:1])
    rstd_from_ss(k, sm3[0:P, 1:2], sm3[0:P, 0:1], P, 512.0)
    TS(yn_b, y, sm3[0:P, 1:2], None, ALU.mult)
    for i in range(4):
        psb = k.PSB.get()
        c.tr(psb[:, 0:P], k.sl(yn_b, i * 128, 128), k.identb[0:P, 0:P])
        TS(dst_fn(4 * g + i), psb[:, 0:P], k.pv("ssd_norm_w", l * 8 + 4 * g + i), None, ALU.mult)


def mixer_C(k, l, ti, tok0, n, has_s, last, srcs_fm, srcs_tm, oTP, oTS, dense_fm, dense_tm, conv_fm, PB, off_fmb):
    c, di, do = k.c, k.di, k.do
    ACT, CP, TS, PS = k.ACT, k.CP, k.TS, k.PS
    w_in = k.WW("w_in", l)
    nblk = n // 128
    off = off_fmb
    DW, off = k.carve("DWc", off, [128, 9, 256], F32)
    XC, off = k.carve("XC", off, [128, 6, 256], F32)
    XP2, off = k.carve("XP2", off, [128, 2, n + 3], F32)
    if has_s:
        HSC, XSC, XCS = k.C_HSC, k.C_XSC, k.C_XCS
        for i in range(3):
            load_tm2fm(k, di["state_ssd_conv"][l, :, i, :], 1536, lambda j, i=i: HSC[:, j, i, :])
    for g in range(2):
        chunks = [4 * g, 4 * g + 1, 4 * g + 2, 4 * g + 3, 8 + g, 10 + g]
        for slot, ch in enumerate(chunks):
            def sink(gi, col, m, ps, slot=slot, ch=ch):
                if gi == 0:
                    s2 = slot % 2
                    CP(XP2[:, s2, 0:3], k.HI_SSD[:, l, ch, :])
                    CP(XP2[:, s2, 3:n + 3], ps[:, 0:n], eng=k.act)
                    conv_fm(XC[:, slot, 0:n], [XP2[:, s2, i:i + n] for i in range(4)], "ssd_conv_w", "ssd_conv_b", l, 4, ch, 12)
                    CP(k.HI_SSD[:, l, ch, :], XP2[:, s2, n:n + 3])
                    ACT(XC[:, slot, 0:n], XC[:, slot, 0:n], AF.Silu)
                else:
                    CP(XSC[:, ch, :], ps[:, 0:NS], eng=k.act)
                    conv_fm(XCS[:, slot, :], [HSC[:, ch, 0, :], HSC[:, ch, 1, :], HSC[:, ch, 2, :], XSC[:, ch, :]], "ssd_conv_w", "ssd_conv_b", l, 4, ch, 12)
                    ACT(XCS[:, slot, :], XCS[:, slot, :], AF.Silu)
            dense_fm(w_in, 16, O_XBC + 128 * ch, 128, srcs_fm, sink, cw=128)

        def sink_z(gi, tt, P, col0, n_, ps):
            cc = col0 - (O_SZ + 512 * g)
            dst = PB[0:P, tt // 128, cc:cc + n_] if gi == 0 else k.SPB[0:P, cc:cc + n_]
            ACT(dst, ps[0:P, 0:n_], AF.Silu)
        dense_tm(w_in, 16, O_SZ + 512 * g, 512, srcs_tm, sink_z)

        def sink_dt(gi, tt, P, col0, n_, ps):
            dst = PB[0:P, tt // 128, 512:520] if gi == 0 else k.SPB[0:P, 512:520]
            CP(dst, ps[0:P, 0:8], eng=k.act)
        dense_tm(w_in, 16, O_DT + 8 * g, 8, srcs_tm, sink_dt)
        for b in range(nblk):
            block_C(k, l, 128, g, lambda i, b=b: XC[:, i, b * 128:(b + 1) * 128], PB[:, b, 0:512], PB[:, b, 512:520],
                    k.S_C[:, l, 8 * g:8 * g + 8, :], lambda j, b=b: oTP[:, 16 + j, b * 128:(b + 1) * 128], DW)
        if has_s:
            for j in range(NS):
                sp0 = k.SP0p.get()
                c.dma(k.sp, sp0[0:1, 0:520], k.SPB[j:j + 1, 0:520])
                ss = k.SSp.get()
                ssv = ss.v(ss.t[:, :, :].rearrange("p a b -> p (a b)")[:, 0:512].rearrange("p (h d) -> p h d", h=8))
                c.dma(k.sp, ssv, di["state_ssd"][l, j, 8 * g:8 * g + 8].rearrange("h n v -> n h v"))
                block_C(k, l, 1, g, lambda i, j=j: XCS[:, i, j:j + 1], sp0[0:1, 0:512], sp0[0:1, 512:520],
                        ssv, lambda jj, j=j: oTS[:, 16 + jj, j:j + 1], DW)
                c.dma(k.sp, do["ssd_sample"][l, j, 8 * g:8 * g + 8].rearrange("h n v -> n h v"), ssv)
    if has_s:
        store_fm2tm(k, lambda j: XSC[:, j, :], 12, do["ssd_conv_sample"][l, :, 2, :])
        for i in range(2):
            c.dma(k.sp, do["ssd_conv_sample"][l, :, i, :], di["state_ssd_conv"][l, :, i + 1, :])
    if last:
        c.dma(k.sp, do["ssd_prompt"][l].rearrange("h n v -> n h v"), k.S_C[:, l, :, :])
        for i in range(3):
            store_fm2tm(k, lambda jj, i=i: k.HI_SSD[:, l, :, i], 1, do["ssd_conv_prompt"][l, i].rearrange("(j p) -> j p", p=128), P=12)

def block_B(k, l, P, h0, hq_v, hf_v, hi_v, sgh_fn, S_fn, dst_fn, DW):
    c = k.c
    TT, TS, STT, ACT, CP, PS, TBp = k.TT, k.TS, k.STT, k.ACT, k.CP, k.PS, k.TBp
    nch = max(1, P // 32)
    cs = min(32, P)

    def w(i):
        return DW[0:P, i, :]
    cols = slice(h0 * 128, h0 * 128 + 256)
    ACT(w(0), hq_v, AF.Silu)
    ACT(w(1), hf_v, AF.Sigmoid)
    ACT(w(2), hf_v, AF.Sigmoid, scale=-1.0)
    if l == 1:
        TT(w(1), w(1), k.OML1[0:P, cols], ALU.mult)
        TT(w(1), w(1), k.LB1[0:P, cols], ALU.add)
        TT(w(2), w(2), k.OML1[0:P, cols], ALU.mult)
    ACT(w(1), w(1), AF.Ln)
    ps_b = PS.get()
    c.mm(ps_b[0:P, 0:256], k.tri32[0:P, 0:P], w(1))
    ps_r = PS.get()
    c.mm(ps_r[0:P, 0:256], k.rev32[0:P, 0:P], w(1))
    A = TBp.get(); B = TBp.get()
    qt_b, kt_b = A[0:P, 0:256], A[0:P, 256:512]
    v_b, khc = B[0:P, 0:256], B[0:P, 256:384]
    ACT(w(4), ps_b[0:P, 0:256], AF.Exp)
    TT(qt_b, w(0), w(4), ALU.mult)
    ACT(w(4), ps_b[0:P, 0:256], AF.Exp, scale=-1.0)
    TT(kt_b, w(2), w(4), ALU.mult)
    ACT(w(4), ps_r[0:P, 0:256], AF.Exp)
    TT(w(3), w(2), w(4), ALU.mult)
    CP(v_b, hi_v, eng=k.act)
    for hh in range(2):
        h = h0 + hh
        hs = slice(hh * 128, (hh + 1) * 128)
        S = S_fn(hh)
        ps_d = PS.get()
        c.mm(ps_d[:, 0:nch], w(1)[:, hs], k.ind[0:P, 0:nch])
        sm = k.TSm.get()
        ACT(sm[:, 0:nch], ps_d[:, 0:nch], AF.Exp)
        Tt = k.TTp.get()
        qT, kT = Tt[:, 0:P], Tt[:, 128:128 + P]
        tr_bf(k, qT, qt_b[:, hs], P)
        tr_bf(k, kT, kt_b[:, hs], P)
        Sb = k.SB4p.get()
        for cc in range(nch):
            CP(Sb[:, cc * 128:(cc + 1) * 128], S, eng=k.act)
            TS(khc, w(3)[:, hs], k.ind[0:P, cc:cc + 1], None, ALU.mult)
            ps_u = PS.get()
            c.mm(ps_u[:, 0:128], khc, v_b[:, hs])
            STT(S, S, sm[:, cc:cc + 1], ps_u[:, 0:128], ALU.mult, ALU.add)
        ps_sc = PS.get()
        c.mm(ps_sc[0:P, 0:P], kT, qT)
        Pm = k.PMp.get()
        TT(Pm[0:P, 0:P], ps_sc[0:P, 0:P], k.tri32[0:P, 0:P], ALU.mult)
        ps_oi = PS.get()
        c.mm(ps_oi[:, 0:P], v_b[:, hs], Pm[0:P, 0:P])
        ps_oc = PS.get()
        for cc in range(nch):
            c.mm(ps_oc[:, cc * 32:cc * 32 + cs], Sb[:, cc * 128:(cc + 1) * 128], qT[:, cc * 32:cc * 32 + cs])
        o = DW[:, 5, 0:P]
        sq = DW[:, 6, 0:P]
        rs = DW[:, 7, 0:P]
        CP(o, ps_oi[:, 0:P], eng=k.act)
        TT(o, o, ps_oc[:, 0:P], ALU.add)
        ACT(sq, o, AF.Square)
        ps_ss = PS.get()
        c.mm(ps_ss[:, 0:P], k.ones[:, :], sq)
        ACT(rs, ps_ss[:, 0:P], AF.Ln, scale=1.0 / 128.0, bias=k.epsb[:, 0:1])
        ACT(rs, rs, AF.Exp, scale=-0.5)
        STT(sq, o, k.pv("hg_norm_w", l), rs, ALU.mult, ALU.mult)
        TT(dst_fn(h), sq, sgh_fn(hh), ALU.mult)


def mixer_B(k, l, ti, tok0, n, has_s, last, srcs_fm, srcs_tm, oTP, oTS, dense_fm, dense_tm, PB, DW):
    c, di, do = k.c, k.di, k.do
    ACT, CP = k.ACT, k.CP
    w_in = k.WW("w_in", l)
    nblk = n // 128
    for hp in range(4):
        h0 = 2 * hp
        for wi, o_ in enumerate([O_HQ + 256 * hp, O_HF + 256 * hp, O_HI + 256 * hp]):
            def sink(gi, tt, P, col0, n_, ps, wi=wi):
                dst = PB[0:P, tt // 128, wi * 256:(wi + 1) * 256] if gi == 0 else k.SPB[0:P, wi * 256:(wi + 1) * 256]
                CP(dst, ps[0:P, 0:256], eng=k.act)
            dense_tm(w_in, 16, o_, 256, srcs_tm, sink)

        def sink_g(gi, col, m, ps):
            hh = (col - (O_HG + 256 * hp)) // 128
            if gi == 0:
                ACT(PB[:, 2, hh * 256:hh * 256 + n], ps[:, 0:n], AF.Silu)
            else:
                ACT(PB[:, 3, hh * NS:(hh + 1) * NS], ps[:, 0:NS], AF.Silu)
        dense_fm(w_in, 16, O_HG + 256 * hp, 256, srcs_fm, sink_g)
        for b in range(nblk):
            block_B(k, l, 128, h0, PB[:, b, 0:256], PB[:, b, 256:512], PB[:, b, 512:768],
                    lambda hh, b=b: PB[:, 2, hh * 256 + b * 128:hh * 256 + (b + 1) * 128],
                    lambda hh: k.S_B[:, l, h0 + hh, :],
                    lambda h, b=b: oTP[:, 8 + h, b * 128:(b + 1) * 128], DW)
        if has_s:
            for j in range(NS):
                sp0 = k.SP0p.get()
                c.dma(k.sp, sp0[0:1, 0:768], k.SPB[j:j + 1, 0:768])
                ss = k.SSp.get()
                c.dma(k.sp, ss[:, 0:2, :], di["state_hgrn"][l, j, h0:h0 + 2].rearrange("h k v -> k h v"))
                block_B(k, l, 1, h0, sp0[0:1, 0:256], sp0[0:1, 256:512], sp0[0:1, 512:768],
                        lambda hh, j=j: PB[:, 3, hh * NS + j:hh * NS + j + 1],
                        lambda hh, ss=ss: ss[:, hh, :],
                        lambda h, j=j: oTS[:, 8 + h, j:j + 1], DW)
                c.dma(k.sp, do["hgrn_sample"][l, j, h0:h0 + 2].rearrange("h k v -> k h v"), ss[:, 0:2, :])
    if last:
        c.dma(k.sp, do["hgrn_prompt"][l].rearrange("h k v -> k h v"), k.S_B[:, l, :, :])


def _shard_inputs(inputs, ci):
    b = ci % 4
    s0 = ci * NS
    m = {}
    for name, a in inputs.items():
        a = np.asarray(a)
        if name == "x_prompt":
            m[name] = np.ascontiguousarray(a[b])
        elif name == "x_sample":
            m[name] = np.ascontiguousarray(a[s0:s0 + NS, 0, :])
        elif name.startswith("state_"):
            m[name] = np.ascontiguousarray(a[:, s0:s0 + NS])
        else:
            m[name] = np.ascontiguousarray(a)
    return m


_NC_CACHE = {}


def kernel(**inputs):
    T = int(np.asarray(inputs["x_prompt"]).shape[1])
    B = int(np.asarray(inputs["x_prompt"]).shape[0])
    if T not in _NC_CACHE:
        _NC_CACHE[T] = build(T)
    nc = _NC_CACHE[T]
    in_maps = [_shard_inputs(inputs, ci) for ci in range(8)]
    res = run_bass_kernel_spmd(nc, in_maps, core_ids=list(range(8)))
    R = res.results
    f = np.float32
    y_prompt = np.stack([R[b]["y_prompt"] for b in range(B)], 0).astype(f)
    y_sample = np.concatenate([R[ci]["y_sample"] for ci in range(8)], 0)[:, None, :].astype(f)
    outs = [y_prompt, y_sample]
    for nm in ["lru_h", "lru_conv", "hgrn", "ssd", "ssd_conv", "ret", "ffn_conv"]:
        outs.append(np.stack([R[b][nm + "_prompt"] for b in range(B)], 1).astype(f))
        outs.append(np.concatenate([R[ci][nm + "_sample"] for ci in range(8)], 1).astype(f))
    return tuple(outs)
```

```python
import math
from contextlib import ExitStack
from concourse.bass_utils import run_bass_kernel_spmd
import numpy as np
import concourse.bass as bass
import concourse.mybir as mybir

F32 = mybir.dt.float32
BF16 = mybir.dt.bfloat16
I32 = mybir.dt.int32
AF = mybir.ActivationFunctionType
ALU = mybir.AluOpType
AX = mybir.AxisListType


class V:
    __slots__ = ("buf", "ap")

    def __init__(self, buf, ap):
        self.buf = buf
        self.ap = ap

    def __getitem__(self, idx):
        return V(self.buf, self.ap[idx])


class Buf:
    def __init__(self, ctx, tensor, name):
        self.ctx = ctx
        self.t = tensor
        self.name = name
        self.w = None
        self.r = []
        self.dsem = None
        self.dcnt = 0

    def __getitem__(self, idx):
        return V(self, self.t[idx])

    def v(self, ap):
        return V(self, ap)


class Eng:
    def __init__(self, ctx, e, name):
        self.ctx = ctx
        self.e = e
        self.name = name
        self.sem = ctx.new_sem("c_" + name)
        self.cnt = 0
        self.waited = {}
        self.pend_r = []
        self.pend_w = []
        self.old = []

    def need(self, tok):
        if tok is None:
            return
        sem, val = tok
        k = id(sem)
        if self.waited.get(k, 0) >= val:
            return
        self.e.wait_ge(sem, val)
        self.waited[k] = val

    def deps(self, reads, writes):
        for b in reads:
            self.need(b.w)
        for b in writes:
            self.need(b.w)
            for t in b.r:
                self.need(t)

    def done(self, inst, reads, writes, inc=True):
        self.pend_r += reads
        self.pend_w += writes
        if inc:
            if self.cnt >= 30000:
                self.old.append((self.sem, self.cnt))
                self.sem = self.ctx.new_sem("c_" + self.name + str(self.ctx.nsem))
                self.cnt = 0
            inst.then_inc(self.sem, 1)
            self.cnt += 1
            tok = (self.sem, self.cnt)
            for b in self.pend_w:
                b.w = tok
                b.r = []
            for b in self.pend_r:
                if b.w is not tok:
                    b.r.append(tok)
                    if len(b.r) > 6:
                        b.r = b.r[-6:] if False else b.r
            self.pend_r = []
            self.pend_w = []


def _bufs(views):
    out = []
    for v in views:
        if isinstance(v, V) and v.buf is not None and v.buf not in out:
            out.append(v.buf)
    return out


def _ap(x):
    return x.ap if isinstance(x, V) else x


class Ctx:
    def __init__(self, nc, stack):
        self.nc = nc
        self.stack = stack
        self.nsem = 0
        self.pe = Eng(self, nc.tensor, "pe")
        self.dve = Eng(self, nc.vector, "dve")
        self.act = Eng(self, nc.scalar, "act")
        self.pool = Eng(self, nc.gpsimd, "pool")
        self.sp = Eng(self, nc.sync, "sp")
        self.dma_bufs = []
        self.drambuf = Buf(self, None, "dram")
        self.uid = 0

    def new_sem(self, name):
        self.nsem += 1
        return self.stack.enter_context(self.nc.semaphore(name))

    def sbuf(self, name, shape, dt=F32):
        t = self.stack.enter_context(self.nc.sbuf_tensor(name, list(shape), dt))
        return Buf(self, t, name)

    def psum(self, name, shape, dt=F32):
        t = self.stack.enter_context(self.nc.psum_tensor(name, list(shape), dt))
        return Buf(self, t, name)

    def op(self, eng, fn, out, ins, *args, **kw):
        extra_r = kw.pop("_reads", [])
        rb = _bufs(list(ins) + list(extra_r) + [v for v in kw.values() if isinstance(v, V)])
        wb = _bufs([out] + ([kw["accum_out"]] if "accum_out" in kw else []))
        eng.deps(rb, wb)
        kw2 = {k: _ap(v) for k, v in kw.items()}
        inst = getattr(eng.e, fn)(_ap(out), *[_ap(i) for i in ins], *args, **kw2)
        eng.done(inst, rb, wb)
        return inst

    def mm(self, out, lhsT, rhs, start=True, stop=True, inc=None, **kw):
        pe = self.pe
        rb = _bufs([lhsT, rhs])
        wb = _bufs([out])
        pe.deps(rb, wb if start else [])
        inst = pe.e.matmul(_ap(out), _ap(lhsT), _ap(rhs), start=start, stop=stop, **kw)
        pe.done(inst, rb, wb, inc=(stop if inc is None else inc))
        return inst

    def tr(self, out, in_, ident):
        pe = self.pe
        rb = _bufs([in_, ident])
        wb = _bufs([out])
        pe.deps(rb, wb)
        inst = pe.e.transpose(_ap(out), _ap(in_), _ap(ident))
        pe.done(inst, rb, wb, inc=True)
        return inst

    def dma(self, q, out, in_, **kw):
        rb = _bufs([in_])
        wb = _bufs([out])
        q.deps(rb, wb)
        b = (wb + rb)[0] if (wb + rb) else self.drambuf
        if b.dsem is None:
            b.dsem = self.new_sem("d_" + b.name)
            self.dma_bufs.append(b)
        inst = q.e.dma_start(out=_ap(out), in_=_ap(in_), **kw)
        inst.then_inc(b.dsem, 16)
        b.dcnt += 16
        tok = (b.dsem, b.dcnt)
        for x in wb:
            x.w = tok
            x.r = []
        for x in rb:
            x.r.append(tok)
        return inst

    def finish(self):
        for b in self.dma_bufs:
            self.sp.need((b.dsem, b.dcnt))


class Pool:
    def __init__(self, ctx, name, n, shape, dt=F32, psum=False):
        self.bufs = [(ctx.psum if psum else ctx.sbuf)(f"{name}{i}", shape, dt) for i in range(n)]
        self.i = 0

    def get(self):
        b = self.bufs[self.i % len(self.bufs)]
        self.i += 1
        return b


def bc(v, shape_ap):
    a = _ap(v)
    ap = bass.AP(a.tensor, a.offset, [list(a.ap[0])] + [list(x) for x in shape_ap])
    return V(v.buf, ap) if isinstance(v, V) else ap

D = 2048
KC = 16
DEPTH = 2
NS = 16
LRU_W = 1024
DFF = 5632
FC = 44
N_IN = 12816
EPS = 1e-6
O_XA, O_YA = 0, 1024
O_HQ, O_HF, O_HI, O_HG = 2048, 3072, 4096, 5120
O_SZ, O_XBC, O_DT = 6144, 7168, 8704
O_RQ, O_RK, O_RV, O_RG = 8720, 9744, 10768, 11792
LOG_GAMMA = [math.log1p(-2.0 ** (-5.0 - h)) for h in range(8)]


class K:
    pass


_PLANS = {}


def build(T):
    if T not in _PLANS:
        rec = []
        _build(T, None, rec)
        seen = {}
        off = [0, 0]
        for key in rec:
            if key not in seen:
                seen[key] = off[key[1]]
                off[key[1]] += 128 * key[4] * key[6]
        _PLANS[T] = (seen, off)
    return _build(T, _PLANS[T], None)


def _build(T, plan, rec):
    NT = T // 512
    nc = bass.Bass("TRN2", target_bir_lowering=False)
    di = {}
    do = {}

    def din(name, shape):
        di[name] = nc.dram_tensor(name, list(shape), F32, kind="ExternalInput").ap()

    def dout(name, shape):
        do[name] = nc.dram_tensor(name, list(shape), F32, kind="ExternalOutput").ap()

    din("x_prompt", [T, D]); din("x_sample", [NS, D])
    din("state_lru_h", [2, NS, 1024]); din("state_lru_conv", [2, NS, 3, 1024])
    din("state_hgrn", [2, NS, 8, 128, 128]); din("state_ssd", [2, NS, 16, 128, 64])
    din("state_ssd_conv", [2, NS, 3, 1536]); din("state_ret", [2, NS, 8, 128, 128])
    din("state_ffn_conv", [2, NS, 2, DFF])
    din("g_mix", [2, D]); din("g_ffn", [2, D]); din("w_in", [2, D, N_IN])
    din("lru_conv_w", [2, 4, 1024]); din("lru_conv_b", [2, 1024]); din("lru_wa", [2, 8, 128, 128])
    din("lru_ba", [2, 8, 128]); din("lru_wx", [2, 8, 128, 128]); din("lru_bx", [2, 8, 128])
    din("lru_lambda", [2, 1024]); din("hg_lb_logits", [2, 1024]); din("hg_norm_w", [2, 128])
    din("ssd_conv_w", [2, 4, 1536]); din("ssd_conv_b", [2, 1536]); din("ssd_dt_bias", [2, 16])
    din("ssd_a_log", [2, 16]); din("ssd_d", [2, 16]); din("ssd_norm_w", [2, 1024])
    din("w_branch", [2, 4, 1024, D]); din("w_gate", [2, D, 4, D]); din("w_out", [2, D, D])
    din("ffn_w_up", [2, D, DFF]); din("ffn_w_val", [2, D, DFF]); din("ffn_conv_w", [2, 3, DFF])
    din("ffn_conv_b", [2, DFF]); din("ffn_w_down", [2, DFF, D]); din("g_final", [D])
    dout("y_prompt", [T, D]); dout("y_sample", [NS, D])
    dout("lru_h_prompt", [2, 1024]); dout("lru_h_sample", [2, NS, 1024])
    dout("lru_conv_prompt", [2, 3, 1024]); dout("lru_conv_sample", [2, NS, 3, 1024])
    dout("hgrn_prompt", [2, 8, 128, 128]); dout("hgrn_sample", [2, NS, 8, 128, 128])
    dout("ssd_prompt", [2, 16, 128, 64]); dout("ssd_sample", [2, NS, 16, 128, 64])
    dout("ssd_conv_prompt", [2, 3, 1536]); dout("ssd_conv_sample", [2, NS, 3, 1536])
    dout("ret_prompt", [2, 8, 128, 128]); dout("ret_sample", [2, NS, 8, 128, 128])
    dout("ffn_conv_prompt", [2, 2, DFF]); dout("ffn_conv_sample", [2, NS, 2, DFF])

    bfw = [nc.dram_tensor(f"bfw{l}", [plan[1][l] if plan else 128], BF16, kind="Internal").ap() for l in range(2)]
    with ExitStack() as st:
        c = Ctx(nc, st)
        c.plan = plan
        c.rec = rec
        c.bfw = bfw
        _program(c, nc, di, do, T, NT)
        c.finish()
    return nc

def _program(c, nc, di, do, T, NT):
    dve, act, pool, pe, sp = c.dve, c.act, c.pool, c.pe, c.sp
    PI = math.pi
    NB = T // 128

    def OP(eng, fn, out, ins, *a, **k):
        return c.op(eng, fn, out, ins, *a, **k)

    def TT(out, a, b, op, eng=None):
        return c.op(eng or dve, "tensor_tensor", out, [a, b], op)

    def TS(out, a, s1, s2, op0, op1=ALU.bypass, eng=None):
        return c.op(eng or dve, "tensor_scalar", out, [a, s1, s2], op0, op1)

    def STT(out, a, s, b, op0, op1):
        return c.op(dve, "scalar_tensor_tensor", out, [a, s, b], op0, op1)

    def ACT(out, a, f, **k):
        return c.op(act, "activation", out, [a], f, **k)

    def CP(out, a, eng=None):
        e = eng or dve
        if e is act:
            return c.op(act, "activation", out, [a], AF.Copy)
        return c.op(e, "tensor_copy", out, [a])

    def MS(buf_view, val, eng=None):
        return c.op(eng or dve, "memset", buf_view, [], val)

    def ASEL(out, in_, pattern, cmp, base, cm):
        return c.op(pool, "affine_select", out, [in_], pattern=pattern, compare_op=cmp, fill=0.0,
                    base=base, channel_multiplier=cm)

    def wsrc(nm, l, sub):
        a_ = di[nm][l]
        if nm == "w_branch":
            a_ = a_[sub]
        elif nm == "w_gate":
            a_ = a_[:, sub, :]
        return a_
    CV = {}
    if c.plan is not None:
        for key, off_ in c.plan[0].items():
            nm, l_, sub, k0, kc, col0, ncols = key
            if (nm, l_) not in CV:
                CV[(nm, l_)] = Buf(c, None, f"cv_{nm}{l_}")
            src = wsrc(nm, l_, sub)[k0 * 128:(k0 + kc) * 128, col0:col0 + ncols].rearrange("(k p) n -> p k n", p=128)
            dst = c.bfw[l_][off_:off_ + 128 * kc * ncols].rearrange("(p k n) -> p k n", p=128, k=kc)
            c.dma(pool, V(CV[(nm, l_)], dst), src)
    PS = Pool(c, "ps", 6, [128, 512], F32, psum=True)
    PSB = Pool(c, "psb", 2, [128, 1024], BF16, psum=True)
    WP = Pool(c, "wbuf", 3, [128, 4096], BF16)
    TA = Pool(c, "ta", 5, [128, 512], F32)
    TBp = Pool(c, "tb", 4, [128, 512], BF16)
    TSm = Pool(c, "tsm", 8, [128, 64], F32)

    ones = c.sbuf("ones", [128, 128], F32); MS(ones[:], 1.0)
    onesb = c.sbuf("onesb", [128, 128], BF16); MS(onesb[:], 1.0)
    ident = c.sbuf("ident", [128, 128], F32)
    ASEL(ident[:], ones[:], [[-1, 128]], ALU.is_equal, 0, 1)
    identb = c.sbuf("identb", [128, 128], BF16); CP(identb[:], ident[:])
    causal = c.sbuf("causal", [128, 128], F32)
    ASEL(causal[:], ones[:], [[1, 128]], ALU.is_ge, 0, -1)
    trirev = c.sbuf("trirev", [128, 128], F32)
    ASEL(trirev[:], ones[:], [[-1, 128]], ALU.is_gt, 0, 1)
    same = c.sbuf("same", [128, 4, 32], F32)
    tmpc = c.sbuf("tmpc", [128, 4, 32], F32)
    ASEL(tmpc[:], bc(ones[:, 0:1], [[0, 4], [0, 32]]), [[-32, 4], [0, 32]], ALU.is_ge, 0, 1)
    ASEL(same[:], tmpc[:], [[32, 4], [0, 32]], ALU.is_ge, 31, -1)
    samef = same.v(same.t[:].rearrange("p c j -> p (c j)"))
    tri32 = c.sbuf("tri32", [128, 128], F32); TT(tri32[:], causal[:], samef, ALU.mult)
    rev32 = c.sbuf("rev32", [128, 128], F32); TT(rev32[:], trirev[:], samef, ALU.mult)
    mbd = tri32
    ind = c.sbuf("ind", [128, 4], F32); CP(ind[:], same[:, :, 0])
    negm = c.sbuf("negm", [128, 128], F32)
    TS(negm[:], causal[:], 30000.0, -30000.0, ALU.mult, ALU.add)
    dti = c.sbuf("dti", [128, 128], I32)
    c.op(pool, "iota", dti[:], [], pattern=[[1, 128]], base=0, channel_multiplier=-1)
    dtf = c.sbuf("dtf", [128, 128], F32); CP(dtf[:], dti[:])
    GM = c.sbuf("GM", [128, 8, 128], F32)
    pidx_i = c.sbuf("pidx_i", [128, 1], I32)
    c.op(pool, "iota", pidx_i[:], [], pattern=[[0, 1]], base=0, channel_multiplier=1)
    pidx = c.sbuf("pidx", [128, 1], F32); CP(pidx[:], pidx_i[:])
    pp1 = c.sbuf("pp1", [128, 1], F32); TS(pp1[:], pidx[:], 1.0, None, ALU.add)
    prv = c.sbuf("prv", [128, 1], F32); TS(prv[:], pidx[:], -1.0, 127.0, ALU.mult, ALU.add)
    gpow = c.sbuf("gpow", [128, 8], F32)
    grev = c.sbuf("grev", [128, 8], F32)
    for h in range(8):
        ACT(GM[:, h, :], dtf[:], AF.Exp, scale=LOG_GAMMA[h])
        TT(GM[:, h, :], GM[:, h, :], causal[:], ALU.mult)
        ACT(gpow[:, h:h + 1], pp1[:], AF.Exp, scale=LOG_GAMMA[h])
        ACT(grev[:, h:h + 1], prv[:], AF.Exp, scale=LOG_GAMMA[h])
    fi = c.sbuf("fi", [128, 64], I32)
    c.op(pool, "iota", fi[:], [], pattern=[[1, 64]], base=0, channel_multiplier=0)
    FR = c.sbuf("FR", [128, 64], F32); CP(FR[:], fi[:])
    ACT(FR[:], FR[:], AF.Exp, scale=-math.log(10000.0) / 64.0)
    RB = 2
    cosT = c.sbuf("cosT", [128, RB, 64], F32)
    sinT = c.sbuf("sinT", [128, RB, 64], F32)
    angb = c.sbuf("angb", [128, RB, 64], F32)
    ang2 = c.sbuf("ang2", [128, RB, 64], F32)
    angi = c.sbuf("angi", [128, RB, 64], I32)
    angm = c.sbuf("angm", [128, RB, 64], F32)
    posf = c.sbuf("posf", [128, RB], F32)

    def sin_of(out, src, P, nb):
        a2 = ang2[0:P, 0:nb, :]; ai = angi[0:P, 0:nb, :]; am = angm[0:P, 0:nb, :]
        TS(a2, src, 1.0 / (2 * PI), None, ALU.mult)
        CP(ai, a2)
        CP(a2, ai)
        STT(a2, a2, -2 * PI, src, ALU.mult, ALU.add)
        TS(am, a2, PI, None, ALU.is_gt)
        STT(a2, am, -2 * PI, a2, ALU.mult, ALU.add)
        TS(am, a2, -PI, None, ALU.is_lt)
        STT(a2, am, 2 * PI, a2, ALU.mult, ALU.add)
        TS(a2, a2, PI, -PI, ALU.min, ALU.max)
        ACT(out, a2, AF.Sin)

    def make_rope(tok0, nb):
        for b_ in range(nb):
            TS(posf[:, b_:b_ + 1], pidx[:, 0:1], float(tok0 + 128 * b_), None, ALU.add)
        TT(angb[:, 0:nb, :], bc(posf[:, 0:1], [[1, nb], [0, 64]]), bc(FR[:, 0:1], [[0, nb], [1, 64]]), ALU.mult)
        sin_of(sinT[:, 0:nb, :], angb[:, 0:nb, :], 128, nb)
        TS(angb[:, 0:nb, :], angb[:, 0:nb, :], PI / 2, None, ALU.add)
        sin_of(cosT[:, 0:nb, :], angb[:, 0:nb, :], 128, nb)

    cosS = c.sbuf("cosS", [1, 1, 64], F32)
    sinS = c.sbuf("sinS", [1, 1, 64], F32)
    TS(angb[0:1, 0, :], FR[0:1, :], 16384.0, None, ALU.mult)
    sin_of(sinS[:, :, :], angb[0:1, 0:1, :], 1, 1)
    TS(angb[0:1, 0:1, :], angb[0:1, 0:1, :], PI / 2, None, ALU.add)
    sin_of(cosS[:, :, :], angb[0:1, 0:1, :], 1, 1)
    TTp = Pool(c, "ttp", 2, [128, 384], BF16)
    PMp = Pool(c, "pmp", 2, [128, 128], BF16)
    SBFp = Pool(c, "sbfp", 2, [128, 128], BF16)
    SP0p = Pool(c, "sp0p", 2, [1, 1024], F32)
    SSp = Pool(c, "ssp", 2, [128, 4, 128], F32)
    BTp = Pool(c, "btp", 2, [128, 128], BF16)
    SB4p = Pool(c, "sb4p", 2, [128, 512], BF16)
    QHp = Pool(c, "qhp", 2, [128, 128], BF16)
    C_HSC = c.sbuf("C_HSC", [128, 12, 3, NS], F32)
    C_XSC = c.sbuf("C_XSC", [128, 12, NS], F32)
    C_XCS = c.sbuf("C_XCS", [128, 6, NS], F32)
    B_SGS = Buf(c, C_XCS.t[:, 0:4, :].rearrange("p (a b) n -> p a b n", a=2), "B_SGS")
    SPB = c.sbuf("SPB", [NS, 1024], F32)

    plist = [("g_mix", di["g_mix"].rearrange("l (c p) -> (l c) p", p=128)),
             ("g_ffn", di["g_ffn"].rearrange("l (c p) -> (l c) p", p=128)),
             ("g_final", di["g_final"].rearrange("(c p) -> c p", p=128)),
             ("lru_conv_w", di["lru_conv_w"].rearrange("l i (j p) -> (l i j) p", p=128)),
             ("lru_conv_b", di["lru_conv_b"].rearrange("l (j p) -> (l j) p", p=128)),
             ("lru_ba", di["lru_ba"].rearrange("l j p -> (l j) p")),
             ("lru_bx", di["lru_bx"].rearrange("l j p -> (l j) p")),
             ("lru_lambda", di["lru_lambda"].rearrange("l (j p) -> (l j) p", p=128)),
             ("hg_norm_w", di["hg_norm_w"]),
             ("ssd_conv_w", di["ssd_conv_w"].rearrange("l i (j p) -> (l i j) p", p=128)),
             ("ssd_conv_b", di["ssd_conv_b"].rearrange("l (j p) -> (l j) p", p=128)),
             ("ffn_conv_w", di["ffn_conv_w"].rearrange("l i (j p) -> (l i j) p", p=128)),
             ("ffn_conv_b", di["ffn_conv_b"].rearrange("l (j p) -> (l j) p", p=128)),
             ("ssd_norm_w", di["ssd_norm_w"].rearrange("l (j p) -> (l j) p", p=128))]
    tot = sum(a.shape[0] for _, a in plist)
    ntile = (tot + 127) // 128
    PVT = c.sbuf("PVT", [128, ntile * 128], F32)
    prow = c.sbuf("prow", [128, 128], F32)
    poff = {}
    g = 0
    segs = []
    for name, a in plist:
        poff[name] = g
        R = a.shape[0]
        r = 0
        while r < R:
            ti, ro = divmod(g + r, 128)
            m = min(R - r, 128 - ro)
            segs.append((ti, ro, a[r:r + m, :], m))
            r += m
        g += R
    for ti in range(ntile):
        MS(prow[:], 0.0)
        for (t2, ro, ap_, m) in segs:
            if t2 == ti:
                c.dma(sp, prow[ro:ro + m, :], ap_)
        pst = PS.get()
        c.tr(pst[:, 0:128], prow[:], ident[:])
        CP(PVT[:, ti * 128:(ti + 1) * 128], pst[:, 0:128])

    def pv(name, idx, n=1):
        o = poff[name] + idx
        return PVT[:, o:o + n]

    def bload(name, src2d_row, ncol):
        b = c.sbuf(name, [128, ncol], F32)
        a = src2d_row
        c.dma(sp, b[:], bass.AP(a.tensor, a.offset, [[0, 128], [1, ncol]]))
        return b

    LB1 = bload("lb1", di["hg_lb_logits"][1, :], 1024)
    OML1 = bload("oml1", di["hg_lb_logits"][0, :], 1024)
    TT(OML1[:], LB1[:], OML1[:], ALU.subtract)
    ACT(LB1[:], OML1[:], AF.Sigmoid)
    ACT(OML1[:], OML1[:], AF.Sigmoid, scale=-1.0)
    DTB = [bload(f"dtb{l}", di["ssd_dt_bias"][l, :], 16) for l in range(2)]
    NEGA = [bload(f"nega{l}", di["ssd_a_log"][l, :], 16) for l in range(2)]
    SSD_D = [bload(f"ssdd{l}", di["ssd_d"][l, :], 16) for l in range(2)]
    for l in range(2):
        ACT(NEGA[l][:], NEGA[l][:], AF.Exp)
        TS(NEGA[l][:], NEGA[l][:], -1.0, None, ALU.mult)
    LCC = c.sbuf("lcc", [128, 16], F32)
    LCC2 = c.sbuf("lcc2", [128, 16], F32)
    ACT(LCC[:], pv("lru_lambda", 0, 16), AF.Exp, scale=-1.0)
    ACT(LCC[:], LCC[:], AF.Ln, bias=1.0)
    TS(LCC2[:], LCC[:], -16.0, None, ALU.mult)
    TS(LCC[:], LCC[:], -8.0, None, ALU.mult)
    LWA = c.sbuf("lwa", [128, 2, 8, 128], BF16)
    LWX = c.sbuf("lwx", [128, 2, 8, 128], BF16)
    c.dma(pool, LWA[:], di["lru_wa"].rearrange("l n c d -> c l n d"))
    c.dma(pool, LWX[:], di["lru_wx"].rearrange("l n c d -> c l n d"))

    def zbuf(name, shape, dt=F32):
        b = c.sbuf(name, shape, dt)
        MS(b[:], 0.0)
        return b
    H_LRU = zbuf("H_LRU", [128, 2, 8])
    HI_LRU = zbuf("HI_LRU", [128, 2, 8, 3])
    HI_SSD = zbuf("HI_SSD", [128, 2, 12, 3])
    HI_FFN = zbuf("HI_FFN", [128, 2, 44, 2])
    S_B = zbuf("S_B", [128, 2, 8, 128])
    S_C = zbuf("S_C", [128, 2, 16, 64])
    S_D = zbuf("S_D", [128, 2, 8, 128])

    TILE = 256 if T > 256 else T
    xP = c.sbuf("xP", [128, 16, TILE], F32)
    xS = c.sbuf("xS", [128, 16, NS], F32)
    hP = c.sbuf("hP", [128, 16, TILE], BF16)
    hS = c.sbuf("hS", [128, 16, NS], BF16)
    NTK = TILE + NS
    UBN = max(32 * NTK + 2 * 4 * 1024 + max(2 * 6 * (TILE + 3), 2 * 9 * 256 + 2 * 6 * 256 + 4 * (TILE + 3)) + 64, 44 * NTK + 2 * 2 * (TILE + 2) + 2 * 44 * 3 * NS)
    UB = c.sbuf("UB", [128, UBN], BF16)

    def carve(name, off, shape, dt):
        sz = 2 if dt in (F32, I32) else 1
        n = int(np.prod(shape[1:])) * sz
        a = UB.t[0:shape[0], off:off + n]
        if sz == 2:
            a = a.bitcast(dt)
        if len(shape) == 3:
            a = a.rearrange("p (a b) -> p a b", a=shape[1])
        elif len(shape) == 4:
            a = a.rearrange("p (a b c) -> p a b c", a=shape[1], b=shape[2])
        return Buf(c, a, name), off + n

    def barrier():
        engs = [c.pe, c.dve, c.act, c.pool, c.sp]
        for e in engs:
            for f in engs:
                if f is not e and f.cnt > 0:
                    e.need((f.sem, f.cnt))
                if f is not e:
                    for tok in f.old:
                        e.need(tok)
            for b in c.dma_bufs:
                e.need((b.dsem, b.dcnt))

    oneb = c.sbuf("oneb", [128, 1], F32); MS(oneb[:], 1.0)
    MACC = c.sbuf("MACC", [128, 2, TILE + NS], F32)
    RSB = c.sbuf("RSB", [128, 512], F32)
    LS_HS = c.sbuf("LS_HS", [128, 8, 3, NS], F32)
    LS_H0 = c.sbuf("LS_H0", [128, 8, NS], F32)
    LS_XA = c.sbuf("LS_XA", [128, 8, NS], F32)
    LS_HN = c.sbuf("LS_HN", [128, 8, NS], F32)
    LS_GY = c.sbuf("LS_GY", [128, 8, NS], F32)
    sl = lambda v, a, n_: V(v.buf, v.ap[:, a:a + n_])
    K_ = K()
    K_.__dict__.update(locals())
    return _program2(K_)

def _program2(k):
    c, nc, di, do, T = k.c, k.nc, k.di, k.do, k.T
    dve, act, pool, pe, sp = k.dve, k.act, k.pool, k.pe, k.sp
    TT, TS, STT, ACT, CP, MS = k.TT, k.TS, k.STT, k.ACT, k.CP, k.MS
    PS, PSB, WP, TA, TBp, TSm = k.PS, k.PSB, k.WP, k.TA, k.TBp, k.TSm
    ones, ident, identb, pv = k.ones, k.ident, k.identb, k.pv
    xP, xS, hP, hS, TILE, carve, barrier = k.xP, k.xS, k.hP, k.hS, k.TILE, k.carve, k.barrier
    tiles = []
    t0 = 0
    while t0 < T:
        n = min(TILE, T - t0)
        tiles.append((t0, n))
        t0 += n

    def wchunk(W, k0, kc, col0, ncols, q=None):
        nm, l_, sub = W
        key = (nm, l_, sub, k0, kc, col0, ncols)
        wb = WP.get()
        v = wb.t[:, 0:kc * ncols].rearrange("p (k n) -> p k n", k=kc)
        if c.plan is None:
            c.rec.append(key)
            src = k.wsrc(nm, l_, sub)[k0 * 128:(k0 + kc) * 128, col0:col0 + ncols].rearrange("(k p) n -> p k n", p=128)
            c.dma(pool, wb.v(v), src)
        else:
            off_ = c.plan[0][key]
            src = c.bfw[l_][off_:off_ + 128 * kc * ncols].rearrange("(p m) -> p m", p=128)
            c.dma(sp, wb[:, 0:kc * ncols], V(k.CV[(nm, l_)], src))
        return wb, v

    def WW(nm, l, sub=None):
        return (nm, l, sub)

    def dense_fm(w2d, kc, c0, ncols, srcs, sink, cw=256):
        for col0 in range(c0, c0 + ncols, cw):
            n_ = min(cw, c0 + ncols - col0)
            wb, wv = wchunk(w2d, 0, kc, col0, n_)
            for j in range(0, n_, 128):
                m = min(128, n_ - j)
                for gi, (hv, nt) in enumerate(srcs):
                    ps = PS.get()
                    for kk in range(kc):
                        c.mm(ps[0:m, 0:nt], wb.v(wv[:, kk, j:j + m]), hv(kk), start=(kk == 0), stop=(kk == kc - 1))
                    sink(gi, col0 + j, m, ps)

    def dense_tm(w2d, kc, c0, ncols, srcs, sink, cw=256):
        for col0 in range(c0, c0 + ncols, cw):
            n_ = min(cw, c0 + ncols - col0)
            wb, wv = wchunk(w2d, 0, kc, col0, n_)
            for gi, (hv, nt) in enumerate(srcs):
                for tt in range(0, nt, 128):
                    P = min(128, nt - tt)
                    ps = PS.get()
                    for kk in range(kc):
                        c.mm(ps[0:P, 0:n_], hv(kk, tt, P), wb.v(wv[:, kk, :]), start=(kk == 0), stop=(kk == kc - 1))
                    sink(gi, tt, P, col0, n_, ps)

    def tm2fm(dst_fn, src, P, ncol, dt=F32):
        idn = ident if dt == F32 else identb
        for j in range(ncol // 128):
            if dt == F32:
                ps = PS.get()
                pv_ = ps[:, 0:P]
            else:
                ps = PSB.get()
                pv_ = ps[:, 0:P]
            c.tr(pv_, src(j * 128, 128), idn[0:P, 0:P])
            CP(dst_fn(j), pv_, eng=act)

    def rmsnorm(xv, hv, n, gname, gi):
        ps = PS.get()
        for cc in range(16):
            sq = TA.get()
            ACT(sq[:, 0:n], xv(cc), AF.Square)
            c.mm(ps[:, 0:n], ones[:, :], sq[:, 0:n], start=(cc == 0), stop=(cc == 15), inc=True)
        rs = TA.get()
        ACT(rs[:, 0:n], ps[:, 0:n], AF.Sqrt, scale=1.0 / D, bias=k.epsb[:, 0:1])
        c.op(dve, "reciprocal", rs[:, 0:n], [rs[:, 0:n]])
        for cc in range(16):
            STT(hv(cc), xv(cc), pv(gname, gi * 16 + cc), rs[:, 0:n], ALU.mult, ALU.mult)

    epsb = c.sbuf("epsb", [128, 1], F32); MS(epsb[:], EPS)
    k.epsb = epsb

    def conv_fm(out, taps, wname, bname, l, ntap, j, nj):
        TS(out, taps[ntap - 1], pv(wname, (l * ntap + ntap - 1) * nj + j), pv(bname, l * nj + j), ALU.mult, ALU.add)
        for i in range(ntap - 1):
            STT(out, taps[i], pv(wname, (l * ntap + i) * nj + j), out, ALU.mult, ALU.add)

    for ti, (tok0, n) in enumerate(tiles):
        has_s = (ti == 0)
        nblk = n // 128
        for b in range(nblk):
            for q4 in range(4):
                xr = TA.get()
                c.dma(sp, xr[:, :], di["x_prompt"][tok0 + b * 128: tok0 + (b + 1) * 128, q4 * 512:(q4 + 1) * 512])
                ps = PS.get()
                for jj in range(4):
                    c.tr(ps[:, jj * 128:(jj + 1) * 128], xr[:, jj * 128:(jj + 1) * 128], ident[:, :])
                CP(xP[:, q4 * 4:(q4 + 1) * 4, b * 128:(b + 1) * 128], ps.v(ps.t[:, :].rearrange("p (a b) -> p a b", a=4)))
        if has_s:
            for q4 in range(4):
                xr = TA.get()
                c.dma(sp, xr[0:NS, :], di["x_sample"][:, q4 * 512:(q4 + 1) * 512])
                ps = PS.get()
                for jj in range(4):
                    c.tr(ps[:, jj * NS:(jj + 1) * NS], xr[0:NS, jj * 128:(jj + 1) * 128], ident[0:NS, 0:NS])
                CP(xS[:, q4 * 4:(q4 + 1) * 4, :], ps.v(ps.t[:, 0:4 * NS].rearrange("p (a b) -> p a b", a=4)))

        for l in range(DEPTH):
            last = (ti == len(tiles) - 1)
            off = 0
            oTP, off = carve("oTP", off, [128, 32, n], BF16)
            oTS, off = carve("oTS", off, [128, 32, NS], BF16)
            PBa, _ = carve("PBa", off, [128, 2, 1024], F32)
            PBb, _ = carve("PBb", off + 4096, [128, 2, 1024], F32)
            PB = PBa
            PBs = [PBa, PBb]
            GYb, off = carve("GY", off, [128, 8, 512], F32)
            k.PBv8 = GYb
            off_fmb = off
            FMB, off = carve("FMB", off, [128, 6, n + 3], F32)
            rmsnorm(lambda cc: xP[:, cc, 0:n], lambda cc: hP[:, cc, 0:n], n, "g_mix", l)
            if has_s:
                rmsnorm(lambda cc: xS[:, cc, :], lambda cc: hS[:, cc, :], NS, "g_mix", l)
            srcs_fm = [(lambda kk: hP[:, kk, 0:n], n)] + ([(lambda kk: hS[:, kk, :], NS)] if has_s else [])
            srcs_tm = [(lambda kk, tt, P: hP[:, kk, tt:tt + P], n)] + ([(lambda kk, tt, P: hS[:, kk, tt:tt + P], NS)] if has_s else [])
            w_in = WW("w_in", l)
            MS(oTP[:, :, :], 0.0)
            if has_s:
                MS(oTS[:, :, :], 0.0)

            k.WW = WW
            mixer_lru(k, l, ti, n, has_s, last, srcs_fm, oTP, oTS, dense_fm, conv_fm, tm2fm, FMB)
            barrier()
            DWb, _ = carve("DWb", off_fmb, [128, 9, 256], F32)
            mixer_B(k, l, ti, tok0, n, has_s, last, srcs_fm, srcs_tm, oTP, oTS, dense_fm, dense_tm, PBs, DWb)
            barrier()
            mixer_C(k, l, ti, tok0, n, has_s, last, srcs_fm, srcs_tm, oTP, oTS, dense_fm, dense_tm, conv_fm, PB, off_fmb)
            barrier()
            DW, _ = carve("DW", off_fmb, [128, 9, 256], F32)
            mixer_D(k, l, ti, tok0, n, has_s, last, srcs_tm, oTP, oTS, dense_tm, PBs, DW)

            barrier()
            MT, _ = carve("MT", 32 * (n + NS), [128, 16, n + NS], BF16)
            groups = [(oTP, hP, n, 0)] + ([(oTS, hS, NS, n)] if has_s else [])
            for jb in range(0, 16, 2):
                accs = {}
                for br in range(4):
                    wbb, wbv = wchunk(WW("w_branch", l, br), 0, 8, jb * 128, 256)
                    wgb, wgv = wchunk(WW("w_gate", l, br), 0, 16, jb * 128, 256)
                    for j2 in range(2):
                        for gi, (oT_, h_, nt, moff) in enumerate(groups):
                            psb_ = PS.get()
                            for kk in range(8):
                                c.mm(psb_[:, 0:nt], wbb.v(wbv[:, kk, j2 * 128:(j2 + 1) * 128]), oT_[:, br * 8 + kk, 0:nt], start=(kk == 0), stop=(kk == 7))
                            psg = PS.get()
                            for kk in range(16):
                                c.mm(psg[:, 0:nt], wgb.v(wgv[:, kk, j2 * 128:(j2 + 1) * 128]), h_[:, kk, 0:nt], start=(kk == 0), stop=(kk == 15))
                            sg = TA.get()
                            ACT(sg[:, 0:nt], psg[:, 0:nt], AF.Sigmoid)
                            mv = k.MACC[:, j2, moff:moff + nt]
                            if br == 0:
                                TT(mv, sg[:, 0:nt], psb_[:, 0:nt], ALU.mult)
                            else:
                                TT(sg[:, 0:nt], sg[:, 0:nt], psb_[:, 0:nt], ALU.mult)
                                TT(mv, mv, sg[:, 0:nt], ALU.add)
                            if br == 3:
                                CP(MT[:, jb + j2, moff:moff + nt], mv, eng=act)
            def sink_res(xg):
                def f(gi, col, m, ps):
                    xv = (xP[:, col // 128, 0:n] if gi == 0 else xS[:, col // 128, :])
                    nt = n if gi == 0 else NS
                    TT(xv, xv, ps[:, 0:nt], ALU.add)
                return f
            msrc = [(lambda kk: MT[:, kk, 0:n], n)] + ([(lambda kk: MT[:, kk, n:n + NS], NS)] if has_s else [])
            dense_fm(WW("w_out", l), 16, 0, D, msrc, sink_res(None))
            barrier()
            aT, _ = carve("aT", 0, [128, 44, n + NS], BF16)
            rmsnorm(lambda cc: xP[:, cc, 0:n], lambda cc: hP[:, cc, 0:n], n, "g_ffn", l)
            if has_s:
                rmsnorm(lambda cc: xS[:, cc, :], lambda cc: hS[:, cc, :], NS, "g_ffn", l)
            k.WW = WW
            ffn_phase(k, l, ti, n, has_s, last, srcs_fm, aT, wchunk, conv_fm, tm2fm)
            asrc = [(lambda kk: aT[:, kk, 0:n], n)] + ([(lambda kk: aT[:, kk, n:n + NS], NS)] if has_s else [])
            wd = WW("ffn_w_down", l)
            for jb in range(16):
                pss = [PS.get() for _ in asrc]
                kgs = [(0, 16), (16, 16), (32, 12)]
                for (k0, kcn) in kgs:
                    wb, wv = wchunk(wd, k0, kcn, jb * 128, 128)
                    for gi, (av, nt) in enumerate(asrc):
                        for kk in range(kcn):
                            c.mm(pss[gi][:, 0:nt], wb.v(wv[:, kk, :]), av(k0 + kk), start=(k0 + kk == 0), stop=(k0 + kk == 43), inc=(kk == kcn - 1))
                for gi, (av, nt) in enumerate(asrc):
                    xv = (xP[:, jb, 0:n] if gi == 0 else xS[:, jb, :])
                    TT(xv, xv, pss[gi][:, 0:nt], ALU.add)
            barrier()

        def final(xv, nt, ydram):
            ps = PS.get()
            for cc in range(16):
                sq = TA.get()
                ACT(sq[:, 0:nt], xv(cc), AF.Square)
                c.mm(ps[:, 0:nt], ones[:, :], sq[:, 0:nt], start=(cc == 0), stop=(cc == 15), inc=True)
            rs = k.RSB
            ACT(rs[:, 0:nt], ps[:, 0:nt], AF.Sqrt, scale=1.0 / D, bias=epsb[:, 0:1])
            c.op(dve, "reciprocal", rs[:, 0:nt], [rs[:, 0:nt]])
            for tt in range(0, nt, 128):
                P = min(128, nt - tt)
                for q4 in range(4):
                    yt = TA.get()
                    pso = PS.get()
                    for jj in range(4):
                        cc = q4 * 4 + jj
                        yn = TA.get()
                        STT(yn[:, 0:P], k.sl(xv(cc), tt, P), pv("g_final", cc), rs[:, tt:tt + P], ALU.mult, ALU.mult)
                        c.tr(pso[0:P, jj * 128:(jj + 1) * 128], yn[:, 0:P], ident[:, :])
                    CP(yt[0:P, :], pso[0:P, :], eng=act)
                    c.dma(sp, ydram[tt:tt + P, q4 * 512:(q4 + 1) * 512], yt[0:P, :])
        final(lambda cc: xP[:, cc, 0:n], n, do["y_prompt"][tok0:tok0 + n, :])
        if has_s:
            final(lambda cc: xS[:, cc, :], NS, do["y_sample"])
        barrier()

def load_tm2fm(k, dram2d, ncol, dst_fn):
    c = k.c
    for c0 in range(0, ncol, 512):
        w = min(512, ncol - c0)
        xr = k.TA.get()
        c.dma(k.sp, xr[0:NS, 0:w], dram2d[:, c0:c0 + w])
        for jj in range(w // 128):
            ps = k.PS.get()
            c.tr(ps[:, 0:NS], xr[0:NS, jj * 128:(jj + 1) * 128], k.ident[0:NS, 0:NS])
            k.CP(dst_fn(c0 // 128 + jj), ps[:, 0:NS], eng=k.act)


def store_fm2tm(k, src_fn, nch, dram2d, P=NS):
    c = k.c
    for j0 in range(0, nch, 4):
        m = min(4, nch - j0)
        ps = k.PS.get()
        for jj in range(m):
            c.tr(ps[0:P, jj * 128:(jj + 1) * 128], src_fn(j0 + jj), k.ident[:, :])
        yt = k.TA.get()
        k.CP(yt[0:P, 0:m * 128], ps[0:P, 0:m * 128], eng=k.act)
        c.dma(k.sp, dram2d[:, j0 * 128:(j0 + m) * 128], yt[0:P, 0:m * 128])


def mixer_lru(k, l, ti, n, has_s, last, srcs_fm, oTP, oTS, dense_fm, conv_fm, tm2fm, FMB):
    c, di, do = k.c, k.di, k.do
    TT, TS, STT, ACT, CP, MS, TA, TBp, PS, pv = k.TT, k.TS, k.STT, k.ACT, k.CP, k.MS, k.TA, k.TBp, k.PS, k.pv
    w_in = k.WW("w_in", l)
    GY = k.PBv8
    if has_s:
        HS, H0S, XAS, HNS, GYS = k.LS_HS, k.LS_H0, k.LS_XA, k.LS_HN, k.LS_GY
        for i in range(3):
            load_tm2fm(k, di["state_lru_conv"][l, :, i, :], 1024, lambda j, i=i: HS[:, j, i, :])
        load_tm2fm(k, di["state_lru_h"][l], 1024, lambda j: H0S[:, j, :])

    def sink_y(gi, col, m, ps):
        j = (col - O_YA) // 128
        if gi == 0:
            ACT(GY[:, j, 0:n], ps[:, 0:n], AF.Gelu_apprx_tanh)
        else:
            ACT(GYS[:, j, :], ps[:, 0:NS], AF.Gelu_apprx_tanh)
    dense_fm(w_in, 16, O_YA, 1024, srcs_fm, sink_y)

    def lru_core(j, xc, nt, init, hs_out):
        xcb = TBp.get()
        CP(xcb[:, 0:nt], xc, eng=k.act)
        ps_r = PS.get()
        c.mm(ps_r[:, 0:nt], k.LWA[:, l, j, :], xcb[:, 0:nt])
        r = TA.get()
        ACT(r[:, 0:nt], ps_r[:, 0:nt], AF.Sigmoid, bias=pv("lru_ba", l * 8 + j))
        ps_i = PS.get()
        c.mm(ps_i[:, 0:nt], k.LWX[:, l, j, :], xcb[:, 0:nt])
        a = TA.get()
        ACT(a[:, 0:nt], r[:, 0:nt], AF.Exp, scale=k.LCC[:, l * 8 + j:l * 8 + j + 1])
        ACT(r[:, 0:nt], r[:, 0:nt], AF.Exp, scale=k.LCC2[:, l * 8 + j:l * 8 + j + 1])
        ACT(r[:, 0:nt], r[:, 0:nt], AF.Sqrt, scale=-1.0, bias=k.oneb[:, 0:1])
        ig = TA.get()
        ACT(ig[:, 0:nt], ps_i[:, 0:nt], AF.Sigmoid, bias=pv("lru_bx", l * 8 + j))
        TT(ig[:, 0:nt], ig[:, 0:nt], xc, ALU.mult)
        TT(ig[:, 0:nt], ig[:, 0:nt], r[:, 0:nt], ALU.mult)
        return a, ig

    def sink_x(gi, col, m, ps):
        j = (col - O_XA) // 128
        if gi == 0:
            xpad = FMB[:, j % 6, 0:n + 3]
            CP(FMB[:, j % 6, 0:3], k.HI_LRU[:, l, j, :])
            CP(FMB[:, j % 6, 3:n + 3], ps[:, 0:n], eng=k.act)
            xc = TA.get()
            conv_fm(xc[:, 0:n], [FMB[:, j % 6, i:i + n] for i in range(4)], "lru_conv_w", "lru_conv_b", l, 4, j, 8)
            CP(k.HI_LRU[:, l, j, :], FMB[:, j % 6, n:n + 3])
            a, u = lru_core(j, xc[:, 0:n], n, None, None)
            hs = TA.get()
            c.op(k.dve, "tensor_tensor_scan", hs[:, 0:n], [a[:, 0:n], u[:, 0:n], k.H_LRU[:, l, j:j + 1]], ALU.mult, ALU.add)
            CP(k.H_LRU[:, l, j:j + 1], hs[:, n - 1:n])
            TT(oTP[:, j, 0:n], hs[:, 0:n], GY[:, j, 0:n], ALU.mult)
        else:
            CP(XAS[:, j, :], ps[:, 0:NS], eng=k.act)
            xc = TA.get()
            conv_fm(xc[:, 0:NS], [HS[:, j, 0, :], HS[:, j, 1, :], HS[:, j, 2, :], XAS[:, j, :]], "lru_conv_w", "lru_conv_b", l, 4, j, 8)
            a, u = lru_core(j, xc[:, 0:NS], NS, None, None)
            hs = TA.get()
            TT(hs[:, 0:NS], a[:, 0:NS], H0S[:, j, :], ALU.mult)
            TT(HNS[:, j, :], hs[:, 0:NS], u[:, 0:NS], ALU.add)
            TT(oTS[:, j, :], HNS[:, j, :], GYS[:, j, :], ALU.mult)
    dense_fm(w_in, 16, O_XA, 1024, srcs_fm, sink_x)

    if has_s:
        store_fm2tm(k, lambda j: HNS[:, j, :], 8, do["lru_h_sample"][l])
        store_fm2tm(k, lambda j: XAS[:, j, :], 8, do["lru_conv_sample"][l, :, 2, :])
        for i in range(2):
            c.dma(k.sp, do["lru_conv_sample"][l, :, i, :], di["state_lru_conv"][l, :, i + 1, :])
    if last:
        store_fm2tm(k, lambda jj: k.H_LRU[:, l, :], 1, do["lru_h_prompt"][l].rearrange("(j p) -> j p", p=128), P=8)
        for i in range(3):
            store_fm2tm(k, lambda jj, i=i: k.HI_LRU[:, l, :, i], 1, do["lru_conv_prompt"][l, i].rearrange("(j p) -> j p", p=128), P=8)


def ffn_phase(k, l, ti, n, has_s, last, srcs_fm, aT, wchunk, conv_fm, tm2fm):
    c, di, do = k.c, k.di, k.do
    TT, TS, STT, ACT, CP, MS, TA, TBp, PS, pv = k.TT, k.TS, k.STT, k.ACT, k.CP, k.MS, k.TA, k.TBp, k.PS, k.pv
    off = 44 * (n + NS)
    UP, off = k.carve("UP", off, [128, 2, n + 2], F32)
    if has_s:
        HSF, off = k.carve("HSF", off, [128, 44, 2, NS], F32)
        US, off = k.carve("US", off, [128, 44, NS], F32)
        for i in range(2):
            load_tm2fm(k, di["state_ffn_conv"][l, :, i, :], DFF, lambda j, i=i: HSF[:, j, i, :])
    wu = k.WW("ffn_w_up", l)
    wv_ = k.WW("ffn_w_val", l)
    for j0 in range(0, 44, 2):
        wub, wuv = wchunk(wu, 0, 16, j0 * 128, 256)
        wvb, wvv = wchunk(wv_, 0, 16, j0 * 128, 256)
        for j2 in range(2):
            j = j0 + j2
            for gi, (hv, nt) in enumerate(srcs_fm):
                psu = PS.get()
                for kk in range(16):
                    c.mm(psu[:, 0:nt], wub.v(wuv[:, kk, j2 * 128:(j2 + 1) * 128]), hv(kk), start=(kk == 0), stop=(kk == 15))
                psv = PS.get()
                for kk in range(16):
                    c.mm(psv[:, 0:nt], wvb.v(wvv[:, kk, j2 * 128:(j2 + 1) * 128]), hv(kk), start=(kk == 0), stop=(kk == 15))
                uc = TA.get()
                if gi == 0:
                    s = j % 2
                    CP(UP[:, s, 0:2], k.HI_FFN[:, l, j, :])
                    CP(UP[:, s, 2:n + 2], psu[:, 0:n], eng=k.act)
                    conv_fm(uc[:, 0:n], [UP[:, s, i:i + n] for i in range(3)], "ffn_conv_w", "ffn_conv_b", l, 3, j, 44)
                    CP(k.HI_FFN[:, l, j, :], UP[:, s, n:n + 2])
                    ACT(uc[:, 0:n], uc[:, 0:n], AF.Gelu_apprx_tanh)
                    TT(aT[:, j, 0:n], uc[:, 0:n], psv[:, 0:n], ALU.mult)
                else:
                    CP(US[:, j, :], psu[:, 0:NS], eng=k.act)
                    conv_fm(uc[:, 0:NS], [HSF[:, j, 0, :], HSF[:, j, 1, :], US[:, j, :]], "ffn_conv_w", "ffn_conv_b", l, 3, j, 44)
                    ACT(uc[:, 0:NS], uc[:, 0:NS], AF.Gelu_apprx_tanh)
                    TT(aT[:, j, n:n + NS], uc[:, 0:NS], psv[:, 0:NS], ALU.mult)
    if has_s:
        store_fm2tm(k, lambda j: US[:, j, :], 44, do["ffn_conv_sample"][l, :, 1, :])
        c.dma(k.sp, do["ffn_conv_sample"][l, :, 0, :], di["state_ffn_conv"][l, :, 1, :])
    if last:
        for i in range(2):
            store_fm2tm(k, lambda jj, i=i: k.HI_FFN[:, l, :, i], 1, do["ffn_conv_prompt"][l, i].rearrange("(j p) -> j p", p=128), P=44)

def rstd_from_ss(k, out, ss, P, denom):
    k.ACT(out, ss, AF.Ln, scale=1.0 / denom, bias=k.epsb[0:P, 0:1])
    k.ACT(out, out, AF.Exp, scale=-0.5)


def tr_bf(k, dst, src, P, ncol=128):
    ps = k.PSB.get()
    k.c.tr(ps[0:ncol, 0:P], src, k.identb[0:P, 0:P])
    k.CP(dst, ps[0:ncol, 0:P], eng=k.act)


def block_D(k, l, P, h0, qv, kv, vv, gv, cosv, sinv, Sv, grevv, dst_fn, DW):
    c = k.c
    TT, TS, STT, ACT, CP, PS, TBp = k.TT, k.TS, k.STT, k.ACT, k.CP, k.PS, k.TBp

    def w(i):
        return DW[0:P, i, :]

    def v4(x):
        return V(x.buf, x.ap.rearrange("p (h a d) -> p h a d", h=2, a=2))

    def bc4(t):
        a = t.ap
        return V(t.buf, bass.AP(a.tensor, a.offset, [list(a.ap[0]), [0, 2], [0, 2], [1, 64]]))

    def bc3(t):
        a = t.ap
        return V(t.buf, bass.AP(a.tensor, a.offset, [list(a.ap[0]), [0, 2], [1, 64]]))

    def half(x, a_):
        x4 = x.ap.rearrange("p (h a d) -> p h a d", h=2, a=2)
        return V(x.buf, x4[:, :, a_, :])

    def rope(dst, src):
        TT(v4(w(0)), v4(src), bc4(cosv), ALU.mult)
        TT(half(w(1), 0), half(src, 1), bc3(sinv), ALU.mult)
        TT(half(w(1), 1), half(src, 0), bc3(sinv), ALU.mult)
        TT(half(dst, 0), half(w(0), 0), half(w(1), 0), ALU.subtract)
        TT(half(dst, 1), half(w(0), 1), half(w(1), 1), ALU.add)

    rope(w(2), qv)
    rope(w(3), kv)
    A = TBp.get(); B = TBp.get(); Cb = TBp.get()
    qr_b, kr_b = A[0:P, 0:256], A[0:P, 256:512]
    qh_b, v_b = B[0:P, 0:256], B[0:P, 256:512]
    vh_b, ob_b = Cb[0:P, 0:256], Cb[0:P, 256:512]
    CP(qr_b, w(2)); CP(kr_b, w(3), eng=k.act)
    for hh in range(2):
        h = h0 + hh
        TS(k.sl(qh_b, hh * 128, 128), k.sl(w(2), hh * 128, 128), k.gpow[0:P, h:h + 1], None, ALU.mult)
        TS(k.sl(vh_b, hh * 128, 128), k.sl(vv, hh * 128, 128), grevv[0:P, h:h + 1], None, ALU.mult)
    CP(v_b, vv, eng=k.act)
    ACT(w(6), gv, AF.Silu)
    for hh in range(2):
        h = h0 + hh
        cs = slice(hh * 128, (hh + 1) * 128)
        Tt = k.TTp.get()
        qT, kT, qhT = Tt[:, 0:P], Tt[:, 128:128 + P], Tt[:, 256:256 + P]
        tr_bf(k, qT, k.sl(qr_b, hh * 128, 128), P)
        tr_bf(k, kT, k.sl(kr_b, hh * 128, 128), P)
        tr_bf(k, qhT, k.sl(qh_b, hh * 128, 128), P)
        S = Sv(hh)
        Sb = k.SBFp.get()
        CP(Sb[:, :], S, eng=k.act)
        ps_sc = PS.get()
        c.mm(ps_sc[0:P, 0:P], kT, qT)
        Pm = k.PMp.get()
        TT(Pm[0:P, 0:P], ps_sc[0:P, 0:P], k.GM[0:P, h, 0:P], ALU.mult)
        ps_o = PS.get()
        c.mm(ps_o[0:P, 0:128], Pm[0:P, 0:P], k.sl(v_b, hh * 128, 128), start=True, stop=False)
        c.mm(ps_o[0:P, 0:128], qhT, Sb[:, :], start=False, stop=True)
        ps_u = PS.get()
        c.mm(ps_u[:, 0:128], k.sl(kr_b, hh * 128, 128), k.sl(vh_b, hh * 128, 128))
        STT(S, S, math.exp(LOG_GAMMA[h] * P), ps_u[:, 0:128], ALU.mult, ALU.add)
        sm = k.TSm.get()
        ACT(k.sl(w(7), 0, 128), ps_o[0:P, 0:128], AF.Square, accum_out=sm[0:P, 0:1])
        rstd_from_ss(k, sm[0:P, 1:2], sm[0:P, 0:1], P, 128.0)
        STT(k.sl(ob_b, hh * 128, 128), ps_o[0:P, 0:128], sm[0:P, 1:2], k.sl(w(6), hh * 128, 128), ALU.mult, ALU.mult)
        tr_bf(k, dst_fn(h), k.sl(ob_b, hh * 128, 128), P)


def mixer_D(k, l, ti, tok0, n, has_s, last, srcs_tm, oTP, oTS, dense_tm, PBs, DW):
    c, di, do = k.c, k.di, k.do
    ACT, CP, TS = k.ACT, k.CP, k.TS
    w_in = k.WW("w_in", l)
    nblk = n // 128
    k.make_rope(tok0, nblk)

    def proj(hp):
        PB = PBs[hp % 2]
        offs = [O_RQ + 256 * hp, O_RK + 256 * hp, O_RV + 256 * hp, O_RG + 256 * hp]
        for wi, o_ in enumerate(offs):
            def sink(gi, tt, P, col0, n_, ps, wi=wi):
                dst = PB[0:P, tt // 128, wi * 256:(wi + 1) * 256] if gi == 0 else k.SPB[0:P, wi * 256:(wi + 1) * 256]
                if wi == 1:
                    ACT(dst, ps[0:P, 0:256], AF.Copy, scale=128.0 ** -0.5)
                else:
                    CP(dst, ps[0:P, 0:256], eng=k.act)
            dense_tm(w_in, 16, o_, 256, srcs_tm, sink)

    def blocks(hp):
        PB = PBs[hp % 2]
        h0 = 2 * hp
        for b in range(nblk):
            block_D(k, l, 128, h0,
                    PB[:, b, 0:256], PB[:, b, 256:512], PB[:, b, 512:768], PB[:, b, 768:1024],
                    k.cosT[:, b, :], k.sinT[:, b, :],
                    lambda hh: k.S_D[:, l, h0 + hh, :], k.grev,
                    lambda h, b=b: oTP[:, 24 + h, b * 128:(b + 1) * 128], DW)
        if has_s:
            for j in range(NS):
                sp0 = k.SP0p.get()
                c.dma(k.sp, sp0[0:1, :], k.SPB[j:j + 1, :])
                ss = k.SSp.get()
                c.dma(k.sp, ss[:, 0:2, :], di["state_ret"][l, j, h0:h0 + 2].rearrange("h k v -> k h v"))
                block_D(k, l, 1, h0,
                        sp0[0:1, 0:256], sp0[0:1, 256:512], sp0[0:1, 512:768], sp0[0:1, 768:1024],
                        k.cosS[0:1, 0, :], k.sinS[0:1, 0, :],
                        lambda hh, ss=ss: ss[:, hh, :], k.ones,
                        lambda h, j=j: oTS[:, 24 + h, j:j + 1], DW)
                c.dma(k.sp, do["ret_sample"][l, j, h0:h0 + 2].rearrange("h k v -> k h v"), ss[:, 0:2, :])
    if has_s:
        for hp in range(4):
            proj(hp)
            blocks(hp)
    else:
        proj(0)
        for hp in range(4):
            if hp + 1 < 4:
                proj(hp + 1)
            blocks(hp)
    if last:
        c.dma(k.sp, do["ret_prompt"][l].rearrange("h k v -> k h v"), k.S_D[:, l, :, :])

def block_C(k, l, P, g, xc_fn, zs_v, sdt_v, Sg, dst_fn, DW):
    c = k.c
    TT, TS, STT, ACT, CP, PS, TBp = k.TT, k.TS, k.STT, k.ACT, k.CP, k.PS, k.TBp
    h0 = 8 * g
    xs = V(DW.ap.tensor and DW.buf if False else DW.buf, DW.t[0:P, 0:2, :].rearrange("p a b -> p (a b)")) if False else None
    xs = DW.v(DW.t[0:P, 0:2, :].rearrange("p a b -> p (a b)"))
    y = DW.v(DW.t[0:P, 2:4, :].rearrange("p a b -> p (a b)"))
    sc_sb = DW[0:P, 5, 0:P]
    LBh = DW[0:P, 6, 0:128]
    tmp = DW[0:P, 7, 0:P]
    tmp2 = DW[:, 8, 0:P]
    for i in range(4):
        ps = PS.get()
        c.tr(ps[0:P, 0:128], xc_fn(i), k.ident[:, :])
        CP(k.sl(xs, i * 128, 128), ps[0:P, 0:128], eng=k.act)
    Tt = k.TTp.get()
    B_fm, C_fm = Tt[:, 0:P], Tt[:, 128:128 + P]
    CP(B_fm, xc_fn(4)); CP(C_fm, xc_fn(5), eng=k.act)
    Bt = k.BTp.get()
    B_tm = Bt[0:P, :]
    psb = k.PSB.get()
    c.tr(psb[0:P, 0:128], B_fm, k.identb[:, :])
    CP(B_tm, psb[0:P, 0:128], eng=k.act)
    sm = k.TSm.get()
    dt, logd, b_sb, erev, dtr = sm[0:P, 0:8], sm[0:P, 8:16], sm[0:P, 16:24], sm[0:P, 24:32], sm[0:P, 32:40]
    sm2 = k.TSm.get()
    dS = sm2[:, 0:8]
    TT(dt, sdt_v, k.DTB[l][0:P, h0:h0 + 8], ALU.add)
    ACT(dt, dt, AF.Exp)
    ACT(dt, dt, AF.Ln, bias=k.oneb[0:P, 0:1])
    TT(logd, dt, k.NEGA[l][0:P, h0:h0 + 8], ALU.mult)
    ps1 = PS.get()
    c.mm(ps1[0:P, 0:8], k.causal[0:P, 0:P], logd)
    CP(b_sb, ps1[0:P, 0:8])
    ps2 = PS.get()
    c.mm(ps2[0:P, 0:8], k.trirev[0:P, 0:P], logd)
    ACT(erev, ps2[0:P, 0:8], AF.Exp)
    ps3 = PS.get()
    c.mm(ps3[:, 0:8], k.ones[0:P, 0:128], logd)
    ACT(dS, ps3[:, 0:8], AF.Exp)
    TT(dtr, dt, erev, ALU.mult)

    def bc64(t):
        a = t.ap
        return V(t.buf, bass.AP(a.tensor, a.offset, [list(a.ap[0]), [1, 8], [0, 64]]))

    def r3(t):
        return V(t.buf, t.ap.rearrange("p (h d) -> p h d", h=8))
    vb_ = TBp.get(); vh_ = TBp.get(); yn_ = TBp.get(); Sb_ = TBp.get()
    v_b, vh_b, yn_b = vb_[0:P, :], vh_[0:P, :], yn_[0:P, :]
    TT(r3(v_b), r3(xs), bc64(dt), ALU.mult)
    TT(r3(vh_b), r3(xs), bc64(dtr), ALU.mult)
    Sb = Sb_.v(Sb_.t[:, :].rearrange("p (h d) -> p h d", h=8))
    CP(Sb, Sg, eng=k.act)
    ps_sc = PS.get()
    c.mm(ps_sc[0:P, 0:P], B_fm, C_fm)
    CP(sc_sb, ps_sc[0:P, 0:P])
    for hh in range(8):
        h = h0 + hh
        a = logd.ap
        CP(LBh, V(logd.buf, bass.AP(a.tensor, a.offset + hh, [list(a.ap[0]), [0, 128]])))
        ps_bt = PS.get()
        c.mm(ps_bt[:, 0:P], LBh, k.causal[0:P, 0:P])
        STT(tmp, ps_bt[0:P, 0:P], b_sb[:, hh:hh + 1], k.negm[0:P, 0:P], ALU.subtract, ALU.add)
        TS(tmp, tmp, 0.0, None, ALU.min)
        ACT(tmp, tmp, AF.Exp)
        Pm = k.PMp.get()
        TT(Pm[0:P, 0:P], tmp, sc_sb, ALU.mult)
        ACT(tmp2, ps_bt[:, 0:P], AF.Exp)
        qh = k.QHp.get()
        TT(qh[:, 0:P], xc_fn(5), tmp2, ALU.mult)
        ps_o = PS.get()
        c.mm(ps_o[0:P, 0:64], Pm[0:P, 0:P], k.sl(v_b, hh * 64, 64), start=True, stop=False)
        c.mm(ps_o[0:P, 0:64], qh[:, 0:P], Sb[:, hh, :], start=False, stop=True)
        STT(k.sl(y, hh * 64, 64), k.sl(xs, hh * 64, 64), k.SSD_D[l][0:P, h:h + 1], ps_o[0:P, 0:64], ALU.mult, ALU.add)
    ps_u = PS.get()
    c.mm(ps_u[:, 0:512], B_tm, vh_b)
    a = dS.ap
    TT(Sg, Sg, V(dS.buf, bass.AP(a.tensor, a.offset, [list(a.ap[0]), [1, 8], [0, 64]])), ALU.mult)
    TT(Sg, Sg, ps_u.v(ps_u.t[:, 0:512].rearrange("p (h d) -> p h d", h=8)), ALU.add)
    TT(y, y, zs_v, ALU.mult)
    sm3 = k.TSm.get()
    ACT(k.sl(xs, 0, 512), y, AF.Square, accum_out=sm3[0:P, 0:1])
    rstd_from_ss(k, sm3[0:P, 1:2], sm3[0:P, 0:1], P, 512.0)
    TS(yn_b, y, sm3[0:P, 1:2], None, ALU.mult)
    for i in range(4):
        psb = k.PSB.get()
        c.tr(psb[:, 0:P], k.sl(yn_b, i * 128, 128), k.identb[0:P, 0:P])
        TS(dst_fn(4 * g + i), psb[:, 0:P], k.pv("ssd_norm_w", l * 8 + 4 * g + i), None, ALU.mult)


def mixer_C(k, l, ti, tok0, n, has_s, last, srcs_fm, srcs_tm, oTP, oTS, dense_fm, dense_tm, conv_fm, PB, off_fmb):
    c, di, do = k.c, k.di, k.do
    ACT, CP, TS, PS = k.ACT, k.CP, k.TS, k.PS
    w_in = k.WW("w_in", l)
    nblk = n // 128
    off = off_fmb
    DW, off = k.carve("DWc", off, [128, 9, 256], F32)
    XC, off = k.carve("XC", off, [128, 6, 256], F32)
    XP2, off = k.carve("XP2", off, [128, 2, n + 3], F32)
    if has_s:
        HSC, XSC, XCS = k.C_HSC, k.C_XSC, k.C_XCS
        for i in range(3):
            load_tm2fm(k, di["state_ssd_conv"][l, :, i, :], 1536, lambda j, i=i: HSC[:, j, i, :])
    for g in range(2):
        chunks = [4 * g, 4 * g + 1, 4 * g + 2, 4 * g + 3, 8 + g, 10 + g]
        for slot, ch in enumerate(chunks):
            def sink(gi, col, m, ps, slot=slot, ch=ch):
                if gi == 0:
                    s2 = slot % 2
                    CP(XP2[:, s2, 0:3], k.HI_SSD[:, l, ch, :])
                    CP(XP2[:, s2, 3:n + 3], ps[:, 0:n], eng=k.act)
                    conv_fm(XC[:, slot, 0:n], [XP2[:, s2, i:i + n] for i in range(4)], "ssd_conv_w", "ssd_conv_b", l, 4, ch, 12)
                    CP(k.HI_SSD[:, l, ch, :], XP2[:, s2, n:n + 3])
                    ACT(XC[:, slot, 0:n], XC[:, slot, 0:n], AF.Silu)
                else:
                    CP(XSC[:, ch, :], ps[:, 0:NS], eng=k.act)
                    conv_fm(XCS[:, slot, :], [HSC[:, ch, 0, :], HSC[:, ch, 1, :], HSC[:, ch, 2, :], XSC[:, ch, :]], "ssd_conv_w", "ssd_conv_b", l, 4, ch, 12)
                    ACT(XCS[:, slot, :], XCS[:, slot, :], AF.Silu)
            dense_fm(w_in, 16, O_XBC + 128 * ch, 128, srcs_fm, sink, cw=128)

        def sink_z(gi, tt, P, col0, n_, ps):
            cc = col0 - (O_SZ + 512 * g)
            dst = PB[0:P, tt // 128, cc:cc + n_] if gi == 0 else k.SPB[0:P, cc:cc + n_]
            ACT(dst, ps[0:P, 0:n_], AF.Silu)
        dense_tm(w_in, 16, O_SZ + 512 * g, 512, srcs_tm, sink_z)

        def sink_dt(gi, tt, P, col0, n_, ps):
            dst = PB[0:P, tt // 128, 512:520] if gi == 0 else k.SPB[0:P, 512:520]
            CP(dst, ps[0:P, 0:8], eng=k.act)
        dense_tm(w_in, 16, O_DT + 8 * g, 8, srcs_tm, sink_dt)
        for b in range(nblk):
            block_C(k, l, 128, g, lambda i, b=b: XC[:, i, b * 128:(b + 1) * 128], PB[:, b, 0:512], PB[:, b, 512:520],
                    k.S_C[:, l, 8 * g:8 * g + 8, :], lambda j, b=b: oTP[:, 16 + j, b * 128:(b + 1) * 128], DW)
        if has_s:
            for j in range(NS):
                sp0 = k.SP0p.get()
                c.dma(k.sp, sp0[0:1, 0:520], k.SPB[j:j + 1, 0:520])
                ss = k.SSp.get()
                ssv = ss.v(ss.t[:, :, :].rearrange("p a b -> p (a b)")[:, 0:512].rearrange("p (h d) -> p h d", h=8))
                c.dma(k.sp, ssv, di["state_ssd"][l, j, 8 * g:8 * g + 8].rearrange("h n v -> n h v"))
                block_C(k, l, 1, g, lambda i, j=j: XCS[:, i, j:j + 1], sp0[0:1, 0:512], sp0[0:1, 512:520],
                        ssv, lambda jj, j=j: oTS[:, 16 + jj, j:j + 1], DW)
                c.dma(k.sp, do["ssd_sample"][l, j, 8 * g:8 * g + 8].rearrange("h n v -> n h v"), ssv)
    if has_s:
        store_fm2tm(k, lambda j: XSC[:, j, :], 12, do["ssd_conv_sample"][l, :, 2, :])
        for i in range(2):
            c.dma(k.sp, do["ssd_conv_sample"][l, :, i, :], di["state_ssd_conv"][l, :, i + 1, :])
    if last:
        c.dma(k.sp, do["ssd_prompt"][l].rearrange("h n v -> n h v"), k.S_C[:, l, :, :])
        for i in range(3):
            store_fm2tm(k, lambda jj, i=i: k.HI_SSD[:, l, :, i], 1, do["ssd_conv_prompt"][l, i].rearrange("(j p) -> j p", p=128), P=12)

def block_B(k, l, P, h0, hq_v, hf_v, hi_v, sgh_fn, S_fn, dst_fn, DW):
    c = k.c
    TT, TS, STT, ACT, CP, PS, TBp = k.TT, k.TS, k.STT, k.ACT, k.CP, k.PS, k.TBp
    nch = max(1, P // 32)
    cs = min(32, P)

    def w(i):
        return DW[0:P, i, :]
    cols = slice(h0 * 128, h0 * 128 + 256)
    ACT(w(0), hq_v, AF.Silu)
    ACT(w(1), hf_v, AF.Sigmoid)
    ACT(w(2), hf_v, AF.Sigmoid, scale=-1.0)
    if l == 1:
        TT(w(1), w(1), k.OML1[0:P, cols], ALU.mult)
        TT(w(1), w(1), k.LB1[0:P, cols], ALU.add)
        TT(w(2), w(2), k.OML1[0:P, cols], ALU.mult)
    ACT(w(1), w(1), AF.Ln)
    ps_b = PS.get()
    c.mm(ps_b[0:P, 0:256], k.tri32[0:P, 0:P], w(1))
    ps_r = PS.get()
    c.mm(ps_r[0:P, 0:256], k.rev32[0:P, 0:P], w(1))
    A = TBp.get(); B = TBp.get()
    qt_b, kt_b = A[0:P, 0:256], A[0:P, 256:512]
    v_b, khc = B[0:P, 0:256], B[0:P, 256:384]
    ACT(w(4), ps_b[0:P, 0:256], AF.Exp)
    TT(qt_b, w(0), w(4), ALU.mult)
    ACT(w(4), ps_b[0:P, 0:256], AF.Exp, scale=-1.0)
    TT(kt_b, w(2), w(4), ALU.mult)
    ACT(w(4), ps_r[0:P, 0:256], AF.Exp)
    TT(w(3), w(2), w(4), ALU.mult)
    CP(v_b, hi_v, eng=k.act)
    for hh in range(2):
        h = h0 + hh
        hs = slice(hh * 128, (hh + 1) * 128)
        S = S_fn(hh)
        ps_d = PS.get()
        c.mm(ps_d[:, 0:nch], w(1)[:, hs], k.ind[0:P, 0:nch])
        sm = k.TSm.get()
        ACT(sm[:, 0:nch], ps_d[:, 0:nch], AF.Exp)
        Tt = k.TTp.get()
        qT, kT = Tt[:, 0:P], Tt[:, 128:128 + P]
        tr_bf(k, qT, qt_b[:, hs], P)
        tr_bf(k, kT, kt_b[:, hs], P)
        Sb = k.SB4p.get()
        for cc in range(nch):
            CP(Sb[:, cc * 128:(cc + 1) * 128], S, eng=k.act)
            TS(khc, w(3)[:, hs], k.ind[0:P, cc:cc + 1], None, ALU.mult)
            ps_u = PS.get()
            c.mm(ps_u[:, 0:128], khc, v_b[:, hs])
            STT(S, S, sm[:, cc:cc + 1], ps_u[:, 0:128], ALU.mult, ALU.add)
        ps_sc = PS.get()
        c.mm(ps_sc[0:P, 0:P], kT, qT)
        Pm = k.PMp.get()
        TT(Pm[0:P, 0:P], ps_sc[0:P, 0:P], k.tri32[0:P, 0:P], ALU.mult)
        ps_oi = PS.get()
        c.mm(ps_oi[:, 0:P], v_b[:, hs], Pm[0:P, 0:P])
        ps_oc = PS.get()
        for cc in range(nch):
            c.mm(ps_oc[:, cc * 32:cc * 32 + cs], Sb[:, cc * 128:(cc + 1) * 128], qT[:, cc * 32:cc * 32 + cs])
        o = DW[:, 5, 0:P]
        sq = DW[:, 6, 0:P]
        rs = DW[:, 7, 0:P]
        CP(o, ps_oi[:, 0:P], eng=k.act)
        TT(o, o, ps_oc[:, 0:P], ALU.add)
        ACT(sq, o, AF.Square)
        ps_ss = PS.get()
        c.mm(ps_ss[:, 0:P], k.ones[:, :], sq)
        ACT(rs, ps_ss[:, 0:P], AF.Ln, scale=1.0 / 128.0, bias=k.epsb[:, 0:1])
        ACT(rs, rs, AF.Exp, scale=-0.5)
        STT(sq, o, k.pv("hg_norm_w", l), rs, ALU.mult, ALU.mult)
        TT(dst_fn(h), sq, sgh_fn(hh), ALU.mult)


def mixer_B(k, l, ti, tok0, n, has_s, last, srcs_fm, srcs_tm, oTP, oTS, dense_fm, dense_tm, PBs, DW):
    c, di, do = k.c, k.di, k.do
    ACT, CP = k.ACT, k.CP
    w_in = k.WW("w_in", l)
    nblk = n // 128

    def proj(hp):
        PB = PBs[hp % 2]
        for wi, o_ in enumerate([O_HQ + 256 * hp, O_HF + 256 * hp, O_HI + 256 * hp]):
            def sink(gi, tt, P, col0, n_, ps, wi=wi):
                dst = PB[0:P, tt // 128, wi * 256:(wi + 1) * 256] if gi == 0 else k.SPB[0:P, wi * 256:(wi + 1) * 256]
                CP(dst, ps[0:P, 0:256], eng=k.act)
            dense_tm(w_in, 16, o_, 256, srcs_tm, sink)

        def sink_g(gi, col, m, ps):
            hh = (col - (O_HG + 256 * hp)) // 128
            if gi == 0:
                ACT(PB[:, hh, 768:768 + n], ps[:, 0:n], AF.Silu)
            else:
                ACT(k.B_SGS[:, hp % 2, hh, :], ps[:, 0:NS], AF.Silu)
        dense_fm(w_in, 16, O_HG + 256 * hp, 256, srcs_fm, sink_g)

    def blocks(hp):
        PB = PBs[hp % 2]
        h0 = 2 * hp
        for b in range(nblk):
            block_B(k, l, 128, h0, PB[:, b, 0:256], PB[:, b, 256:512], PB[:, b, 512:768],
                    lambda hh, b=b: PB[:, hh, 768 + b * 128:768 + (b + 1) * 128],
                    lambda hh: k.S_B[:, l, h0 + hh, :],
                    lambda h, b=b: oTP[:, 8 + h, b * 128:(b + 1) * 128], DW)
        if has_s:
            for j in range(NS):
                sp0 = k.SP0p.get()
                c.dma(k.sp, sp0[0:1, 0:768], k.SPB[j:j + 1, 0:768])
                ss = k.SSp.get()
                c.dma(k.sp, ss[:, 0:2, :], di["state_hgrn"][l, j, h0:h0 + 2].rearrange("h k v -> k h v"))
                block_B(k, l, 1, h0, sp0[0:1, 0:256], sp0[0:1, 256:512], sp0[0:1, 512:768],
                        lambda hh, j=j: k.B_SGS[:, hp % 2, hh, j:j + 1],
                        lambda hh, ss=ss: ss[:, hh, :],
                        lambda h, j=j: oTS[:, 8 + h, j:j + 1], DW)
                c.dma(k.sp, do["hgrn_sample"][l, j, h0:h0 + 2].rearrange("h k v -> k h v"), ss[:, 0:2, :])
    if has_s:
        for hp in range(4):
            proj(hp)
            blocks(hp)
    else:
        proj(0)
        for hp in range(4):
            if hp + 1 < 4:
                proj(hp + 1)
            blocks(hp)
    if last:
        c.dma(k.sp, do["hgrn_prompt"][l].rearrange("h k v -> k h v"), k.S_B[:, l, :, :])


def _shard_inputs(inputs, ci):
    b = ci % 4
    s0 = ci * NS
    m = {}
    for name, a in inputs.items():
        a = np.asarray(a)
        if name == "x_prompt":
            m[name] = np.ascontiguousarray(a[b])
        elif name == "x_sample":
            m[name] = np.ascontiguousarray(a[s0:s0 + NS, 0, :])
        elif name.startswith("state_"):
            m[name] = np.ascontiguousarray(a[:, s0:s0 + NS])
        else:
            m[name] = np.ascontiguousarray(a)
    return m


_NC_CACHE = {}


def kernel(**inputs):
    T = int(np.asarray(inputs["x_prompt"]).shape[1])
    B = int(np.asarray(inputs["x_prompt"]).shape[0])
    if T not in _NC_CACHE:
        _NC_CACHE[T] = build(T)
    nc = _NC_CACHE[T]
    in_maps = [_shard_inputs(inputs, ci) for ci in range(8)]
    res = run_bass_kernel_spmd(nc, in_maps, core_ids=list(range(8)))
    R = res.results
    f = np.float32
    y_prompt = np.stack([R[b]["y_prompt"] for b in range(B)], 0).astype(f)
    y_sample = np.concatenate([R[ci]["y_sample"] for ci in range(8)], 0)[:, None, :].astype(f)
    outs = [y_prompt, y_sample]
    for nm in ["lru_h", "lru_conv", "hgrn", "ssd", "ssd_conv", "ret", "ffn_conv"]:
        outs.append(np.stack([R[b][nm + "_prompt"] for b in range(B)], 1).astype(f))
        outs.append(np.concatenate([R[ci][nm + "_sample"] for ci in range(8)], 1).astype(f))
    return tuple(outs)
```

```python
import math
from contextlib import ExitStack
from concourse.bass_utils import run_bass_kernel_spmd
import numpy as np
import concourse.bass as bass
import concourse.mybir as mybir

F32 = mybir.dt.float32
BF16 = mybir.dt.bfloat16
I32 = mybir.dt.int32
AF = mybir.ActivationFunctionType
ALU = mybir.AluOpType
AX = mybir.AxisListType


class V:
    __slots__ = ("buf", "ap")

    def __init__(self, buf, ap):
        self.buf = buf
        self.ap = ap

    def __getitem__(self, idx):
        return V(self.buf, self.ap[idx])


class Buf:
    def __init__(self, ctx, tensor, name):
        self.ctx = ctx
        self.t = tensor
        self.name = name
        self.w = None
        self.r = []
        self.dsem = None
        self.dcnt = 0

    def __getitem__(self, idx):
        return V(self, self.t[idx])

    def v(self, ap):
        return V(self, ap)


class Eng:
    def __init__(self, ctx, e, name):
        self.ctx = ctx
        self.e = e
        self.name = name
        self.sem = ctx.new_sem("c_" + name)
        self.cnt = 0
        self.waited = {}
        self.pend_r = []
        self.pend_w = []
        self.old = []

    def need(self, tok):
        if tok is None:
            return
        sem, val = tok
        k = id(sem)
        if self.waited.get(k, 0) >= val:
            return
        self.e.wait_ge(sem, val)
        self.waited[k] = val

    def deps(self, reads, writes):
        for b in reads:
            self.need(b.w)
        for b in writes:
            self.need(b.w)
            for t in b.r:
                self.need(t)

    def done(self, inst, reads, writes, inc=True):
        self.pend_r += reads
        self.pend_w += writes
        if inc:
            if self.cnt >= 30000:
                self.old.append((self.sem, self.cnt))
                self.sem = self.ctx.new_sem("c_" + self.name + str(self.ctx.nsem))
                self.cnt = 0
            inst.then_inc(self.sem, 1)
            self.cnt += 1
            tok = (self.sem, self.cnt)
            for b in self.pend_w:
                b.w = tok
                b.r = []
            for b in self.pend_r:
                if b.w is not tok:
                    b.r.append(tok)
                    if len(b.r) > 6:
                        b.r = b.r[-6:] if False else b.r
            self.pend_r = []
            self.pend_w = []


def _bufs(views):
    out = []
    for v in views:
        if isinstance(v, V) and v.buf is not None and v.buf not in out:
            out.append(v.buf)
    return out


def _ap(x):
    return x.ap if isinstance(x, V) else x


class Ctx:
    def __init__(self, nc, stack):
        self.nc = nc
        self.stack = stack
        self.nsem = 0
        self.pe = Eng(self, nc.tensor, "pe")
        self.dve = Eng(self, nc.vector, "dve")
        self.act = Eng(self, nc.scalar, "act")
        self.pool = Eng(self, nc.gpsimd, "pool")
        self.sp = Eng(self, nc.sync, "sp")
        self.dma_bufs = []
        self.drambuf = Buf(self, None, "dram")
        self.uid = 0

    def new_sem(self, name):
        self.nsem += 1
        return self.stack.enter_context(self.nc.semaphore(name))

    def sbuf(self, name, shape, dt=F32):
        t = self.stack.enter_context(self.nc.sbuf_tensor(name, list(shape), dt))
        return Buf(self, t, name)

    def psum(self, name, shape, dt=F32):
        t = self.stack.enter_context(self.nc.psum_tensor(name, list(shape), dt))
        return Buf(self, t, name)

    def op(self, eng, fn, out, ins, *args, **kw):
        extra_r = kw.pop("_reads", [])
        rb = _bufs(list(ins) + list(extra_r) + [v for v in kw.values() if isinstance(v, V)])
        wb = _bufs([out] + ([kw["accum_out"]] if "accum_out" in kw else []))
        eng.deps(rb, wb)
        kw2 = {k: _ap(v) for k, v in kw.items()}
        inst = getattr(eng.e, fn)(_ap(out), *[_ap(i) for i in ins], *args, **kw2)
        eng.done(inst, rb, wb)
        return inst

    def mm(self, out, lhsT, rhs, start=True, stop=True, inc=None, **kw):
        pe = self.pe
        rb = _bufs([lhsT, rhs])
        wb = _bufs([out])
        pe.deps(rb, wb if start else [])
        inst = pe.e.matmul(_ap(out), _ap(lhsT), _ap(rhs), start=start, stop=stop, **kw)
        pe.done(inst, rb, wb, inc=(stop if inc is None else inc))
        return inst

    def tr(self, out, in_, ident):
        pe = self.pe
        rb = _bufs([in_, ident])
        wb = _bufs([out])
        pe.deps(rb, wb)
        inst = pe.e.transpose(_ap(out), _ap(in_), _ap(ident))
        pe.done(inst, rb, wb, inc=True)
        return inst

    def dma(self, q, out, in_, **kw):
        rb = _bufs([in_])
        wb = _bufs([out])
        q.deps(rb, wb)
        b = (wb + rb)[0] if (wb + rb) else self.drambuf
        if b.dsem is None:
            b.dsem = self.new_sem("d_" + b.name)
            self.dma_bufs.append(b)
        inst = q.e.dma_start(out=_ap(out), in_=_ap(in_), **kw)
        inst.then_inc(b.dsem, 16)
        b.dcnt += 16
        tok = (b.dsem, b.dcnt)
        for x in wb:
            x.w = tok
            x.r = []
        for x in rb:
            x.r.append(tok)
        return inst

    def finish(self):
        for b in self.dma_bufs:
            self.sp.need((b.dsem, b.dcnt))


class Pool:
    def __init__(self, ctx, name, n, shape, dt=F32, psum=False):
        self.bufs = [(ctx.psum if psum else ctx.sbuf)(f"{name}{i}", shape, dt) for i in range(n)]
        self.i = 0

    def get(self):
        b = self.bufs[self.i % len(self.bufs)]
        self.i += 1
        return b


def bc(v, shape_ap):
    a = _ap(v)
    ap = bass.AP(a.tensor, a.offset, [list(a.ap[0])] + [list(x) for x in shape_ap])
    return V(v.buf, ap) if isinstance(v, V) else ap

D = 2048
KC = 16
DEPTH = 2
NS = 16
LRU_W = 1024
DFF = 5632
FC = 44
N_IN = 12816
EPS = 1e-6
O_XA, O_YA = 0, 1024
O_HQ, O_HF, O_HI, O_HG = 2048, 3072, 4096, 5120
O_SZ, O_XBC, O_DT = 6144, 7168, 8704
O_RQ, O_RK, O_RV, O_RG = 8720, 9744, 10768, 11792
LOG_GAMMA = [math.log1p(-2.0 ** (-5.0 - h)) for h in range(8)]


class K:
    pass


_PLANS = {}


def build(T):
    if T not in _PLANS:
        rec = []
        _build(T, None, rec)
        seen = {}
        off = [0, 0]
        for key in rec:
            if key not in seen:
                seen[key] = off[key[1]]
                off[key[1]] += 128 * key[4] * key[6]
        _PLANS[T] = (seen, off)
    return _build(T, _PLANS[T], None)


def _build(T, plan, rec):
    NT = T // 512
    nc = bass.Bass("TRN2", target_bir_lowering=False)
    di = {}
    do = {}

    def din(name, shape):
        di[name] = nc.dram_tensor(name, list(shape), F32, kind="ExternalInput").ap()

    def dout(name, shape):
        do[name] = nc.dram_tensor(name, list(shape), F32, kind="ExternalOutput").ap()

    din("x_prompt", [T, D]); din("x_sample", [NS, D])
    din("state_lru_h", [2, NS, 1024]); din("state_lru_conv", [2, NS, 3, 1024])
    din("state_hgrn", [2, NS, 8, 128, 128]); din("state_ssd", [2, NS, 16, 128, 64])
    din("state_ssd_conv", [2, NS, 3, 1536]); din("state_ret", [2, NS, 8, 128, 128])
    din("state_ffn_conv", [2, NS, 2, DFF])
    din("g_mix", [2, D]); din("g_ffn", [2, D]); din("w_in", [2, D, N_IN])
    din("lru_conv_w", [2, 4, 1024]); din("lru_conv_b", [2, 1024]); din("lru_wa", [2, 8, 128, 128])
    din("lru_ba", [2, 8, 128]); din("lru_wx", [2, 8, 128, 128]); din("lru_bx", [2, 8, 128])
    din("lru_lambda", [2, 1024]); din("hg_lb_logits", [2, 1024]); din("hg_norm_w", [2, 128])
    din("ssd_conv_w", [2, 4, 1536]); din("ssd_conv_b", [2, 1536]); din("ssd_dt_bias", [2, 16])
    din("ssd_a_log", [2, 16]); din("ssd_d", [2, 16]); din("ssd_norm_w", [2, 1024])
    din("w_branch", [2, 4, 1024, D]); din("w_gate", [2, D, 4, D]); din("w_out", [2, D, D])
    din("ffn_w_up", [2, D, DFF]); din("ffn_w_val", [2, D, DFF]); din("ffn_conv_w", [2, 3, DFF])
    din("ffn_conv_b", [2, DFF]); din("ffn_w_down", [2, DFF, D]); din("g_final", [D])
    dout("y_prompt", [T, D]); dout("y_sample", [NS, D])
    dout("lru_h_prompt", [2, 1024]); dout("lru_h_sample", [2, NS, 1024])
    dout("lru_conv_prompt", [2, 3, 1024]); dout("lru_conv_sample", [2, NS, 3, 1024])
    dout("hgrn_prompt", [2, 8, 128, 128]); dout("hgrn_sample", [2, NS, 8, 128, 128])
    dout("ssd_prompt", [2, 16, 128, 64]); dout("ssd_sample", [2, NS, 16, 128, 64])
    dout("ssd_conv_prompt", [2, 3, 1536]); dout("ssd_conv_sample", [2, NS, 3, 1536])
    dout("ret_prompt", [2, 8, 128, 128]); dout("ret_sample", [2, NS, 8, 128, 128])
    dout("ffn_conv_prompt", [2, 2, DFF]); dout("ffn_conv_sample", [2, NS, 2, DFF])

    bfw = [nc.dram_tensor(f"bfw{l}", [plan[1][l] if plan else 128], BF16, kind="Internal").ap() for l in range(2)]
    with ExitStack() as st:
        c = Ctx(nc, st)
        c.plan = plan
        c.rec = rec
        c.bfw = bfw
        _program(c, nc, di, do, T, NT)
        c.finish()
    return nc

def _program(c, nc, di, do, T, NT):
    dve, act, pool, pe, sp = c.dve, c.act, c.pool, c.pe, c.sp
    PI = math.pi
    NB = T // 128

    def OP(eng, fn, out, ins, *a, **k):
        return c.op(eng, fn, out, ins, *a, **k)

    def TT(out, a, b, op, eng=None):
        return c.op(eng or dve, "tensor_tensor", out, [a, b], op)

    def TS(out, a, s1, s2, op0, op1=ALU.bypass, eng=None):
        return c.op(eng or dve, "tensor_scalar", out, [a, s1, s2], op0, op1)

    def STT(out, a, s, b, op0, op1):
        return c.op(dve, "scalar_tensor_tensor", out, [a, s, b], op0, op1)

    def ACT(out, a, f, **k):
        return c.op(act, "activation", out, [a], f, **k)

    def CP(out, a, eng=None):
        e = eng or dve
        if e is act:
            return c.op(act, "activation", out, [a], AF.Copy)
        return c.op(e, "tensor_copy", out, [a])

    def MS(buf_view, val, eng=None):
        return c.op(eng or dve, "memset", buf_view, [], val)

    def ASEL(out, in_, pattern, cmp, base, cm):
        return c.op(pool, "affine_select", out, [in_], pattern=pattern, compare_op=cmp, fill=0.0,
                    base=base, channel_multiplier=cm)

    def wsrc(nm, l, sub):
        a_ = di[nm][l]
        if nm == "w_branch":
            a_ = a_[sub]
        elif nm == "w_gate":
            a_ = a_[:, sub, :]
        return a_
    CV = {}
    if c.plan is not None:
        for key, off_ in c.plan[0].items():
            nm, l_, sub, k0, kc, col0, ncols = key
            if (nm, l_) not in CV:
                CV[(nm, l_)] = Buf(c, None, f"cv_{nm}{l_}")
            src = wsrc(nm, l_, sub)[k0 * 128:(k0 + kc) * 128, col0:col0 + ncols].rearrange("(k p) n -> p k n", p=128)
            dst = c.bfw[l_][off_:off_ + 128 * kc * ncols].rearrange("(p k n) -> p k n", p=128, k=kc)
            c.dma(pool, V(CV[(nm, l_)], dst), src)
    PS = Pool(c, "ps", 6, [128, 512], F32, psum=True)
    PSB = Pool(c, "psb", 2, [128, 1024], BF16, psum=True)
    WP = Pool(c, "wbuf", 4, [128, 4096], BF16)
    TA = Pool(c, "ta", 5, [128, 512], F32)
    TBp = Pool(c, "tb", 4, [128, 512], BF16)
    TSm = Pool(c, "tsm", 8, [128, 64], F32)

    ones = c.sbuf("ones", [128, 128], F32); MS(ones[:], 1.0)
    onesb = c.sbuf("onesb", [128, 128], BF16); MS(onesb[:], 1.0)
    ident = c.sbuf("ident", [128, 128], F32)
    ASEL(ident[:], ones[:], [[-1, 128]], ALU.is_equal, 0, 1)
    identb = c.sbuf("identb", [128, 128], BF16); CP(identb[:], ident[:])
    causal = c.sbuf("causal", [128, 128], F32)
    ASEL(causal[:], ones[:], [[1, 128]], ALU.is_ge, 0, -1)
    trirev = c.sbuf("trirev", [128, 128], F32)
    ASEL(trirev[:], ones[:], [[-1, 128]], ALU.is_gt, 0, 1)
    same = c.sbuf("same", [128, 4, 32], F32)
    tmpc = c.sbuf("tmpc", [128, 4, 32], F32)
    ASEL(tmpc[:], bc(ones[:, 0:1], [[0, 4], [0, 32]]), [[-32, 4], [0, 32]], ALU.is_ge, 0, 1)
    ASEL(same[:], tmpc[:], [[32, 4], [0, 32]], ALU.is_ge, 31, -1)
    samef = same.v(same.t[:].rearrange("p c j -> p (c j)"))
    tri32 = c.sbuf("tri32", [128, 128], F32); TT(tri32[:], causal[:], samef, ALU.mult)
    rev32 = c.sbuf("rev32", [128, 128], F32); TT(rev32[:], trirev[:], samef, ALU.mult)
    mbd = tri32
    ind = c.sbuf("ind", [128, 4], F32); CP(ind[:], same[:, :, 0])
    negm = c.sbuf("negm", [128, 128], F32)
    TS(negm[:], causal[:], 30000.0, -30000.0, ALU.mult, ALU.add)
    dti = c.sbuf("dti", [128, 128], I32)
    c.op(pool, "iota", dti[:], [], pattern=[[1, 128]], base=0, channel_multiplier=-1)
    dtf = c.sbuf("dtf", [128, 128], F32); CP(dtf[:], dti[:])
    GM = c.sbuf("GM", [128, 8, 128], F32)
    pidx_i = c.sbuf("pidx_i", [128, 1], I32)
    c.op(pool, "iota", pidx_i[:], [], pattern=[[0, 1]], base=0, channel_multiplier=1)
    pidx = c.sbuf("pidx", [128, 1], F32); CP(pidx[:], pidx_i[:])
    pp1 = c.sbuf("pp1", [128, 1], F32); TS(pp1[:], pidx[:], 1.0, None, ALU.add)
    prv = c.sbuf("prv", [128, 1], F32); TS(prv[:], pidx[:], -1.0, 127.0, ALU.mult, ALU.add)
    gpow = c.sbuf("gpow", [128, 8], F32)
    grev = c.sbuf("grev", [128, 8], F32)
    for h in range(8):
        ACT(GM[:, h, :], dtf[:], AF.Exp, scale=LOG_GAMMA[h])
        TT(GM[:, h, :], GM[:, h, :], causal[:], ALU.mult)
        ACT(gpow[:, h:h + 1], pp1[:], AF.Exp, scale=LOG_GAMMA[h])
        ACT(grev[:, h:h + 1], prv[:], AF.Exp, scale=LOG_GAMMA[h])
    fi = c.sbuf("fi", [128, 64], I32)
    c.op(pool, "iota", fi[:], [], pattern=[[1, 64]], base=0, channel_multiplier=0)
    FR = c.sbuf("FR", [128, 64], F32); CP(FR[:], fi[:])
    ACT(FR[:], FR[:], AF.Exp, scale=-math.log(10000.0) / 64.0)
    RB = 2
    cosT = c.sbuf("cosT", [128, RB, 64], F32)
    sinT = c.sbuf("sinT", [128, RB, 64], F32)
    angb = c.sbuf("angb", [128, RB, 64], F32)
    ang2 = c.sbuf("ang2", [128, RB, 64], F32)
    angi = c.sbuf("angi", [128, RB, 64], I32)
    angm = c.sbuf("angm", [128, RB, 64], F32)
    posf = c.sbuf("posf", [128, RB], F32)

    def sin_of(out, src, P, nb):
        a2 = ang2[0:P, 0:nb, :]; ai = angi[0:P, 0:nb, :]; am = angm[0:P, 0:nb, :]
        TS(a2, src, 1.0 / (2 * PI), None, ALU.mult)
        CP(ai, a2)
        CP(a2, ai)
        STT(a2, a2, -2 * PI, src, ALU.mult, ALU.add)
        TS(am, a2, PI, None, ALU.is_gt)
        STT(a2, am, -2 * PI, a2, ALU.mult, ALU.add)
        TS(am, a2, -PI, None, ALU.is_lt)
        STT(a2, am, 2 * PI, a2, ALU.mult, ALU.add)
        TS(a2, a2, PI, -PI, ALU.min, ALU.max)
        ACT(out, a2, AF.Sin)

    def make_rope(tok0, nb):
        for b_ in range(nb):
            TS(posf[:, b_:b_ + 1], pidx[:, 0:1], float(tok0 + 128 * b_), None, ALU.add)
        TT(angb[:, 0:nb, :], bc(posf[:, 0:1], [[1, nb], [0, 64]]), bc(FR[:, 0:1], [[0, nb], [1, 64]]), ALU.mult)
        sin_of(sinT[:, 0:nb, :], angb[:, 0:nb, :], 128, nb)
        TS(angb[:, 0:nb, :], angb[:, 0:nb, :], PI / 2, None, ALU.add)
        sin_of(cosT[:, 0:nb, :], angb[:, 0:nb, :], 128, nb)

    cosS = c.sbuf("cosS", [1, 1, 64], F32)
    sinS = c.sbuf("sinS", [1, 1, 64], F32)
    TS(angb[0:1, 0, :], FR[0:1, :], 16384.0, None, ALU.mult)
    sin_of(sinS[:, :, :], angb[0:1, 0:1, :], 1, 1)
    TS(angb[0:1, 0:1, :], angb[0:1, 0:1, :], PI / 2, None, ALU.add)
    sin_of(cosS[:, :, :], angb[0:1, 0:1, :], 1, 1)
    TTp = Pool(c, "ttp", 2, [128, 384], BF16)
    PMp = Pool(c, "pmp", 2, [128, 128], BF16)
    SBFp = Pool(c, "sbfp", 2, [128, 128], BF16)
    SP0p = Pool(c, "sp0p", 2, [1, 1024], F32)
    SSp = Pool(c, "ssp", 2, [128, 4, 128], F32)
    BTp = Pool(c, "btp", 2, [128, 128], BF16)
    SB4p = Pool(c, "sb4p", 2, [128, 512], BF16)
    QHp = Pool(c, "qhp", 2, [128, 128], BF16)
    C_HSC = c.sbuf("C_HSC", [128, 12, 3, NS], F32)
    C_XSC = c.sbuf("C_XSC", [128, 12, NS], F32)
    C_XCS = c.sbuf("C_XCS", [128, 6, NS], F32)
    B_SGS = Buf(c, C_XCS.t[:, 0:4, :].rearrange("p (a b) n -> p a b n", a=2), "B_SGS")
    SPB = c.sbuf("SPB", [NS, 1024], F32)

    plist = [("g_mix", di["g_mix"].rearrange("l (c p) -> (l c) p", p=128)),
             ("g_ffn", di["g_ffn"].rearrange("l (c p) -> (l c) p", p=128)),
             ("g_final", di["g_final"].rearrange("(c p) -> c p", p=128)),
             ("lru_conv_w", di["lru_conv_w"].rearrange("l i (j p) -> (l i j) p", p=128)),
             ("lru_conv_b", di["lru_conv_b"].rearrange("l (j p) -> (l j) p", p=128)),
             ("lru_ba", di["lru_ba"].rearrange("l j p -> (l j) p")),
             ("lru_bx", di["lru_bx"].rearrange("l j p -> (l j) p")),
             ("lru_lambda", di["lru_lambda"].rearrange("l (j p) -> (l j) p", p=128)),
             ("hg_norm_w", di["hg_norm_w"]),
             ("ssd_conv_w", di["ssd_conv_w"].rearrange("l i (j p) -> (l i j) p", p=128)),
             ("ssd_conv_b", di["ssd_conv_b"].rearrange("l (j p) -> (l j) p", p=128)),
             ("ffn_conv_w", di["ffn_conv_w"].rearrange("l i (j p) -> (l i j) p", p=128)),
             ("ffn_conv_b", di["ffn_conv_b"].rearrange("l (j p) -> (l j) p", p=128)),
             ("ssd_norm_w", di["ssd_norm_w"].rearrange("l (j p) -> (l j) p", p=128))]
    tot = sum(a.shape[0] for _, a in plist)
    ntile = (tot + 127) // 128
    PVT = c.sbuf("PVT", [128, ntile * 128], F32)
    prow = c.sbuf("prow", [128, 128], F32)
    poff = {}
    g = 0
    segs = []
    for name, a in plist:
        poff[name] = g
        R = a.shape[0]
        r = 0
        while r < R:
            ti, ro = divmod(g + r, 128)
            m = min(R - r, 128 - ro)
            segs.append((ti, ro, a[r:r + m, :], m))
            r += m
        g += R
    for ti in range(ntile):
        MS(prow[:], 0.0)
        for (t2, ro, ap_, m) in segs:
            if t2 == ti:
                c.dma(sp, prow[ro:ro + m, :], ap_)
        pst = PS.get()
        c.tr(pst[:, 0:128], prow[:], ident[:])
        CP(PVT[:, ti * 128:(ti + 1) * 128], pst[:, 0:128])

    def pv(name, idx, n=1):
        o = poff[name] + idx
        return PVT[:, o:o + n]

    def bload(name, src2d_row, ncol):
        b = c.sbuf(name, [128, ncol], F32)
        a = src2d_row
        c.dma(sp, b[:], bass.AP(a.tensor, a.offset, [[0, 128], [1, ncol]]))
        return b

    LB1 = bload("lb1", di["hg_lb_logits"][1, :], 1024)
    OML1 = bload("oml1", di["hg_lb_logits"][0, :], 1024)
    TT(OML1[:], LB1[:], OML1[:], ALU.subtract)
    ACT(LB1[:], OML1[:], AF.Sigmoid)
    ACT(OML1[:], OML1[:], AF.Sigmoid, scale=-1.0)
    DTB = [bload(f"dtb{l}", di["ssd_dt_bias"][l, :], 16) for l in range(2)]
    NEGA = [bload(f"nega{l}", di["ssd_a_log"][l, :], 16) for l in range(2)]
    SSD_D = [bload(f"ssdd{l}", di["ssd_d"][l, :], 16) for l in range(2)]
    for l in range(2):
        ACT(NEGA[l][:], NEGA[l][:], AF.Exp)
        TS(NEGA[l][:], NEGA[l][:], -1.0, None, ALU.mult)
    LCC = c.sbuf("lcc", [128, 16], F32)
    LCC2 = c.sbuf("lcc2", [128, 16], F32)
    ACT(LCC[:], pv("lru_lambda", 0, 16), AF.Exp, scale=-1.0)
    ACT(LCC[:], LCC[:], AF.Ln, bias=1.0)
    TS(LCC2[:], LCC[:], -16.0, None, ALU.mult)
    TS(LCC[:], LCC[:], -8.0, None, ALU.mult)
    LWA = c.sbuf("lwa", [128, 2, 8, 128], BF16)
    LWX = c.sbuf("lwx", [128, 2, 8, 128], BF16)
    c.dma(pool, LWA[:], di["lru_wa"].rearrange("l n c d -> c l n d"))
    c.dma(pool, LWX[:], di["lru_wx"].rearrange("l n c d -> c l n d"))

    def zbuf(name, shape, dt=F32):
        b = c.sbuf(name, shape, dt)
        MS(b[:], 0.0)
        return b
    H_LRU = zbuf("H_LRU", [128, 2, 8])
    HI_LRU = zbuf("HI_LRU", [128, 2, 8, 3])
    HI_SSD = zbuf("HI_SSD", [128, 2, 12, 3])
    HI_FFN = zbuf("HI_FFN", [128, 2, 44, 2])
    S_B = zbuf("S_B", [128, 2, 8, 128])
    S_C = zbuf("S_C", [128, 2, 16, 64])
    S_D = zbuf("S_D", [128, 2, 8, 128])

    TILE = 256 if T > 256 else T
    xP = c.sbuf("xP", [128, 16, TILE], F32)
    xS = c.sbuf("xS", [128, 16, NS], F32)
    hP = c.sbuf("hP", [128, 16, TILE], BF16)
    hS = c.sbuf("hS", [128, 16, NS], BF16)
    NTK = TILE + NS
    UBN = max(32 * NTK + 2 * 2 * 1024 + max(2 * 6 * (TILE + 3), 2 * 9 * 256 + 2 * 6 * 256 + 4 * (TILE + 3)) + 64, 44 * NTK + 2 * 2 * (TILE + 2) + 2 * 44 * 3 * NS)
    UB = c.sbuf("UB", [128, UBN], BF16)

    def carve(name, off, shape, dt):
        sz = 2 if dt in (F32, I32) else 1
        n = int(np.prod(shape[1:])) * sz
        a = UB.t[0:shape[0], off:off + n]
        if sz == 2:
            a = a.bitcast(dt)
        if len(shape) == 3:
            a = a.rearrange("p (a b) -> p a b", a=shape[1])
        elif len(shape) == 4:
            a = a.rearrange("p (a b c) -> p a b c", a=shape[1], b=shape[2])
        return Buf(c, a, name), off + n

    def barrier():
        engs = [c.pe, c.dve, c.act, c.pool, c.sp]
        for e in engs[:4]:
            for f in engs:
                if f is not e and f.cnt > 0:
                    e.need((f.sem, f.cnt))
                if f is not e:
                    for tok in f.old:
                        e.need(tok)
            for b in c.dma_bufs:
                e.need((b.dsem, b.dcnt))

    oneb = c.sbuf("oneb", [128, 1], F32); MS(oneb[:], 1.0)
    MACC = c.sbuf("MACC", [128, 2, TILE + NS], F32)
    RSB = c.sbuf("RSB", [128, 512], F32)
    LS_HS = c.sbuf("LS_HS", [128, 8, 3, NS], F32)
    LS_H0 = c.sbuf("LS_H0", [128, 8, NS], F32)
    LS_XA = c.sbuf("LS_XA", [128, 8, NS], F32)
    LS_HN = c.sbuf("LS_HN", [128, 8, NS], F32)
    LS_GY = c.sbuf("LS_GY", [128, 8, NS], F32)
    sl = lambda v, a, n_: V(v.buf, v.ap[:, a:a + n_])
    K_ = K()
    K_.__dict__.update(locals())
    return _program2(K_)

def _program2(k):
    c, nc, di, do, T = k.c, k.nc, k.di, k.do, k.T
    dve, act, pool, pe, sp = k.dve, k.act, k.pool, k.pe, k.sp
    TT, TS, STT, ACT, CP, MS = k.TT, k.TS, k.STT, k.ACT, k.CP, k.MS
    PS, PSB, WP, TA, TBp, TSm = k.PS, k.PSB, k.WP, k.TA, k.TBp, k.TSm
    ones, ident, identb, pv = k.ones, k.ident, k.identb, k.pv
    xP, xS, hP, hS, TILE, carve, barrier = k.xP, k.xS, k.hP, k.hS, k.TILE, k.carve, k.barrier
    tiles = []
    t0 = 0
    while t0 < T:
        n = min(TILE, T - t0)
        tiles.append((t0, n))
        t0 += n

    def wchunk(W, k0, kc, col0, ncols, q=None):
        nm, l_, sub = W
        key = (nm, l_, sub, k0, kc, col0, ncols)
        wb = WP.get()
        v = wb.t[:, 0:kc * ncols].rearrange("p (k n) -> p k n", k=kc)
        if c.plan is None:
            c.rec.append(key)
            src = k.wsrc(nm, l_, sub)[k0 * 128:(k0 + kc) * 128, col0:col0 + ncols].rearrange("(k p) n -> p k n", p=128)
            c.dma(pool, wb.v(v), src)
        else:
            off_ = c.plan[0][key]
            src = c.bfw[l_][off_:off_ + 128 * kc * ncols].rearrange("(p m) -> p m", p=128)
            c.dma(sp, wb[:, 0:kc * ncols], V(k.CV[(nm, l_)], src))
        return wb, v

    def WW(nm, l, sub=None):
        return (nm, l, sub)

    def dense_fm(w2d, kc, c0, ncols, srcs, sink, cw=256):
        for col0 in range(c0, c0 + ncols, cw):
            n_ = min(cw, c0 + ncols - col0)
            wb, wv = wchunk(w2d, 0, kc, col0, n_)
            for j in range(0, n_, 128):
                m = min(128, n_ - j)
                for gi, (hv, nt) in enumerate(srcs):
                    ps = PS.get()
                    for kk in range(kc):
                        c.mm(ps[0:m, 0:nt], wb.v(wv[:, kk, j:j + m]), hv(kk), start=(kk == 0), stop=(kk == kc - 1))
                    sink(gi, col0 + j, m, ps)

    def dense_tm(w2d, kc, c0, ncols, srcs, sink, cw=256):
        for col0 in range(c0, c0 + ncols, cw):
            n_ = min(cw, c0 + ncols - col0)
            wb, wv = wchunk(w2d, 0, kc, col0, n_)
            for gi, (hv, nt) in enumerate(srcs):
                for tt in range(0, nt, 128):
                    P = min(128, nt - tt)
                    ps = PS.get()
                    for kk in range(kc):
                        c.mm(ps[0:P, 0:n_], hv(kk, tt, P), wb.v(wv[:, kk, :]), start=(kk == 0), stop=(kk == kc - 1))
                    sink(gi, tt, P, col0, n_, ps)

    def tm2fm(dst_fn, src, P, ncol, dt=F32):
        idn = ident if dt == F32 else identb
        for j in range(ncol // 128):
            if dt == F32:
                ps = PS.get()
                pv_ = ps[:, 0:P]
            else:
                ps = PSB.get()
                pv_ = ps[:, 0:P]
            c.tr(pv_, src(j * 128, 128), idn[0:P, 0:P])
            CP(dst_fn(j), pv_, eng=act)

    def rmsnorm(xv, hv, n, gname, gi):
        ps = PS.get()
        for cc in range(16):
            sq = TA.get()
            ACT(sq[:, 0:n], xv(cc), AF.Square)
            c.mm(ps[:, 0:n], ones[:, :], sq[:, 0:n], start=(cc == 0), stop=(cc == 15), inc=True)
        rs = TA.get()
        ACT(rs[:, 0:n], ps[:, 0:n], AF.Sqrt, scale=1.0 / D, bias=k.epsb[:, 0:1])
        c.op(dve, "reciprocal", rs[:, 0:n], [rs[:, 0:n]])
        for cc in range(16):
            STT(hv(cc), xv(cc), pv(gname, gi * 16 + cc), rs[:, 0:n], ALU.mult, ALU.mult)

    epsb = c.sbuf("epsb", [128, 1], F32); MS(epsb[:], EPS)
    k.epsb = epsb

    def conv_fm(out, taps, wname, bname, l, ntap, j, nj):
        TS(out, taps[ntap - 1], pv(wname, (l * ntap + ntap - 1) * nj + j), pv(bname, l * nj + j), ALU.mult, ALU.add)
        for i in range(ntap - 1):
            STT(out, taps[i], pv(wname, (l * ntap + i) * nj + j), out, ALU.mult, ALU.add)

    for ti, (tok0, n) in enumerate(tiles):
        has_s = (ti == 0)
        nblk = n // 128
        for b in range(nblk):
            for q4 in range(4):
                xr = TA.get()
                c.dma(sp, xr[:, :], di["x_prompt"][tok0 + b * 128: tok0 + (b + 1) * 128, q4 * 512:(q4 + 1) * 512])
                ps = PS.get()
                for jj in range(4):
                    c.tr(ps[:, jj * 128:(jj + 1) * 128], xr[:, jj * 128:(jj + 1) * 128], ident[:, :])
                CP(xP[:, q4 * 4:(q4 + 1) * 4, b * 128:(b + 1) * 128], ps.v(ps.t[:, :].rearrange("p (a b) -> p a b", a=4)))
        if has_s:
            for q4 in range(4):
                xr = TA.get()
                c.dma(sp, xr[0:NS, :], di["x_sample"][:, q4 * 512:(q4 + 1) * 512])
                ps = PS.get()
                for jj in range(4):
                    c.tr(ps[:, jj * NS:(jj + 1) * NS], xr[0:NS, jj * 128:(jj + 1) * 128], ident[0:NS, 0:NS])
                CP(xS[:, q4 * 4:(q4 + 1) * 4, :], ps.v(ps.t[:, 0:4 * NS].rearrange("p (a b) -> p a b", a=4)))

        for l in range(DEPTH):
            last = (ti == len(tiles) - 1)
            off = 0
            oTP, off = carve("oTP", off, [128, 32, n], BF16)
            oTS, off = carve("oTS", off, [128, 32, NS], BF16)
            PBa, _ = carve("PBa", off, [128, 2, 1024], F32)
            PB = PBa
            PBs = [PBa, PBa]
            GYb, off = carve("GY", off, [128, 8, 256], F32)
            k.PBv8 = GYb
            off_fmb = off
            FMB, off = carve("FMB", off, [128, 6, n + 3], F32)
            rmsnorm(lambda cc: xP[:, cc, 0:n], lambda cc: hP[:, cc, 0:n], n, "g_mix", l)
            if has_s:
                rmsnorm(lambda cc: xS[:, cc, :], lambda cc: hS[:, cc, :], NS, "g_mix", l)
            srcs_fm = [(lambda kk: hP[:, kk, 0:n], n)] + ([(lambda kk: hS[:, kk, :], NS)] if has_s else [])
            srcs_tm = [(lambda kk, tt, P: hP[:, kk, tt:tt + P], n)] + ([(lambda kk, tt, P: hS[:, kk, tt:tt + P], NS)] if has_s else [])
            w_in = WW("w_in", l)
            MS(oTP[:, :, :], 0.0)
            if has_s:
                MS(oTS[:, :, :], 0.0)

            k.WW = WW
            mixer_lru(k, l, ti, n, has_s, last, srcs_fm, oTP, oTS, dense_fm, conv_fm, tm2fm, FMB)
            barrier()
            DWb, _ = carve("DWb", off_fmb, [128, 9, 256], F32)
            mixer_B(k, l, ti, tok0, n, has_s, last, srcs_fm, srcs_tm, oTP, oTS, dense_fm, dense_tm, PBs, DWb)
            barrier()
            mixer_C(k, l, ti, tok0, n, has_s, last, srcs_fm, srcs_tm, oTP, oTS, dense_fm, dense_tm, conv_fm, PB, off_fmb)
            barrier()
            DW, _ = carve("DW", off_fmb, [128, 9, 256], F32)
            mixer_D(k, l, ti, tok0, n, has_s, last, srcs_tm, oTP, oTS, dense_tm, PBs, DW)

            barrier()
            MT, _ = carve("MT", 32 * (n + NS), [128, 16, n + NS], BF16)
            groups = [(oTP, hP, n, 0)] + ([(oTS, hS, NS, n)] if has_s else [])
            for jb in range(0, 16, 2):
                accs = {}
                for br in range(4):
                    wbb, wbv = wchunk(WW("w_branch", l, br), 0, 8, jb * 128, 256)
                    wgb, wgv = wchunk(WW("w_gate", l, br), 0, 16, jb * 128, 256)
                    for j2 in range(2):
                        for gi, (oT_, h_, nt, moff) in enumerate(groups):
                            psb_ = PS.get()
                            for kk in range(8):
                                c.mm(psb_[:, 0:nt], wbb.v(wbv[:, kk, j2 * 128:(j2 + 1) * 128]), oT_[:, br * 8 + kk, 0:nt], start=(kk == 0), stop=(kk == 7))
                            psg = PS.get()
                            for kk in range(16):
                                c.mm(psg[:, 0:nt], wgb.v(wgv[:, kk, j2 * 128:(j2 + 1) * 128]), h_[:, kk, 0:nt], start=(kk == 0), stop=(kk == 15))
                            sg = TA.get()
                            ACT(sg[:, 0:nt], psg[:, 0:nt], AF.Sigmoid)
                            mv = k.MACC[:, j2, moff:moff + nt]
                            if br == 0:
                                TT(mv, sg[:, 0:nt], psb_[:, 0:nt], ALU.mult)
                            else:
                                TT(sg[:, 0:nt], sg[:, 0:nt], psb_[:, 0:nt], ALU.mult)
                                TT(mv, mv, sg[:, 0:nt], ALU.add)
                            if br == 3:
                                CP(MT[:, jb + j2, moff:moff + nt], mv, eng=act)
            def sink_res(xg):
                def f(gi, col, m, ps):
                    xv = (xP[:, col // 128, 0:n] if gi == 0 else xS[:, col // 128, :])
                    nt = n if gi == 0 else NS
                    TT(xv, xv, ps[:, 0:nt], ALU.add)
                return f
            msrc = [(lambda kk: MT[:, kk, 0:n], n)] + ([(lambda kk: MT[:, kk, n:n + NS], NS)] if has_s else [])
            dense_fm(WW("w_out", l), 16, 0, D, msrc, sink_res(None))
            barrier()
            aT, _ = carve("aT", 0, [128, 44, n + NS], BF16)
            rmsnorm(lambda cc: xP[:, cc, 0:n], lambda cc: hP[:, cc, 0:n], n, "g_ffn", l)
            if has_s:
                rmsnorm(lambda cc: xS[:, cc, :], lambda cc: hS[:, cc, :], NS, "g_ffn", l)
            k.WW = WW
            ffn_phase(k, l, ti, n, has_s, last, srcs_fm, aT, wchunk, conv_fm, tm2fm)
            asrc = [(lambda kk: aT[:, kk, 0:n], n)] + ([(lambda kk: aT[:, kk, n:n + NS], NS)] if has_s else [])
            wd = WW("ffn_w_down", l)
            for jb in range(16):
                pss = [PS.get() for _ in asrc]
                kgs = [(0, 16), (16, 16), (32, 12)]
                for (k0, kcn) in kgs:
                    wb, wv = wchunk(wd, k0, kcn, jb * 128, 128)
                    for gi, (av, nt) in enumerate(asrc):
                        for kk in range(kcn):
                            c.mm(pss[gi][:, 0:nt], wb.v(wv[:, kk, :]), av(k0 + kk), start=(k0 + kk == 0), stop=(k0 + kk == 43), inc=(kk == kcn - 1))
                for gi, (av, nt) in enumerate(asrc):
                    xv = (xP[:, jb, 0:n] if gi == 0 else xS[:, jb, :])
                    TT(xv, xv, pss[gi][:, 0:nt], ALU.add)
            barrier()

        def final(xv, nt, ydram):
            ps = PS.get()
            for cc in range(16):
                sq = TA.get()
                ACT(sq[:, 0:nt], xv(cc), AF.Square)
                c.mm(ps[:, 0:nt], ones[:, :], sq[:, 0:nt], start=(cc == 0), stop=(cc == 15), inc=True)
            rs = k.RSB
            ACT(rs[:, 0:nt], ps[:, 0:nt], AF.Sqrt, scale=1.0 / D, bias=epsb[:, 0:1])
            c.op(dve, "reciprocal", rs[:, 0:nt], [rs[:, 0:nt]])
            for tt in range(0, nt, 128):
                P = min(128, nt - tt)
                for q4 in range(4):
                    yt = TA.get()
                    pso = PS.get()
                    for jj in range(4):
                        cc = q4 * 4 + jj
                        yn = TA.get()
                        STT(yn[:, 0:P], k.sl(xv(cc), tt, P), pv("g_final", cc), rs[:, tt:tt + P], ALU.mult, ALU.mult)
                        c.tr(pso[0:P, jj * 128:(jj + 1) * 128], yn[:, 0:P], ident[:, :])
                    CP(yt[0:P, :], pso[0:P, :], eng=act)
                    c.dma(sp, ydram[tt:tt + P, q4 * 512:(q4 + 1) * 512], yt[0:P, :])
        final(lambda cc: xP[:, cc, 0:n], n, do["y_prompt"][tok0:tok0 + n, :])
        if has_s:
            final(lambda cc: xS[:, cc, :], NS, do["y_sample"])
        barrier()

def load_tm2fm(k, dram2d, ncol, dst_fn):
    c = k.c
    for c0 in range(0, ncol, 512):
        w = min(512, ncol - c0)
        xr = k.TA.get()
        c.dma(k.sp, xr[0:NS, 0:w], dram2d[:, c0:c0 + w])
        for jj in range(w // 128):
            ps = k.PS.get()
            c.tr(ps[:, 0:NS], xr[0:NS, jj * 128:(jj + 1) * 128], k.ident[0:NS, 0:NS])
            k.CP(dst_fn(c0 // 128 + jj), ps[:, 0:NS], eng=k.act)


def store_fm2tm(k, src_fn, nch, dram2d, P=NS):
    c = k.c
    for j0 in range(0, nch, 4):
        m = min(4, nch - j0)
        ps = k.PS.get()
        for jj in range(m):
            c.tr(ps[0:P, jj * 128:(jj + 1) * 128], src_fn(j0 + jj), k.ident[:, :])
        yt = k.TA.get()
        k.CP(yt[0:P, 0:m * 128], ps[0:P, 0:m * 128], eng=k.act)
        c.dma(k.sp, dram2d[:, j0 * 128:(j0 + m) * 128], yt[0:P, 0:m * 128])


def mixer_lru(k, l, ti, n, has_s, last, srcs_fm, oTP, oTS, dense_fm, conv_fm, tm2fm, FMB):
    c, di, do = k.c, k.di, k.do
    TT, TS, STT, ACT, CP, MS, TA, TBp, PS, pv = k.TT, k.TS, k.STT, k.ACT, k.CP, k.MS, k.TA, k.TBp, k.PS, k.pv
    w_in = k.WW("w_in", l)
    GY = k.PBv8
    if has_s:
        HS, H0S, XAS, HNS, GYS = k.LS_HS, k.LS_H0, k.LS_XA, k.LS_HN, k.LS_GY
        for i in range(3):
            load_tm2fm(k, di["state_lru_conv"][l, :, i, :], 1024, lambda j, i=i: HS[:, j, i, :])
        load_tm2fm(k, di["state_lru_h"][l], 1024, lambda j: H0S[:, j, :])

    def sink_y(gi, col, m, ps):
        j = (col - O_YA) // 128
        if gi == 0:
            ACT(GY[:, j, 0:n], ps[:, 0:n], AF.Gelu_apprx_tanh)
        else:
            ACT(GYS[:, j, :], ps[:, 0:NS], AF.Gelu_apprx_tanh)
    dense_fm(w_in, 16, O_YA, 1024, srcs_fm, sink_y)

    def lru_core(j, xc, nt, init, hs_out):
        xcb = TBp.get()
        CP(xcb[:, 0:nt], xc, eng=k.act)
        ps_r = PS.get()
        c.mm(ps_r[:, 0:nt], k.LWA[:, l, j, :], xcb[:, 0:nt])
        r = TA.get()
        ACT(r[:, 0:nt], ps_r[:, 0:nt], AF.Sigmoid, bias=pv("lru_ba", l * 8 + j))
        ps_i = PS.get()
        c.mm(ps_i[:, 0:nt], k.LWX[:, l, j, :], xcb[:, 0:nt])
        a = TA.get()
        ACT(a[:, 0:nt], r[:, 0:nt], AF.Exp, scale=k.LCC[:, l * 8 + j:l * 8 + j + 1])
        ACT(r[:, 0:nt], r[:, 0:nt], AF.Exp, scale=k.LCC2[:, l * 8 + j:l * 8 + j + 1])
        ACT(r[:, 0:nt], r[:, 0:nt], AF.Sqrt, scale=-1.0, bias=k.oneb[:, 0:1])
        ig = TA.get()
        ACT(ig[:, 0:nt], ps_i[:, 0:nt], AF.Sigmoid, bias=pv("lru_bx", l * 8 + j))
        TT(ig[:, 0:nt], ig[:, 0:nt], xc, ALU.mult)
        TT(ig[:, 0:nt], ig[:, 0:nt], r[:, 0:nt], ALU.mult)
        return a, ig

    def sink_x(gi, col, m, ps):
        j = (col - O_XA) // 128
        if gi == 0:
            xpad = FMB[:, j % 6, 0:n + 3]
            CP(FMB[:, j % 6, 0:3], k.HI_LRU[:, l, j, :])
            CP(FMB[:, j % 6, 3:n + 3], ps[:, 0:n], eng=k.act)
            xc = TA.get()
            conv_fm(xc[:, 0:n], [FMB[:, j % 6, i:i + n] for i in range(4)], "lru_conv_w", "lru_conv_b", l, 4, j, 8)
            CP(k.HI_LRU[:, l, j, :], FMB[:, j % 6, n:n + 3])
            a, u = lru_core(j, xc[:, 0:n], n, None, None)
            hs = TA.get()
            c.op(k.dve, "tensor_tensor_scan", hs[:, 0:n], [a[:, 0:n], u[:, 0:n], k.H_LRU[:, l, j:j + 1]], ALU.mult, ALU.add)
            CP(k.H_LRU[:, l, j:j + 1], hs[:, n - 1:n])
            TT(oTP[:, j, 0:n], hs[:, 0:n], GY[:, j, 0:n], ALU.mult)
        else:
            CP(XAS[:, j, :], ps[:, 0:NS], eng=k.act)
            xc = TA.get()
            conv_fm(xc[:, 0:NS], [HS[:, j, 0, :], HS[:, j, 1, :], HS[:, j, 2, :], XAS[:, j, :]], "lru_conv_w", "lru_conv_b", l, 4, j, 8)
            a, u = lru_core(j, xc[:, 0:NS], NS, None, None)
            hs = TA.get()
            TT(hs[:, 0:NS], a[:, 0:NS], H0S[:, j, :], ALU.mult)
            TT(HNS[:, j, :], hs[:, 0:NS], u[:, 0:NS], ALU.add)
            TT(oTS[:, j, :], HNS[:, j, :], GYS[:, j, :], ALU.mult)
    dense_fm(w_in, 16, O_XA, 1024, srcs_fm, sink_x)

    if has_s:
        store_fm2tm(k, lambda j: HNS[:, j, :], 8, do["lru_h_sample"][l])
        store_fm2tm(k, lambda j: XAS[:, j, :], 8, do["lru_conv_sample"][l, :, 2, :])
        for i in range(2):
            c.dma(k.sp, do["lru_conv_sample"][l, :, i, :], di["state_lru_conv"][l, :, i + 1, :])
    if last:
        store_fm2tm(k, lambda jj: k.H_LRU[:, l, :], 1, do["lru_h_prompt"][l].rearrange("(j p) -> j p", p=128), P=8)
        for i in range(3):
            store_fm2tm(k, lambda jj, i=i: k.HI_LRU[:, l, :, i], 1, do["lru_conv_prompt"][l, i].rearrange("(j p) -> j p", p=128), P=8)


def ffn_phase(k, l, ti, n, has_s, last, srcs_fm, aT, wchunk, conv_fm, tm2fm):
    c, di, do = k.c, k.di, k.do
    TT, TS, STT, ACT, CP, MS, TA, TBp, PS, pv = k.TT, k.TS, k.STT, k.ACT, k.CP, k.MS, k.TA, k.TBp, k.PS, k.pv
    off = 44 * (n + NS)
    UP, off = k.carve("UP", off, [128, 2, n + 2], F32)
    if has_s:
        HSF, off = k.carve("HSF", off, [128, 44, 2, NS], F32)
        US, off = k.carve("US", off, [128, 44, NS], F32)
        for i in range(2):
            load_tm2fm(k, di["state_ffn_conv"][l, :, i, :], DFF, lambda j, i=i: HSF[:, j, i, :])
    wu = k.WW("ffn_w_up", l)
    wv_ = k.WW("ffn_w_val", l)
    for j0 in range(0, 44, 2):
        wub, wuv = wchunk(wu, 0, 16, j0 * 128, 256)
        wvb, wvv = wchunk(wv_, 0, 16, j0 * 128, 256)
        for j2 in range(2):
            j = j0 + j2
            for gi, (hv, nt) in enumerate(srcs_fm):
                psu = PS.get()
                for kk in range(16):
                    c.mm(psu[:, 0:nt], wub.v(wuv[:, kk, j2 * 128:(j2 + 1) * 128]), hv(kk), start=(kk == 0), stop=(kk == 15))
                psv = PS.get()
                for kk in range(16):
                    c.mm(psv[:, 0:nt], wvb.v(wvv[:, kk, j2 * 128:(j2 + 1) * 128]), hv(kk), start=(kk == 0), stop=(kk == 15))
                uc = TA.get()
                if gi == 0:
                    s = j % 2
                    CP(UP[:, s, 0:2], k.HI_FFN[:, l, j, :])
                    CP(UP[:, s, 2:n + 2], psu[:, 0:n], eng=k.act)
                    conv_fm(uc[:, 0:n], [UP[:, s, i:i + n] for i in range(3)], "ffn_conv_w", "ffn_conv_b", l, 3, j, 44)
                    CP(k.HI_FFN[:, l, j, :], UP[:, s, n:n + 2])
                    ACT(uc[:, 0:n], uc[:, 0:n], AF.Gelu_apprx_tanh)
                    TT(aT[:, j, 0:n], uc[:, 0:n], psv[:, 0:n], ALU.mult)
                else:
                    CP(US[:, j, :], psu[:, 0:NS], eng=k.act)
                    conv_fm(uc[:, 0:NS], [HSF[:, j, 0, :], HSF[:, j, 1, :], US[:, j, :]], "ffn_conv_w", "ffn_conv_b", l, 3, j, 44)
                    ACT(uc[:, 0:NS], uc[:, 0:NS], AF.Gelu_apprx_tanh)
                    TT(aT[:, j, n:n + NS], uc[:, 0:NS], psv[:, 0:NS], ALU.mult)
    if has_s:
        store_fm2tm(k, lambda j: US[:, j, :], 44, do["ffn_conv_sample"][l, :, 1, :])
        c.dma(k.sp, do["ffn_conv_sample"][l, :, 0, :], di["state_ffn_conv"][l, :, 1, :])
    if last:
        for i in range(2):
            store_fm2tm(k, lambda jj, i=i: k.HI_FFN[:, l, :, i], 1, do["ffn_conv_prompt"][l, i].rearrange("(j p) -> j p", p=128), P=44)

def rstd_from_ss(k, out, ss, P, denom):
    k.ACT(out, ss, AF.Ln, scale=1.0 / denom, bias=k.epsb[0:P, 0:1])
    k.ACT(out, out, AF.Exp, scale=-0.5)


def tr_bf(k, dst, src, P, ncol=128):
    ps = k.PSB.get()
    k.c.tr(ps[0:ncol, 0:P], src, k.identb[0:P, 0:P])
    k.CP(dst, ps[0:ncol, 0:P], eng=k.act)


def block_D(k, l, P, h0, qv, kv, vv, gv, cosv, sinv, Sv, grevv, dst_fn, DW):
    c = k.c
    TT, TS, STT, ACT, CP, PS, TBp = k.TT, k.TS, k.STT, k.ACT, k.CP, k.PS, k.TBp

    def w(i):
        return DW[0:P, i, :]

    def v4(x):
        return V(x.buf, x.ap.rearrange("p (h a d) -> p h a d", h=2, a=2))

    def bc4(t):
        a = t.ap
        return V(t.buf, bass.AP(a.tensor, a.offset, [list(a.ap[0]), [0, 2], [0, 2], [1, 64]]))

    def bc3(t):
        a = t.ap
        return V(t.buf, bass.AP(a.tensor, a.offset, [list(a.ap[0]), [0, 2], [1, 64]]))

    def half(x, a_):
        x4 = x.ap.rearrange("p (h a d) -> p h a d", h=2, a=2)
        return V(x.buf, x4[:, :, a_, :])

    def rope(dst, src):
        TT(v4(w(0)), v4(src), bc4(cosv), ALU.mult)
        TT(half(w(1), 0), half(src, 1), bc3(sinv), ALU.mult)
        TT(half(w(1), 1), half(src, 0), bc3(sinv), ALU.mult)
        TT(half(dst, 0), half(w(0), 0), half(w(1), 0), ALU.subtract)
        TT(half(dst, 1), half(w(0), 1), half(w(1), 1), ALU.add)

    rope(w(2), qv)
    rope(w(3), kv)
    A = TBp.get(); B = TBp.get(); Cb = TBp.get()
    qr_b, kr_b = A[0:P, 0:256], A[0:P, 256:512]
    qh_b, v_b = B[0:P, 0:256], B[0:P, 256:512]
    vh_b, ob_b = Cb[0:P, 0:256], Cb[0:P, 256:512]
    CP(qr_b, w(2)); CP(kr_b, w(3), eng=k.act)
    for hh in range(2):
        h = h0 + hh
        TS(k.sl(qh_b, hh * 128, 128), k.sl(w(2), hh * 128, 128), k.gpow[0:P, h:h + 1], None, ALU.mult)
        TS(k.sl(vh_b, hh * 128, 128), k.sl(vv, hh * 128, 128), grevv[0:P, h:h + 1], None, ALU.mult)
    CP(v_b, vv, eng=k.act)
    ACT(w(6), gv, AF.Silu)
    for hh in range(2):
        h = h0 + hh
        cs = slice(hh * 128, (hh + 1) * 128)
        Tt = k.TTp.get()
        qT, kT, qhT = Tt[:, 0:P], Tt[:, 128:128 + P], Tt[:, 256:256 + P]
        tr_bf(k, qT, k.sl(qr_b, hh * 128, 128), P)
        tr_bf(k, kT, k.sl(kr_b, hh * 128, 128), P)
        tr_bf(k, qhT, k.sl(qh_b, hh * 128, 128), P)
        S = Sv(hh)
        Sb = k.SBFp.get()
        CP(Sb[:, :], S, eng=k.act)
        ps_sc = PS.get()
        c.mm(ps_sc[0:P, 0:P], kT, qT)
        Pm = k.PMp.get()
        TT(Pm[0:P, 0:P], ps_sc[0:P, 0:P], k.GM[0:P, h, 0:P], ALU.mult)
        ps_o = PS.get()
        c.mm(ps_o[0:P, 0:128], Pm[0:P, 0:P], k.sl(v_b, hh * 128, 128), start=True, stop=False)
        c.mm(ps_o[0:P, 0:128], qhT, Sb[:, :], start=False, stop=True)
        ps_u = PS.get()
        c.mm(ps_u[:, 0:128], k.sl(kr_b, hh * 128, 128), k.sl(vh_b, hh * 128, 128))
        STT(S, S, math.exp(LOG_GAMMA[h] * P), ps_u[:, 0:128], ALU.mult, ALU.add)
        sm = k.TSm.get()
        ACT(k.sl(w(7), 0, 128), ps_o[0:P, 0:128], AF.Square, accum_out=sm[0:P, 0:1])
        rstd_from_ss(k, sm[0:P, 1:2], sm[0:P, 0:1], P, 128.0)
        STT(k.sl(ob_b, hh * 128, 128), ps_o[0:P, 0:128], sm[0:P, 1:2], k.sl(w(6), hh * 128, 128), ALU.mult, ALU.mult)
        tr_bf(k, dst_fn(h), k.sl(ob_b, hh * 128, 128), P)


def mixer_D(k, l, ti, tok0, n, has_s, last, srcs_tm, oTP, oTS, dense_tm, PBs, DW):
    c, di, do = k.c, k.di, k.do
    ACT, CP, TS = k.ACT, k.CP, k.TS
    w_in = k.WW("w_in", l)
    nblk = n // 128
    k.make_rope(tok0, nblk)

    def proj(hp):
        PB = PBs[hp % 2]
        offs = [O_RQ + 256 * hp, O_RK + 256 * hp, O_RV + 256 * hp, O_RG + 256 * hp]
        for wi, o_ in enumerate(offs):
            def sink(gi, tt, P, col0, n_, ps, wi=wi):
                dst = PB[0:P, tt // 128, wi * 256:(wi + 1) * 256] if gi == 0 else k.SPB[0:P, wi * 256:(wi + 1) * 256]
                if wi == 1:
                    ACT(dst, ps[0:P, 0:256], AF.Copy, scale=128.0 ** -0.5)
                else:
                    CP(dst, ps[0:P, 0:256], eng=k.act)
            dense_tm(w_in, 16, o_, 256, srcs_tm, sink)

    def blocks(hp):
        PB = PBs[hp % 2]
        h0 = 2 * hp
        for b in range(nblk):
            block_D(k, l, 128, h0,
                    PB[:, b, 0:256], PB[:, b, 256:512], PB[:, b, 512:768], PB[:, b, 768:1024],
                    k.cosT[:, b, :], k.sinT[:, b, :],
                    lambda hh: k.S_D[:, l, h0 + hh, :], k.grev,
                    lambda h, b=b: oTP[:, 24 + h, b * 128:(b + 1) * 128], DW)
        if has_s:
            for j in range(NS):
                sp0 = k.SP0p.get()
                c.dma(k.sp, sp0[0:1, :], k.SPB[j:j + 1, :])
                ss = k.SSp.get()
                c.dma(k.sp, ss[:, 0:2, :], di["state_ret"][l, j, h0:h0 + 2].rearrange("h k v -> k h v"))
                block_D(k, l, 1, h0,
                        sp0[0:1, 0:256], sp0[0:1, 256:512], sp0[0:1, 512:768], sp0[0:1, 768:1024],
                        k.cosS[0:1, 0, :], k.sinS[0:1, 0, :],
                        lambda hh, ss=ss: ss[:, hh, :], k.ones,
                        lambda h, j=j: oTS[:, 24 + h, j:j + 1], DW)
                c.dma(k.sp, do["ret_sample"][l, j, h0:h0 + 2].rearrange("h k v -> k h v"), ss[:, 0:2, :])
    if True:
        for hp in range(4):
            proj(hp)
            blocks(hp)
    else:
        proj(0)
        for hp in range(4):
            if hp + 1 < 4:
                proj(hp + 1)
            blocks(hp)
    if last:
        c.dma(k.sp, do["ret_prompt"][l].rearrange("h k v -> k h v"), k.S_D[:, l, :, :])

def block_C(k, l, P, g, xc_fn, zs_v, sdt_v, Sg, dst_fn, DW):
    c = k.c
    TT, TS, STT, ACT, CP, PS, TBp = k.TT, k.TS, k.STT, k.ACT, k.CP, k.PS, k.TBp
    h0 = 8 * g
    xs = V(DW.ap.tensor and DW.buf if False else DW.buf, DW.t[0:P, 0:2, :].rearrange("p a b -> p (a b)")) if False else None
    xs = DW.v(DW.t[0:P, 0:2, :].rearrange("p a b -> p (a b)"))
    y = DW.v(DW.t[0:P, 2:4, :].rearrange("p a b -> p (a b)"))
    sc_sb = DW[0:P, 5, 0:P]
    LBh = DW[0:P, 6, 0:128]
    tmp = DW[0:P, 7, 0:P]
    tmp2 = DW[:, 8, 0:P]
    for i in range(4):
        ps = PS.get()
        c.tr(ps[0:P, 0:128], xc_fn(i), k.ident[:, :])
        CP(k.sl(xs, i * 128, 128), ps[0:P, 0:128], eng=k.act)
    Tt = k.TTp.get()
    B_fm, C_fm = Tt[:, 0:P], Tt[:, 128:128 + P]
    CP(B_fm, xc_fn(4)); CP(C_fm, xc_fn(5), eng=k.act)
    Bt = k.BTp.get()
    B_tm = Bt[0:P, :]
    psb = k.PSB.get()
    c.tr(psb[0:P, 0:128], B_fm, k.identb[:, :])
    CP(B_tm, psb[0:P, 0:128], eng=k.act)
    sm = k.TSm.get()
    dt, logd, b_sb, erev, dtr = sm[0:P, 0:8], sm[0:P, 8:16], sm[0:P, 16:24], sm[0:P, 24:32], sm[0:P, 32:40]
    sm2 = k.TSm.get()
    dS = sm2[:, 0:8]
    TT(dt, sdt_v, k.DTB[l][0:P, h0:h0 + 8], ALU.add)
    ACT(dt, dt, AF.Exp)
    ACT(dt, dt, AF.Ln, bias=k.oneb[0:P, 0:1])
    TT(logd, dt, k.NEGA[l][0:P, h0:h0 + 8], ALU.mult)
    ps1 = PS.get()
    c.mm(ps1[0:P, 0:8], k.causal[0:P, 0:P], logd)
    CP(b_sb, ps1[0:P, 0:8])
    ps2 = PS.get()
    c.mm(ps2[0:P, 0:8], k.trirev[0:P, 0:P], logd)
    ACT(erev, ps2[0:P, 0:8], AF.Exp)
    ps3 = PS.get()
    c.mm(ps3[:, 0:8], k.ones[0:P, 0:128], logd)
    ACT(dS, ps3[:, 0:8], AF.Exp)
    TT(dtr, dt, erev, ALU.mult)

    def bc64(t):
        a = t.ap
        return V(t.buf, bass.AP(a.tensor, a.offset, [list(a.ap[0]), [1, 8], [0, 64]]))

    def r3(t):
        return V(t.buf, t.ap.rearrange("p (h d) -> p h d", h=8))
    vb_ = TBp.get(); vh_ = TBp.get(); yn_ = TBp.get(); Sb_ = TBp.get()
    v_b, vh_b, yn_b = vb_[0:P, :], vh_[0:P, :], yn_[0:P, :]
    TT(r3(v_b), r3(xs), bc64(dt), ALU.mult)
    TT(r3(vh_b), r3(xs), bc64(dtr), ALU.mult)
    Sb = Sb_.v(Sb_.t[:, :].rearrange("p (h d) -> p h d", h=8))
    CP(Sb, Sg, eng=k.act)
    ps_sc = PS.get()
    c.mm(ps_sc[0:P, 0:P], B_fm, C_fm)
    CP(sc_sb, ps_sc[0:P, 0:P])
    for hh in range(8):
        h = h0 + hh
        a = logd.ap
        CP(LBh, V(logd.buf, bass.AP(a.tensor, a.offset + hh, [list(a.ap[0]), [0, 128]])))
        ps_bt = PS.get()
        c.mm(ps_bt[:, 0:P], LBh, k.causal[0:P, 0:P])
        STT(tmp, ps_bt[0:P, 0:P], b_sb[:, hh:hh + 1], k.negm[0:P, 0:P], ALU.subtract, ALU.add)
        TS(tmp, tmp, 0.0, None, ALU.min)
        ACT(tmp, tmp, AF.Exp)
        Pm = k.PMp.get()
        TT(Pm[0:P, 0:P], tmp, sc_sb, ALU.mult)
        ACT(tmp2, ps_bt[:, 0:P], AF.Exp)
        qh = k.QHp.get()
        TT(qh[:, 0:P], xc_fn(5), tmp2, ALU.mult)
        ps_o = PS.get()
        c.mm(ps_o[0:P, 0:64], Pm[0:P, 0:P], k.sl(v_b, hh * 64, 64), start=True, stop=False)
        c.mm(ps_o[0:P, 0:64], qh[:, 0:P], Sb[:, hh, :], start=False, stop=True)
        STT(k.sl(y, hh * 64, 64), k.sl(xs, hh * 64, 64), k.SSD_D[l][0:P, h:h + 1], ps_o[0:P, 0:64], ALU.mult, ALU.add)
    ps_u = PS.get()
    c.mm(ps_u[:, 0:512], B_tm, vh_b)
    a = dS.ap
    TT(Sg, Sg, V(dS.buf, bass.AP(a.tensor, a.offset, [list(a.ap[0]), [1, 8], [0, 64]])), ALU.mult)
    TT(Sg, Sg, ps_u.v(ps_u.t[:, 0:512].rearrange("p (h d) -> p h d", h=8)), ALU.add)
    TT(y, y, zs_v, ALU.mult)
    sm3 = k.TSm.get()
    ACT(k.sl(xs, 0, 512), y, AF.Square, accum_out=sm3[0:P, 0:1])
    rstd_from_ss(k, sm3[0:P, 1:2], sm3[0:P, 0:1], P, 512.0)
    TS(yn_b, y, sm3[0:P, 1:2], None, ALU.mult)
    for i in range(4):
        psb = k.PSB.get()
        c.tr(psb[:, 0:P], k.sl(yn_b, i * 128, 128), k.identb[0:P, 0:P])
        TS(dst_fn(4 * g + i), psb[:, 0:P], k.pv("ssd_norm_w", l * 8 + 4 * g + i), None, ALU.mult)


def mixer_C(k, l, ti, tok0, n, has_s, last, srcs_fm, srcs_tm, oTP, oTS, dense_fm, dense_tm, conv_fm, PB, off_fmb):
    c, di, do = k.c, k.di, k.do
    ACT, CP, TS, PS = k.ACT, k.CP, k.TS, k.PS
    w_in = k.WW("w_in", l)
    nblk = n // 128
    off = off_fmb
    DW, off = k.carve("DWc", off, [128, 9, 256], F32)
    XC, off = k.carve("XC", off, [128, 6, 256], F32)
    XP2, off = k.carve("XP2", off, [128, 2, n + 3], F32)
    if has_s:
        HSC, XSC, XCS = k.C_HSC, k.C_XSC, k.C_XCS
        for i in range(3):
            load_tm2fm(k, di["state_ssd_conv"][l, :, i, :], 1536, lambda j, i=i: HSC[:, j, i, :])
    for g in range(2):
        chunks = [4 * g, 4 * g + 1, 4 * g + 2, 4 * g + 3, 8 + g, 10 + g]
        for slot, ch in enumerate(chunks):
            def sink(gi, col, m, ps, slot=slot, ch=ch):
                if gi == 0:
                    s2 = slot % 2
                    CP(XP2[:, s2, 0:3], k.HI_SSD[:, l, ch, :])
                    CP(XP2[:, s2, 3:n + 3], ps[:, 0:n], eng=k.act)
                    conv_fm(XC[:, slot, 0:n], [XP2[:, s2, i:i + n] for i in range(4)], "ssd_conv_w", "ssd_conv_b", l, 4, ch, 12)
                    CP(k.HI_SSD[:, l, ch, :], XP2[:, s2, n:n + 3])
                    ACT(XC[:, slot, 0:n], XC[:, slot, 0:n], AF.Silu)
                else:
                    CP(XSC[:, ch, :], ps[:, 0:NS], eng=k.act)
                    conv_fm(XCS[:, slot, :], [HSC[:, ch, 0, :], HSC[:, ch, 1, :], HSC[:, ch, 2, :], XSC[:, ch, :]], "ssd_conv_w", "ssd_conv_b", l, 4, ch, 12)
                    ACT(XCS[:, slot, :], XCS[:, slot, :], AF.Silu)
            dense_fm(w_in, 16, O_XBC + 128 * ch, 128, srcs_fm, sink, cw=128)

        def sink_z(gi, tt, P, col0, n_, ps):
            cc = col0 - (O_SZ + 512 * g)
            dst = PB[0:P, tt // 128, cc:cc + n_] if gi == 0 else k.SPB[0:P, cc:cc + n_]
            ACT(dst, ps[0:P, 0:n_], AF.Silu)
        dense_tm(w_in, 16, O_SZ + 512 * g, 512, srcs_tm, sink_z)

        def sink_dt(gi, tt, P, col0, n_, ps):
            dst = PB[0:P, tt // 128, 512:520] if gi == 0 else k.SPB[0:P, 512:520]
            CP(dst, ps[0:P, 0:8], eng=k.act)
        dense_tm(w_in, 16, O_DT + 8 * g, 8, srcs_tm, sink_dt)
        for b in range(nblk):
            block_C(k, l, 128, g, lambda i, b=b: XC[:, i, b * 128:(b + 1) * 128], PB[:, b, 0:512], PB[:, b, 512:520],
                    k.S_C[:, l, 8 * g:8 * g + 8, :], lambda j, b=b: oTP[:, 16 + j, b * 128:(b + 1) * 128], DW)
        if has_s:
            for j in range(NS):
                sp0 = k.SP0p.get()
                c.dma(k.sp, sp0[0:1, 0:520], k.SPB[j:j + 1, 0:520])
                ss = k.SSp.get()
                ssv = ss.v(ss.t[:, :, :].rearrange("p a b -> p (a b)")[:, 0:512].rearrange("p (h d) -> p h d", h=8))
                c.dma(k.sp, ssv, di["state_ssd"][l, j, 8 * g:8 * g + 8].rearrange("h n v -> n h v"))
                block_C(k, l, 1, g, lambda i, j=j: XCS[:, i, j:j + 1], sp0[0:1, 0:512], sp0[0:1, 512:520],
                        ssv, lambda jj, j=j: oTS[:, 16 + jj, j:j + 1], DW)
                c.dma(k.sp, do["ssd_sample"][l, j, 8 * g:8 * g + 8].rearrange("h n v -> n h v"), ssv)
    if has_s:
        store_fm2tm(k, lambda j: XSC[:, j, :], 12, do["ssd_conv_sample"][l, :, 2, :])
        for i in range(2):
            c.dma(k.sp, do["ssd_conv_sample"][l, :, i, :], di["state_ssd_conv"][l, :, i + 1, :])
    if last:
        c.dma(k.sp, do["ssd_prompt"][l].rearrange("h n v -> n h v"), k.S_C[:, l, :, :])
        for i in range(3):
            store_fm2tm(k, lambda jj, i=i: k.HI_SSD[:, l, :, i], 1, do["ssd_conv_prompt"][l, i].rearrange("(j p) -> j p", p=128), P=12)

def block_B(k, l, P, h0, hq_v, hf_v, hi_v, sgh_fn, S_fn, dst_fn, DW):
    c = k.c
    TT, TS, STT, ACT, CP, PS, TBp = k.TT, k.TS, k.STT, k.ACT, k.CP, k.PS, k.TBp
    nch = max(1, P // 32)
    cs = min(32, P)

    def w(i):
        return DW[0:P, i, :]
    cols = slice(h0 * 128, h0 * 128 + 256)
    ACT(w(0), hq_v, AF.Silu)
    ACT(w(1), hf_v, AF.Sigmoid)
    ACT(w(2), hf_v, AF.Sigmoid, scale=-1.0)
    if l == 1:
        TT(w(1), w(1), k.OML1[0:P, cols], ALU.mult)
        TT(w(1), w(1), k.LB1[0:P, cols], ALU.add)
        TT(w(2), w(2), k.OML1[0:P, cols], ALU.mult)
    ACT(w(1), w(1), AF.Ln)
    ps_b = PS.get()
    c.mm(ps_b[0:P, 0:256], k.tri32[0:P, 0:P], w(1))
    ps_r = PS.get()
    c.mm(ps_r[0:P, 0:256], k.rev32[0:P, 0:P], w(1))
    A = TBp.get(); B = TBp.get()
    qt_b, kt_b = A[0:P, 0:256], A[0:P, 256:512]
    v_b, khc = B[0:P, 0:256], B[0:P, 256:384]
    ACT(w(4), ps_b[0:P, 0:256], AF.Exp)
    TT(qt_b, w(0), w(4), ALU.mult)
    ACT(w(4), ps_b[0:P, 0:256], AF.Exp, scale=-1.0)
    TT(kt_b, w(2), w(4), ALU.mult)
    ACT(w(4), ps_r[0:P, 0:256], AF.Exp)
    TT(w(3), w(2), w(4), ALU.mult)
    CP(v_b, hi_v, eng=k.act)
    for hh in range(2):
        h = h0 + hh
        hs = slice(hh * 128, (hh + 1) * 128)
        S = S_fn(hh)
        ps_d = PS.get()
        c.mm(ps_d[:, 0:nch], w(1)[:, hs], k.ind[0:P, 0:nch])
        sm = k.TSm.get()
        ACT(sm[:, 0:nch], ps_d[:, 0:nch], AF.Exp)
        Tt = k.TTp.get()
        qT, kT = Tt[:, 0:P], Tt[:, 128:128 + P]
        tr_bf(k, qT, qt_b[:, hs], P)
        tr_bf(k, kT, kt_b[:, hs], P)
        Sb = k.SB4p.get()
        for cc in range(nch):
            CP(Sb[:, cc * 128:(cc + 1) * 128], S, eng=k.act)
            TS(khc, w(3)[:, hs], k.ind[0:P, cc:cc + 1], None, ALU.mult)
            ps_u = PS.get()
            c.mm(ps_u[:, 0:128], khc, v_b[:, hs])
            STT(S, S, sm[:, cc:cc + 1], ps_u[:, 0:128], ALU.mult, ALU.add)
        ps_sc = PS.get()
        c.mm(ps_sc[0:P, 0:P], kT, qT)
        Pm = k.PMp.get()
        TT(Pm[0:P, 0:P], ps_sc[0:P, 0:P], k.tri32[0:P, 0:P], ALU.mult)
        ps_oi = PS.get()
        c.mm(ps_oi[:, 0:P], v_b[:, hs], Pm[0:P, 0:P])
        ps_oc = PS.get()
        for cc in range(nch):
            c.mm(ps_oc[:, cc * 32:cc * 32 + cs], Sb[:, cc * 128:(cc + 1) * 128], qT[:, cc * 32:cc * 32 + cs])
        o = DW[:, 5, 0:P]
        sq = DW[:, 6, 0:P]
        rs = DW[:, 7, 0:P]
        CP(o, ps_oi[:, 0:P], eng=k.act)
        TT(o, o, ps_oc[:, 0:P], ALU.add)
        ACT(sq, o, AF.Square)
        ps_ss = PS.get()
        c.mm(ps_ss[:, 0:P], k.ones[:, :], sq)
        ACT(rs, ps_ss[:, 0:P], AF.Ln, scale=1.0 / 128.0, bias=k.epsb[:, 0:1])
        ACT(rs, rs, AF.Exp, scale=-0.5)
        STT(sq, o, k.pv("hg_norm_w", l), rs, ALU.mult, ALU.mult)
        TT(dst_fn(h), sq, sgh_fn(hh), ALU.mult)


def mixer_B(k, l, ti, tok0, n, has_s, last, srcs_fm, srcs_tm, oTP, oTS, dense_fm, dense_tm, PBs, DW):
    c, di, do = k.c, k.di, k.do
    ACT, CP = k.ACT, k.CP
    w_in = k.WW("w_in", l)
    nblk = n // 128

    def proj(hp):
        PB = PBs[hp % 2]
        for wi, o_ in enumerate([O_HQ + 256 * hp, O_HF + 256 * hp, O_HI + 256 * hp]):
            def sink(gi, tt, P, col0, n_, ps, wi=wi):
                dst = PB[0:P, tt // 128, wi * 256:(wi + 1) * 256] if gi == 0 else k.SPB[0:P, wi * 256:(wi + 1) * 256]
                CP(dst, ps[0:P, 0:256], eng=k.act)
            dense_tm(w_in, 16, o_, 256, srcs_tm, sink)

        def sink_g(gi, col, m, ps):
            hh = (col - (O_HG + 256 * hp)) // 128
            if gi == 0:
                ACT(PB[:, hh, 768:768 + n], ps[:, 0:n], AF.Silu)
            else:
                ACT(k.B_SGS[:, hp % 2, hh, :], ps[:, 0:NS], AF.Silu)
        dense_fm(w_in, 16, O_HG + 256 * hp, 256, srcs_fm, sink_g)

    def blocks(hp):
        PB = PBs[hp % 2]
        h0 = 2 * hp
        for b in range(nblk):
            block_B(k, l, 128, h0, PB[:, b, 0:256], PB[:, b, 256:512], PB[:, b, 512:768],
                    lambda hh, b=b: PB[:, hh, 768 + b * 128:768 + (b + 1) * 128],
                    lambda hh: k.S_B[:, l, h0 + hh, :],
                    lambda h, b=b: oTP[:, 8 + h, b * 128:(b + 1) * 128], DW)
        if has_s:
            for j in range(NS):
                sp0 = k.SP0p.get()
                c.dma(k.sp, sp0[0:1, 0:768], k.SPB[j:j + 1, 0:768])
                ss = k.SSp.get()
                c.dma(k.sp, ss[:, 0:2, :], di["state_hgrn"][l, j, h0:h0 + 2].rearrange("h k v -> k h v"))
                block_B(k, l, 1, h0, sp0[0:1, 0:256], sp0[0:1, 256:512], sp0[0:1, 512:768],
                        lambda hh, j=j: k.B_SGS[:, hp % 2, hh, j:j + 1],
                        lambda hh, ss=ss: ss[:, hh, :],
                        lambda h, j=j: oTS[:, 8 + h, j:j + 1], DW)
                c.dma(k.sp, do["hgrn_sample"][l, j, h0:h0 + 2].rearrange("h k v -> k h v"), ss[:, 0:2, :])
    if True:
        for hp in range(4):
            proj(hp)
            blocks(hp)
    else:
        proj(0)
        for hp in range(4):
            if hp + 1 < 4:
                proj(hp + 1)
            blocks(hp)
    if last:
        c.dma(k.sp, do["hgrn_prompt"][l].rearrange("h k v -> k h v"), k.S_B[:, l, :, :])


def _shard_inputs(inputs, ci):
    b = ci % 4
    s0 = ci * NS
    m = {}
    for name, a in inputs.items():
        a = np.asarray(a)
        if name == "x_prompt":
            m[name] = np.ascontiguousarray(a[b])
        elif name == "x_sample":
            m[name] = np.ascontiguousarray(a[s0:s0 + NS, 0, :])
        elif name.startswith("state_"):
            m[name] = np.ascontiguousarray(a[:, s0:s0 + NS])
        else:
            m[name] = np.ascontiguousarray(a)
    return m


_NC_CACHE = {}


def kernel(**inputs):
    T = int(np.asarray(inputs["x_prompt"]).shape[1])
    B = int(np.asarray(inputs["x_prompt"]).shape[0])
    if T not in _NC_CACHE:
        _NC_CACHE[T] = build(T)
    nc = _NC_CACHE[T]
    in_maps = [_shard_inputs(inputs, ci) for ci in range(8)]
    res = run_bass_kernel_spmd(nc, in_maps, core_ids=list(range(8)))
    R = res.results
    f = np.float32
    y_prompt = np.stack([R[b]["y_prompt"] for b in range(B)], 0).astype(f)
    y_sample = np.concatenate([R[ci]["y_sample"] for ci in range(8)], 0)[:, None, :].astype(f)
    outs = [y_prompt, y_sample]
    for nm in ["lru_h", "lru_conv", "hgrn", "ssd", "ssd_conv", "ret", "ffn_conv"]:
        outs.append(np.stack([R[b][nm + "_prompt"] for b in range(B)], 1).astype(f))
        outs.append(np.concatenate([R[ci][nm + "_sample"] for ci in range(8)], 1).astype(f))
    return tuple(outs)
```
